# Optimizing a Trainium2 kernel written in Bass

```python
import jax, jax.numpy as jnp
from jax import lax
import numpy as np

D_MODEL = 1024
BATCH = 4
SEQ = 4096
DEPTH = 1

GLA_HEADS = 4
GLA_DK = D_MODEL // 2
GLA_DV = D_MODEL
GLA_HEAD_K = GLA_DK // GLA_HEADS
GLA_HEAD_V = GLA_DV // GLA_HEADS
GATE_RANK = 16
GATE_TAU = 16.0
CHUNK = 64
CONV_WIDTH = D_MODEL
CONV_K = 3
D_FF = 4 * D_MODEL
EPS = 1e-6
N_MOD = 6

SPLITS = [GLA_DK, GLA_DK, GLA_DV, GLA_DV, GATE_RANK,
          CONV_WIDTH, CONV_WIDTH, CONV_WIDTH, D_MODEL, D_MODEL]
IN_WIDTH = sum(SPLITS)
SPLIT_IDX = [int(s) for s in np.cumsum(SPLITS)[:-1]]

kernel_name = "hybrid_gla_shortconv_adaln_block"


def rmsnorm(x, w):
    xf = x.astype(jnp.float32)
    y = xf * lax.rsqrt(jnp.mean(xf * xf, axis=-1, keepdims=True) + EPS)
    return (y * w.astype(jnp.float32)).astype(x.dtype)


def modulate(h, shift, scale):
    return h * (1.0 + scale[:, None, :]) + shift[:, None, :]


def gla_chunked(q, k, v, log_a):
    B, S, H, dk = q.shape
    dv = v.shape[-1]
    n = S // CHUNK

    def blk(t):
        return t.astype(jnp.float32).reshape(B, n, CHUNK, H, t.shape[-1]).transpose(0, 3, 1, 2, 4)

    q, k, v, la = blk(q), blk(k), blk(v), blk(log_a)
    b = jnp.cumsum(la, axis=3)
    b_last = b[:, :, :, -1:, :]
    q_dec = q * jnp.exp(b)
    k_dec = k * jnp.exp(-b)
    k_end = k * jnp.exp(b_last - b)
    causal = jnp.tril(jnp.ones((CHUNK, CHUNK), dtype=bool))
    scores = jnp.where(causal, jnp.einsum('bhncd,bhnsd->bhncs', q_dec, k_dec), 0.0)
    o_intra = jnp.einsum('bhncs,bhnsv->bhncv', scores, v)
    upd = jnp.einsum('bhncd,bhncv->bhndv', k_end, v)
    decay = jnp.exp(b_last[:, :, :, 0, :])

    def step(state, inp):
        d, u = inp
        return d[..., None] * state + u, state

    s0 = jnp.zeros((B, H, dk, dv), jnp.float32)
    _, s_prev = lax.scan(step, s0, (jnp.moveaxis(decay, 2, 0), jnp.moveaxis(upd, 2, 0)))
    s_prev = jnp.moveaxis(s_prev, 0, 2)
    o = o_intra + jnp.einsum('bhncd,bhndv->bhncv', q_dec, s_prev)
    return o.transpose(0, 2, 3, 1, 4).reshape(B, S, H, dv)


def causal_depthwise_conv(u, w):
    C = u.shape[-1]
    return lax.conv_general_dilated(
        u, w[:, None, :].astype(u.dtype), window_strides=(1,), padding=[(CONV_K - 1, 0)],
        dimension_numbers=('NWC', 'WIO', 'NWC'), feature_group_count=C)


def setup_inputs(seed: int = 0) -> dict:
    key = jax.random.key(seed)
    ks = jax.random.split(key, 20)
    L, D = DEPTH, D_MODEL
    nrm = lambda k, shape, s: jax.random.normal(k, shape, jnp.float32) * s
    return {
        "x": jax.random.normal(ks[0], (BATCH, SEQ, D), jnp.float32),
        "c": jax.random.normal(ks[1], (BATCH, D), jnp.float32),
        "w_ada": nrm(ks[2], (L, D, N_MOD * D), 0.2 * D ** -0.5),
        "b_ada": nrm(ks[3], (L, N_MOD * D), 0.02),
        "norm1_w": 1.0 + nrm(ks[4], (L, D), 0.02),
        "w_in": nrm(ks[5], (L, D, IN_WIDTH), D ** -0.5),
        "w_gate_up": nrm(ks[6], (L, GATE_RANK, GLA_DK), GATE_RANK ** -0.5),
        "b_gate": nrm(ks[7], (L, GLA_DK), 0.02),
        "gla_norm_w": 1.0 + nrm(ks[8], (L, GLA_HEAD_V), 0.02),
        "conv_w": nrm(ks[9], (L, CONV_K, CONV_WIDTH), CONV_K ** -0.5),
        "w_proj_a": nrm(ks[10], (L, GLA_DV, D), GLA_DV ** -0.5),
        "w_proj_b": nrm(ks[11], (L, CONV_WIDTH, D), CONV_WIDTH ** -0.5),
        "w_out": nrm(ks[12], (L, D, D), D ** -0.5),
        "norm2_w": 1.0 + nrm(ks[13], (L, D), 0.02),
        "w_mlp1": nrm(ks[14], (L, D, D_FF), D ** -0.5),
        "w_mlp2": nrm(ks[15], (L, D_FF, D), D_FF ** -0.5),
        "final_norm_w": 1.0 + nrm(ks[16], (D,), 0.02),
    }


def reference(x, c, w_ada, b_ada, norm1_w, w_in, w_gate_up, b_gate, gla_norm_w, conv_w,
              w_proj_a, w_proj_b, w_out, norm2_w, w_mlp1, w_mlp2, final_norm_w):
    B, S, D = x.shape
    c_act = jax.nn.silu(c)
    for l in range(DEPTH):
        mod = c_act @ w_ada[l] + b_ada[l]
        shift1, scale1, gate1, shift2, scale2, gate2 = jnp.split(mod, N_MOD, axis=-1)

        h = modulate(rmsnorm(x, norm1_w[l]), shift1, scale1)
        p = h @ w_in[l]
        q, k, v, g, lr, cb, cc, cx, ga, gb = jnp.split(p, SPLIT_IDX, axis=-1)

        log_a = jax.nn.log_sigmoid((lr @ w_gate_up[l] + b_gate[l]).astype(jnp.float32)) / GATE_TAU
        qh = q.reshape(B, S, GLA_HEADS, GLA_HEAD_K) * (GLA_HEAD_K ** -0.5)
        kh = k.reshape(B, S, GLA_HEADS, GLA_HEAD_K)
        vh = v.reshape(B, S, GLA_HEADS, GLA_HEAD_V)
        lah = log_a.reshape(B, S, GLA_HEADS, GLA_HEAD_K)
        o = gla_chunked(qh, kh, vh, lah)
        o = rmsnorm(o, gla_norm_w[l]).reshape(B, S, GLA_DV).astype(x.dtype)
        y_a = (o * jax.nn.silu(g)) @ w_proj_a[l]

        u = cc * cx
        y_b = (cb * causal_depthwise_conv(u, conv_w[l])) @ w_proj_b[l]

        z = jax.nn.sigmoid(ga) * y_a + jax.nn.sigmoid(gb) * y_b
        x = x + gate1[:, None, :] * (z @ w_out[l])

        h2 = modulate(rmsnorm(x, norm2_w[l]), shift2, scale2)
        m = jnp.square(jax.nn.relu(h2 @ w_mlp1[l])) @ w_mlp2[l]
        x = x + gate2[:, None, :] * m
    return rmsnorm(x, final_norm_w)
```

```python
import numpy as np
from contextlib import ExitStack
import concourse.bass as bass
import concourse.mybir as mybir
from concourse.bass_utils import run_bass_kernel_spmd

F32 = mybir.dt.float32
BF16 = mybir.dt.bfloat16
AF = mybir.ActivationFunctionType
ALU = mybir.AluOpType

D = 1024
TOK = 2048
T = 512
NW = 5
NTMP = 8
NSCR = 40
PRECONV = ("w2",)
EPS = 1e-6
Q0, K0, V0, G0, LR0, CB0, CC0, CX0, GA0, GB0 = 0, 512, 1024, 2048, 3072, 3088, 4112, 5136, 6160, 7184
C_CT, C_BADA, C_N1, C_N2, C_CW, C_GNW, C_FLAG, NCONST = 0, 8, 40, 48, 56, 80, 82, 84


class Ev:
    __slots__ = ("eng", "sem", "val", "know", "dma")

    def __init__(self, eng, sem, val, know, dma):
        self.eng, self.sem, self.val, self.know, self.dma = eng, sem, val, know, dma


class Prog:
    ENGS = ("pe", "act", "dve", "pool", "sp")

    def __init__(self, dry, wplan):
        self.dry = dry
        self.wplan = wplan
        self.streams = {e: [] for e in self.ENGS}
        self.cnt = {e: 0 for e in self.ENGS}
        self.dcnt = {}
        self.know = {e: {} for e in self.ENGS}
        self.last_w = {}
        self.readers = {}
        self.groups = {}
        self.cur_view = {}
        self.barrier = {}
        self.name_keys = {}
        self.bi = 0
        self.bbi = 0
        self.ti = 0
        self.wi = 0
        self.wdone = 0
        self.wmode = "cast"
        self.widx = -1

    def op(self, eng, fn, reads=(), writes=(), dma=None):
        if self.dry:
            return
        deps = []
        for k in list(reads) + list(writes):
            name = k[0] if isinstance(k, tuple) else k
            self.name_keys.setdefault(name, set()).add(k)
            for g in self.groups.get(name, ()):
                if self.cur_view.get(g) != name:
                    old = self.cur_view.get(g)
                    evs = []
                    if old is not None:
                        for kk in self.name_keys.get(old, ()):
                            if kk in self.last_w:
                                evs.append(self.last_w[kk])
                            evs += self.readers.get(kk, [])
                    self.barrier[g] = evs
                    self.cur_view[g] = name
                deps += self.barrier.get(g, [])
        is_dma = dma is not None
        for k in reads:
            ev = self.last_w.get(k)
            if ev is not None:
                deps.append(ev)
        for k in writes:
            ev = self.last_w.get(k)
            if ev is not None and (is_dma or ev.dma or ev.eng != eng or eng != "pe"):
                deps.append(ev)
            for ev in self.readers.get(k, []):
                if is_dma or ev.dma or ev.eng != eng or eng != "pe":
                    deps.append(ev)
        kn = self.know[eng]
        waits = {}
        for ev in deps:
            if kn.get(ev.sem, 0) >= ev.val:
                continue
            waits[ev.sem] = max(waits.get(ev.sem, 0), ev.val)
            for s_, v_ in ev.know.items():
                if kn.get(s_, 0) < v_:
                    kn[s_] = v_
        if fn is None:
            self.streams[eng].append((list(waits.items()), None, None, 0))
            return
        if is_dma:
            sem = dma
            self.dcnt[sem] = self.dcnt.get(sem, 0) + 16
            val = self.dcnt[sem]
            amt = 16
        else:
            sem = "E_" + eng
            self.cnt[eng] += 1
            val = self.cnt[eng]
            amt = 1
        evk = dict(kn)
        evk[sem] = val
        ev = Ev(eng, sem, val, evk, is_dma)
        for k in reads:
            self.readers.setdefault(k, []).append(ev)
        for k in writes:
            self.last_w[k] = ev
            self.readers[k] = []
        self.streams[eng].append((list(waits.items()), fn, sem, amt))

    def sems(self):
        s = {"E_" + e for e in self.ENGS}
        s |= set(self.dcnt.keys())
        return sorted(s)


def build_nc():
    nc = bass.Bass("TRN2", target_bir_lowering=False)

    def din(name, shape):
        return nc.dram_tensor(name, shape, F32, kind="ExternalInput").ap()

    x_cur = din("x_cur", [TOK, D])
    x_prev = din("x_prev", [TOK, D])
    W = {
        "w_ada": din("w_ada", [D, 6 * D]),
        "w_in": din("w_in", [D, 8208]),
        "w_pa": din("w_pa", [D, D]),
        "w_pb": din("w_pb", [D, D]),
        "w_o": din("w_o", [D, D]),
        "w1": din("w1", [D, 4 * D]),
        "w2": din("w2", [4 * D, D]),
    }
    consts_d = din("consts", [128, NCONST])
    fnwb_d = din("fnw_b", [128, D])
    bgb_d = din("bgate_b", [128, 2 * D])
    wga_d = din("wg_aug", [17, 512])
    out_d = nc.dram_tensor("out", [TOK, D], F32, kind="ExternalOutput").ap()
    wscr = nc.dram_tensor("wscr", [NSCR, 128, 4096], BF16, kind="Internal").ap()

    with ExitStack() as es:
        def sb(name, shape, dt):
            return es.enter_context(nc.sbuf_tensor(name, shape, dt))

        xbuf = [sb(f"xbuf{i}", [128, 4, D], F32) for i in range(2)]
        xn = sb("xn", [128, 4, D], BF16)
        hT = sb("hT", [128, 8, T], BF16)
        wsl = [sb(f"wsl{i}", [128, 8, 512], BF16) for i in range(NW)]
        wlr = sb("wlr", [128, 8, 16], BF16)
        lrT = sb("lrT", [32, T], F32)
        wg = sb("wg", [32, 512], F32)
        spb = sb("spb", [128, 4, 512], F32)
        E1 = sb("E1", [128, 4, T], F32)
        spc = sb("spc", [128, 4, 512], F32)
        qdT = sb("qdT", [128, 4, T], BF16)
        kdT = sb("kdT", [128, 4, T], BF16)
        ke = sb("ke", [128, 4, T], BF16)
        big = sb("big", [128, 16384], BF16)
        S32 = sb("S32", [128, 4, 256], F32)
        S_bfs = [sb(f"S_bf{i}", [128, 4, 256], BF16) for i in range(2)]
        scb = [sb(f"scb{i}", [128, 512], BF16) for i in range(2)]
        cmask4 = sb("cmask4", [128, 512], BF16)
        ss2 = sb("ss2", [128, 4], F32)
        lnv2 = sb("lnv2", [128, 4], F32)
        rstd2 = sb("rstd2", [128, 4], F32)
        ub = [sb(f"ub{i}", [128, 514], F32) for i in range(2)]
        uh = sb("uh", [128, 8, 2], F32)
        hTh = sb("hTh", [128, 8, 2], BF16)
        tmps = [sb(f"tmp{i}", [128, 512], F32) for i in range(NTMP)]
        cst = sb("cst", [128, NCONST], F32)
        fnw = sb("fnw", [128, D], F32)
        g1b = sb("g1b", [128, D], F32)
        g2b = sb("g2b", [128, D], F32)
        identf = sb("identf", [128, 128], F32)
        ident = sb("ident", [128, 128], BF16)
        ones_bf = sb("ones_bf", [128, 128], BF16)
        uneg = sb("uneg", [128, 128], F32)
        ce = sb("ce", [128, 8], F32)
        cact = sb("cact", [128, 8], F32)
        cact_bf = sb("cact_bf", [128, 8], BF16)
        modT = sb("modT", [128, 32], F32)
        a1T = sb("a1T", [128, 8], F32)
        a2T = sb("a2T", [128, 8], F32)
        ss = sb("ss", [128, 4], F32)
        lnv = sb("lnv", [128, 4], F32)
        rstd = sb("rstd", [128, 4], F32)
        sso = sb("sso", [128, 16], F32)
        lno = sb("lno", [128, 16], F32)
        rso = sb("rso", [128, 16], F32)

        psf = [es.enter_context(nc.psum_tensor(f"psf{i}", [128, 512], F32)) for i in range(8)]
        psb = [p_[:, :].bitcast(BF16) for p_ in psf]

        v_sb = big[:, 0:4096].rearrange("p (s c) -> p s c", s=4)
        ogT = big[:, 0:4096].rearrange("p (k c) -> p k c", k=8)
        sg = big[:, 4096:8192].rearrange("p (s c) -> p s c", s=4)
        cbuT = big[:, 8192:12288].rearrange("p (k c) -> p k c", k=8)
        zT = big[:, 12288:16384].rearrange("p (k c) -> p k c", k=8)
        aT = big[:, :].rearrange("p (j c) -> p j c", j=32)
        wkp = big[:, 4096:8192].rearrange("p (k c) -> p k c", k=8)
        wvp = [big[:, 8192:12288].rearrange("p (k c) -> p k c", k=8),
               big[:, 12288:16384].rearrange("p (k c) -> p k c", k=8)]
        cbm = qdT[:, 0:2, :].rearrange("p a (k c) -> p (a k) c", k=4)
        w32q = xbuf[1][:, :, :].rearrange("p s (a c) -> p (s a) c", a=2)
        w32k = big[:, 8192:16384].bitcast(F32).rearrange("p (k c) -> p k c", k=8)
        h32T = spc[:, 0:2, :].rearrange("p a (k c) -> p (a k) c", k=4)
        xn32 = spc[:, 2:4, :].rearrange("p a c -> p (a c)")
        qd32 = spc[:, 2, :].rearrange("p (h c) -> p h c", h=4)
        kd32 = spc[:, 3, :].rearrange("p (h c) -> p h c", h=4)

        def record(P):
            P.groups = {"v": ["A"], "ogT": ["A"], "sg": ["B"], "cbuT": ["C"], "zT": ["Dg"],
                        "aT": ["A", "B", "C", "Dg"], "w32k": ["C", "Dg"],
                        "wkp": ["B"], "wvp0": ["C"], "wvp1": ["Dg"],
                        "cbm": ["Q"], "qdT": ["Q"]}

            def A(out, in_, func, r, w, **kw):
                P.op("act", lambda e: e.activation(out=out, in_=in_, func=func, **kw), r, w)

            def Vtt(out, a, b, op, r, w):
                P.op("dve", lambda e: e.tensor_tensor(out=out, in0=a, in1=b, op=op), r, w)

            def Vts(out, a, s1, s2, op0, op1, r, w):
                P.op("dve", lambda e: e.tensor_scalar(out=out, in0=a, scalar1=s1, scalar2=s2, op0=op0, op1=op1), r, w)

            def Vsmul(out, a, s, r, w):
                P.op("dve", lambda e: e.tensor_scalar_mul(out=out, in0=a, scalar1=s), r, w)

            def Vsadd(out, a, s, r, w):
                P.op("dve", lambda e: e.tensor_scalar_add(out=out, in0=a, scalar1=s), r, w)

            def Vstt(out, a, s, b, op0, op1, r, w):
                P.op("dve", lambda e: e.scalar_tensor_tensor(out=out, in0=a, scalar=s, in1=b, op0=op0, op1=op1), r, w)

            def Vcopy(out, a, r, w):
                P.op("dve", lambda e: e.tensor_copy(out=out, in_=a), r, w)

            def Vrecip(out, a, r, w):
                P.op("dve", lambda e: e.reciprocal(out=out, in_=a), r, w)

            def MM(mms, r, w):
                def fn(e):
                    ins = None
                    for (o, l, rh, st, sp_) in mms:
                        ins = e.matmul(out=o, lhsT=l, rhs=rh, start=st, stop=sp_)
                    return ins
                P.op("pe", fn, r, w)

            def TR(trs, r, w):
                def fn(e):
                    ins = None
                    for (o, i) in trs:
                        ins = e.transpose(out=o, in_=i, identity=ident[:])
                    return ins
                P.op("pe", fn, list(r) + ["ident"], w)

            def bank():
                i = P.bi % 8
                P.bi += 1
                return psf[i], ("ps", i)

            def bbank():
                i = P.bi % 8
                P.bi += 1
                return psb[i], ("ps", i)

            def tmp():
                i = P.ti % NTMP
                P.ti += 1
                return tmps[i], ("tmp", i)

            sci = [0]

            def scbuf():
                i = sci[0] % 2
                sci[0] += 1
                return scb[i], ("sc", i)

            ubi = [0]

            def ubuf():
                i = ubi[0] % 2
                ubi[0] += 1
                return ub[i], ("u", i)

            def rec_load(j):
                wname, r0, c0, ncols, mode, idx = P.wplan[j]
                slot = j % NW
                if mode == "scr":
                    src = wscr[idx]
                    dst = wsl[slot][:, :, :].rearrange("p k c -> p (k c)")
                    P.op("pool", lambda e: e.dma_start(out=dst, in_=src), [("scr", idx)], [("w", slot)], dma=f"w{slot}")
                else:
                    src = W[wname][r0:r0 + 1024, c0:c0 + ncols].rearrange("(kc p) c -> p kc c", p=128)
                    dst = wsl[slot][:, :, 0:ncols]
                    P.op("pool", lambda e: e.dma_start(out=dst, in_=src), [], [("w", slot)], dma=f"w{slot}")
                if mode == "castwb":
                    wsrc = wsl[slot][:, :, :].rearrange("p k c -> p (k c)")
                    wdst = wscr[idx]
                    P.op("sp", lambda e: e.dma_start(out=wdst, in_=wsrc), [("w", slot)], [("scr", idx)], dma=f"wb{slot}")

            def is_preconv(spec):
                return spec[0] in PRECONV

            def next_w(spec):
                mode = P.wmode
                if mode in ("t4", "t5"):
                    if is_preconv(spec):
                        mode = "scr"
                    elif P.widx % 2 == 0:
                        mode = "castwb" if mode == "t4" else "scr"
                    else:
                        mode = "cast4" if mode == "t4" else "castwb"
                full = tuple(spec) + (mode, P.widx)
                if P.wmode != "cast":
                    P.widx += 1
                if P.dry:
                    P.wplan.append(full)
                    return wsl[0], ("w", 0)
                i = P.wi
                P.wi += 1
                assert P.wplan[i] == full, (i, P.wplan[i], full)
                return wsl[i % NW], ("w", i % NW)

            def done_w():
                if P.dry:
                    return
                j = P.wdone + NW
                P.wdone += 1
                if j < len(P.wplan):
                    rec_load(j)

            HT_ALL = [("hT", kc) for kc in range(8)]

            conv_list = [] if P.dry else [e_ for e_ in P.wplan if e_[4] == "scr" and is_preconv(e_) and e_[5] >= 0]
            seen_cv = set()
            conv_todo = []
            for e_ in conv_list:
                if e_[5] not in seen_cv:
                    seen_cv.add(e_[5])
                    conv_todo.append(e_)

            def rec_convs(n):
                for _ in range(n):
                    if not conv_todo:
                        return
                    wname, r0, c0, ncols, _m, idx = conv_todo.pop(0)
                    src = W[wname][r0:r0 + 1024, c0:c0 + ncols].rearrange("(kc p) c -> p kc c", p=128)
                    dst = wscr[idx].rearrange("p (k c) -> p k c", k=8)
                    P.op("pool", lambda e, dst=dst, src=src: e.dma_start(out=dst, in_=src), [], [("scr", idx)], dma=f"cv{idx}")

            P.op("sp", lambda e: e.dma_start(out=cst[:, :], in_=consts_d), [], ["cst"], dma="c0")
            P.op("sp", lambda e: e.dma_start(out=wg[0:17, :], in_=wga_d), [], ["wg"], dma="c1")
            P.op("sp", lambda e: e.dma_start(out=g1b[:, :], in_=bgb_d[:, 0:D]), [], ["g1b"], dma="c2")
            P.op("sp", lambda e: e.dma_start(out=g2b[:, :], in_=bgb_d[:, D:2 * D]), [], ["g2b"], dma="c3")
            P.op("sp", lambda e: e.dma_start(out=fnw[:, :], in_=fnwb_d), [], ["fnw"], dma="c4")

            P.op("pool", lambda e: e.memset(identf[:, :], 0.0), [], ["identf"])
            P.op("pool", lambda e: e.affine_select(out=identf[:, :], in_=identf[:, :], pattern=[[-1, 128]],
                                                   compare_op=ALU.not_equal, fill=1.0, base=0, channel_multiplier=1),
                 ["identf"], ["identf"])
            P.op("pool", lambda e: e.memset(cmask4[:, :], 1.0), [], ["cmask4"])
            P.op("pool", lambda e: e.affine_select(out=cmask4[:, :], in_=cmask4[:, :], pattern=[[0, 4], [1, 128]],
                                                   compare_op=ALU.is_ge, fill=0.0, base=0, channel_multiplier=-1),
                 ["cmask4"], ["cmask4"])
            P.op("pool", lambda e: e.memset(uneg[:, :], -1.0 / 16.0), [], ["uneg"])
            P.op("pool", lambda e: e.affine_select(out=uneg[:, :], in_=uneg[:, :], pattern=[[1, 128]],
                                                   compare_op=ALU.is_ge, fill=0.0, base=0, channel_multiplier=-1),
                 ["uneg"], ["uneg"])
            P.op("pool", lambda e: e.memset(lrT[:, :], 1.0), [], ["lrT"])
            P.op("pool", lambda e: e.dma_start(out=wlr[:, :, :], in_=W["w_in"][:, LR0:LR0 + 16].rearrange("(kc p) c -> p kc c", p=128)),
                 [], ["wlr"], dma="c7")
            if not P.dry:
                for j in range(min(4, len(P.wplan))):
                    rec_load(j)
            P.op("pool", lambda e: e.dma_start(out=wkp, in_=W["w_in"][:, K0:K0 + 512].rearrange("(kc p) c -> p kc c", p=128)),
                 [], ["wkp"], dma="c8")
            for n_ in range(2):
                P.op("pool", lambda e, n_=n_: e.dma_start(out=wvp[n_], in_=W["w_in"][:, V0 + n_ * 512:V0 + (n_ + 1) * 512].rearrange("(kc p) c -> p kc c", p=128)),
                     [], [f"wvp{n_}"], dma=f"c{9 + n_}")
            if not P.dry:
                for j in range(4, min(NW, len(P.wplan))):
                    rec_load(j)

            rec_convs(8)

            Vcopy(ident[:, :], identf[:, :], ["identf"], ["ident"])
            P.op("dve", lambda e: e.memset(ones_bf[:, :], 1.0), [], ["ones"])
            P.op("dve", lambda e: e.memset(S32[:, :, :], 0.0), [], [("S32", h) for h in range(4)])
            P.op("dve", lambda e: e.memset(S_bfs[0][:, :, :], 0.0), [], [("Sbf", 0)])
            P.op("dve", lambda e: e.memset(S_bfs[1][:, :, :], 0.0), [], [("Sbf", 1)])
            P.op("dve", lambda e: e.memset(uh[:, :, :], 0.0), [], [("uh", m) for m in range(8)])

            def rec_xload(g):
                src_t = x_prev if g < 4 else x_cur
                t0 = (g % 4) * T
                par = g % 2
                src = src_t[t0:t0 + T, :].rearrange("(s p) d -> p s d", p=128)
                P.op("sp", lambda e: e.dma_start(out=xbuf[par][:, :, :], in_=src), [],
                     [("x", par, s) for s in range(4)], dma=f"x{par}")

            rec_xload(0)
            rec_xload(1)

            cT = cst[:, C_CT:C_CT + 8]
            A(ce[:, :], cT, AF.Exp, ["cst"], ["ce"], scale=-1.0)
            Vsadd(ce[:, :], ce[:, :], 1.0, ["ce"], ["ce"])
            Vrecip(ce[:, :], ce[:, :], ["ce"], ["ce"])
            Vtt(cact[:, :], ce[:, :], cT, ALU.mult, ["ce", "cst"], ["cact"])
            Vcopy(cact_bf[:, :], cact[:, :], ["cact"], ["cactbf"])
            for kc in range(8):
                Vsmul(cbm[:, kc, :], ones_bf[:, :], cact[:, kc:kc + 1], ["ones", "cact"], [("cbm", kc)])

            def ada_chunk(ci):
                wt, wk = next_w(("w_ada", 0, ci * 512, 512))
                fm = {0: 0, 1: 0, 2: 1, 3: 1, 6: 2, 7: 2, 8: 3, 9: 3}
                if ci in fm:
                    j0 = fm[ci] * 8 + (ci % 2) * 4
                    p_, pk = bank()
                    mms = []
                    for jj in range(4):
                        for kc in range(8):
                            mms.append((p_[:, jj:jj + 1], wt[:, kc, jj * 128:(jj + 1) * 128], cact_bf[:, kc:kc + 1],
                                        kc == 0, kc == 7))
                    MM(mms, [wk, "cactbf"], [pk])
                    Vtt(modT[:, j0:j0 + 4], p_[:, 0:4], cst[:, C_BADA + j0:C_BADA + j0 + 4], ALU.add,
                        [pk, "cst"], [("modT", j0)])
                else:
                    gb_ = g1b if ci in (4, 5) else g2b
                    gk = "g1b" if ci in (4, 5) else "g2b"
                    hs_ = slice((ci % 2) * 512, (ci % 2) * 512 + 512)
                    p_, pk = bank()
                    MM([(p_[:, :], cbm[:, kc, :], wt[:, kc, :], kc == 0, kc == 7) for kc in range(8)],
                       [wk] + [("cbm", kc) for kc in range(8)], [pk])
                    Vtt(gb_[:, hs_], p_[:, :], gb_[:, hs_], ALU.add, [pk, gk], [gk])
                done_w()

            def mod_finish(which):
                sc0 = 8 if which == 1 else 24
                nw0 = C_N1 if which == 1 else C_N2
                dst = a1T if which == 1 else a2T
                key = "a1T" if which == 1 else "a2T"
                Vsadd(dst[:, :], modT[:, sc0:sc0 + 8], 1.0, [("modT", sc0), ("modT", sc0 + 4)], [key])
                Vtt(dst[:, :], dst[:, :], cst[:, nw0:nw0 + 8], ALU.mult, [key, "cst"], [key])

            def stage_norm(xb, par, aT_, akey, sh0):
                shkeys = [("modT", sh0), ("modT", sh0 + 4)]
                for s in range(4):
                    xk = ("x", par, s)
                    A(xn[:, s, :], xb[:, s, :], AF.Square, [xk], [("xn", s), ("ss", s)], accum_out=ss[:, s:s + 1])
                    A(lnv[:, s:s + 1], ss[:, s:s + 1], AF.Ln, [("ss", s)], [("lnv", s)], scale=1.0 / D, bias=EPS)
                    A(rstd[:, s:s + 1], lnv[:, s:s + 1], AF.Exp, [("lnv", s)], [("rstd", s)], scale=-0.5)
                    Vsmul(xn[:, s, :], xb[:, s, :], rstd[:, s:s + 1], [xk, ("rstd", s)], [("xn", s)])
                for kc in range(8):
                    pb, pbk = bbank()
                    TR([(pb[:, s * 128:(s + 1) * 128], xn[:, s, kc * 128:(kc + 1) * 128]) for s in range(4)],
                       [("xn", s) for s in range(4)], [pbk])
                    a_ap = aT_[:, kc:kc + 1]
                    s_ap = modT[:, sh0 + kc:sh0 + kc + 1]
                    if kc % 2 == 0:
                        Vts(hT[:, kc, :], pb[:, 0:512], a_ap, s_ap, ALU.mult, ALU.add, [pbk, akey] + shkeys, [("hT", kc)])
                    else:
                        A(hT[:, kc, :], pb[:, 0:512], AF.Identity, [pbk, akey] + shkeys, [("hT", kc)], scale=a_ap, bias=s_ap)

            def sigmoid(p_, pk):
                t, tk = tmp()
                A(t[:, :], p_[:, :], AF.Exp, [pk], [tk], scale=-1.0)
                A(t[:, :], t[:, :], AF.Ln, [tk], [tk], bias=1.0)
                A(t[:, :], t[:, :], AF.Exp, [tk], [tk], scale=-1.0)
                return t, tk

            flag_ap = cst[:, C_FLAG:C_FLAG + 1]

            def la_stage():
                p_, pk = bank()
                MM([(p_[0:16, :], wlr[:, kc, 0:16], hT[:, kc, :], kc == 0, kc == 7) for kc in range(8)],
                   ["wlr"] + HT_ALL, [pk])
                A(lrT[0:16, :], p_[0:16, :], AF.Copy, [pk], ["lrT"])
                for s in range(4):
                    p_, pk = bank()
                    MM([(p_[:, :], lrT[0:17, s * 128:(s + 1) * 128], wg[0:17, :], True, True)], ["lrT", "wg"], [pk])
                    t, tk = tmp()
                    A(t[:, :], p_[:, :], AF.Exp, [pk], [tk], scale=-1.0)
                    A(spb[:, s, :], t[:, :], AF.Ln, [tk], [("sp", s)], bias=1.0)

            def gla_stage(main, last_prev, special=False, hoist=None):
                if last_prev:
                    P.op("sp", lambda e: e.dma_start(out=w32q, in_=W["w_in"][:, Q0:Q0 + 512].rearrange("(kc p) c -> p kc c", p=128)),
                         [], ["w32q"] + [("x", 1, s_) for s_ in range(4)], dma="c5")
                    Vcopy(hTh[:, :, :], hT[:, :, 510:512], HT_ALL, ["hTh"])
                if main:
                    wq, wqk = next_w(("w_in", 0, Q0, 512))
                if main:
                    wk_, wkk = next_w(("w_in", 0, K0, 512))
                else:
                    wk_, wkk = wkp, "wkp"
                e2s = []
                for h in range(4):
                    hs = slice(h * 128, (h + 1) * 128)
                    pbb, pbbk = bank()
                    MM([(pbb[:, s * 128:(s + 1) * 128], spb[:, s, hs], uneg[:, :], True, True) for s in range(4)],
                       [("sp", s) for s in range(4)] + ["uneg"], [pbbk])
                    A(E1[:, h, :], pbb[:, :], AF.Exp, [pbbk], [("E1", h)])
                    e2, e2k = tmp()
                    A(e2[:, :], pbb[:, :], AF.Exp, [pbbk], [e2k], scale=-1.0)
                    e2s.append((e2, e2k))

                def head_front(h):
                    hs = slice(h * 128, (h + 1) * 128)
                    e2, e2k = e2s[h]
                    if main:
                        pq, pqk = bank()
                        MM([(pq[:, :], wq[:, kc, hs], hT[:, kc, :], kc == 0, kc == 7) for kc in range(8)],
                           [wqk] + HT_ALL, [pqk])
                    pkk_, pkkk = bank()
                    MM([(pkk_[:, :], wk_[:, kc, hs], hT[:, kc, :], kc == 0, kc == 7) for kc in range(8)],
                       [wkk] + HT_ALL, [pkkk])
                    if special:
                        MM([(pq[:, 0:128], w32q[:, kc, hs], h32T[:, kc, :], kc == 0, kc == 7) for kc in range(8)],
                           ["w32q", "h32T", ("x", 1, 0)], [pqk])
                        MM([(pkk_[:, 0:128], w32k[:, kc, hs], h32T[:, kc, :], kc == 0, kc == 7) for kc in range(8)],
                           ["w32k", "h32T"], [pkkk])
                    if main:
                        Vstt(qdT[:, h, :], pq[:, :], 128.0 ** -0.5, E1[:, h, :], ALU.mult, ALU.mult,
                             [pqk, ("E1", h)], [("qdT", h)])
                    Vtt(kdT[:, h, :], pkk_[:, :], e2[:, :], ALU.mult, [pkkk, e2k], [("kdT", h)])
                    if special:
                        Vstt(qd32[:, h, :], pq[:, 0:128], 128.0 ** -0.5, E1[:, h, 0:128], ALU.mult, ALU.mult,
                             [pqk, ("E1", h)], [("qd32", h), "xn32"])
                        Vtt(kd32[:, h, :], pkk_[:, 0:128], e2[:, 0:128], ALU.mult, [pkkk, e2k], [("kd32", h), "xn32"])

                def head_back(h):
                    pb, pbk = bbank()
                    TR([(pb[:, s * 128:(s + 1) * 128], kdT[:, h, s * 128:(s + 1) * 128]) for s in range(4)], [("kdT", h)], [pbk])
                    Vcopy(ke[:, h, :], pb[:, 0:512], [pbk], [("ke", h)])

                for h in range(4):
                    head_front(h)
                    if h >= 1:
                        head_back(h - 1)
                head_back(3)
                if main:
                    done_w()
                    done_w()
                for n in range(2):
                    if main:
                        wv, wvk = next_w(("w_in", 0, V0 + n * 512, 512))
                    else:
                        wv, wvk = wvp[n], f"wvp{n}"
                    for s in range(4):
                        p_, pk = bank()
                        MM([(p_[:, :], hT[:, kc, s * 128:(s + 1) * 128], wv[:, kc, :], kc == 0, kc == 7) for kc in range(8)],
                           [wvk] + HT_ALL, [pk])
                        if s % 2 == 0:
                            Vcopy(v_sb[:, s, n * 512:(n + 1) * 512], p_[:, :], [pk], [("v", s)])
                        else:
                            A(v_sb[:, s, n * 512:(n + 1) * 512], p_[:, :], AF.Copy, [pk], [("v", s)])
                    if main:
                        done_w()
                if last_prev:
                    P.op("sp", lambda e: e.dma_start(out=w32k, in_=W["w_in"][:, K0:K0 + 512].rearrange("(kc p) c -> p kc c", p=128)),
                         [], ["w32k"], dma="c6")
                if (not main) and hoist is not None:
                    hoist()
                if main:
                    for n in range(2):
                        wgg, wggk = next_w(("w_in", 0, G0 + n * 512, 512))
                        for s in range(4):
                            p_, pk = bank()
                            MM([(p_[:, :], hT[:, kc, s * 128:(s + 1) * 128], wgg[:, kc, :], kc == 0, kc == 7) for kc in range(8)],
                               [wggk] + HT_ALL, [pk])
                            r_, rk = sigmoid(p_, pk)
                            Vtt(sg[:, s, n * 512:(n + 1) * 512], p_[:, :], r_[:, :], ALU.mult, [pk, rk],
                                [("sg", s, 2 * n), ("sg", s, 2 * n + 1)])
                        done_w()
                def phaseA(s):
                    cs = slice(s * 128, (s + 1) * 128)
                    st = {}
                    if main:
                        psc, psck = bank()
                        if special and s == 0:
                            MM([(psc[:, h * 128:(h + 1) * 128], kd32[:, h, :], qd32[:, h, :], True, True) for h in range(4)],
                               [("kd32", h) for h in range(4)] + [("qd32", h) for h in range(4)], [psck])
                        else:
                            MM([(psc[:, h * 128:(h + 1) * 128], kdT[:, h, cs], qdT[:, h, cs], True, True) for h in range(4)],
                               [("kdT", h) for h in range(4)] + [("qdT", h) for h in range(4)], [psck])
                        sc, sck = scbuf()
                        Vtt(sc[:, :], psc[:, :], cmask4[:, :], ALU.mult, [psck, "cmask4"], [sck])
                        st["sc"] = (sc, sck)
                    pts = []
                    for hp in range(2):
                        pt, ptk = bank()
                        MM([(pt[:, j * 256:(j + 1) * 256], ke[:, hp * 2 + j, cs],
                             v_sb[:, s, (hp * 2 + j) * 256:(hp * 2 + j + 1) * 256], True, True) for j in range(2)],
                           [("ke", hp * 2), ("ke", hp * 2 + 1), ("v", s)], [ptk])
                        pts.append((pt, ptk))
                    st["pts"] = pts
                    return st

                def supd(s, st):
                    for h in range(4):
                        pt, ptk = st["pts"][h // 2]
                        Vtt(S32[:, h, :], S32[:, h, :], pt[:, (h % 2) * 256:(h % 2 + 1) * 256], ALU.add,
                            [("S32", h), ptk], [("S32", h)])
                        Vsmul(S32[:, h, :], S32[:, h, :], E1[:, h, s * 128 + 127:s * 128 + 128],
                              [("S32", h), ("E1", h)], [("S32", h)])

                def o_and_act(s, st):
                    cs = slice(s * 128, (s + 1) * 128)
                    pos = []
                    if main:
                        sc, sck = st["sc"]
                        Sb = S_bfs[s % 2]
                        for hp in range(2):
                            po, pok = bank()
                            mms = []
                            for j in range(2):
                                h = hp * 2 + j
                                mms.append((po[:, j * 256:(j + 1) * 256], sc[:, h * 128:(h + 1) * 128],
                                            v_sb[:, s, h * 256:(h + 1) * 256], True, False))
                                mms.append((po[:, j * 256:(j + 1) * 256], qdT[:, h, cs], Sb[:, h, :], False, True))
                            MM(mms, [sck, ("v", s), ("qdT", hp * 2), ("qdT", hp * 2 + 1), ("Sbf", s % 2)], [pok])
                            pos.append((po, pok))
                    A(S_bfs[(s + 1) % 2][:, :, :], S32[:, :, :], AF.Copy, [("S32", h) for h in range(4)], [("Sbf", (s + 1) % 2)])
                    if main:
                        for h in range(4):
                            po, pok = pos[h // 2]
                            i = s * 4 + h
                            jt, jtk = tmp()
                            A(jt[:, :].bitcast(BF16)[:, 0:256], po[:, (h % 2) * 256:(h % 2 + 1) * 256], AF.Square, [pok],
                              [("sso", i), jtk], accum_out=sso[:, i:i + 1])
                        A(lno[:, s * 4:(s + 1) * 4], sso[:, s * 4:(s + 1) * 4], AF.Ln, [("sso", s * 4 + h) for h in range(4)],
                          [("lno", s)], scale=1.0 / 256, bias=EPS)
                        A(rso[:, s * 4:(s + 1) * 4], lno[:, s * 4:(s + 1) * 4], AF.Exp, [("lno", s)], [("rso", s)], scale=-0.5)
                    return pos

                def og_stage(s, pos):
                    if main:
                        for h in range(4):
                            po, pok = pos[h // 2]
                            i = s * 4 + h
                            sgs = sg[:, s, h * 256:(h + 1) * 256]
                            Vstt(sgs, po[:, (h % 2) * 256:(h % 2 + 1) * 256], rso[:, i:i + 1], sgs, ALU.mult, ALU.mult,
                                 [pok, ("rso", s), ("sg", s, h)], [("sg", s, h)])

                sts = {0: phaseA(0)}
                pos_prev = None
                for s in range(4):
                    supd(s, sts[s])
                    if s + 1 < 4:
                        sts[s + 1] = phaseA(s + 1)
                    pos = o_and_act(s, sts[s])
                    if pos_prev is not None:
                        og_stage(s - 1, pos_prev)
                    pos_prev = pos
                og_stage(3, pos_prev)
                if last_prev:
                    for h in range(4):
                        Vsmul(S32[:, h, :], S32[:, h, :], flag_ap, [("S32", h), "cst"], [("S32", h)])
                    A(S_bfs[0][:, :, :], S32[:, :, :], AF.Copy, [("S32", h) for h in range(4)], [("Sbf", 0)])
                if not main:
                    la_stage()
                    return
                for kc in range(8):
                    pb, pbk = bbank()
                    TR([(pb[:, s * 128:(s + 1) * 128], sg[:, s, kc * 128:(kc + 1) * 128]) for s in range(4)],
                       [("sg", s, kc // 2) for s in range(4)], [pbk])
                    g_ap = cst[:, C_GNW + kc % 2:C_GNW + kc % 2 + 1]
                    if kc % 2:
                        A(ogT[:, kc, :], pb[:, 0:512], AF.Copy, [pbk, "cst"], [("ogT", kc)], scale=g_ap)
                    else:
                        Vsmul(ogT[:, kc, :], pb[:, 0:512], g_ap, [pbk, "cst"], [("ogT", kc)])

            def cw_ap(m, j):
                c = C_CW + m * 3 + j
                return cst[:, c:c + 1]

            def conv_stage(first=False):
                for mg in range(2):
                    wcb, wcbk = next_w(("w_in", 0, CB0 + mg * 512, 512))
                    wcc, wcck = next_w(("w_in", 0, CC0 + mg * 512, 512))
                    wcx, wcxk = next_w(("w_in", 0, CX0 + mg * 512, 512))
                    for mm in range(4):
                        m = mg * 4 + mm
                        ms = slice(mm * 128, (mm + 1) * 128)
                        pc, pck = bank()
                        MM([(pc[:, :], wcc[:, kc, ms], hT[:, kc, :], kc == 0, kc == 7) for kc in range(8)], [wcck] + HT_ALL, [pck])
                        px, pxk = bank()
                        MM([(px[:, :], wcx[:, kc, ms], hT[:, kc, :], kc == 0, kc == 7) for kc in range(8)], [wcxk] + HT_ALL, [pxk])
                        pcb, pcbk = bank()
                        MM([(pcb[:, :], wcb[:, kc, ms], hT[:, kc, :], kc == 0, kc == 7) for kc in range(8)], [wcbk] + HT_ALL, [pcbk])
                        if first:
                            ph, phk = bank()
                            MM([(ph[:, 0:2], wcc[:, kc, ms], hTh[:, kc, :], kc == 0, kc == 7) for kc in range(8)]
                               + [(ph[:, 2:4], wcx[:, kc, ms], hTh[:, kc, :], kc == 0, kc == 7) for kc in range(8)],
                               [wcck, wcxk, "hTh"], [phk])
                            th, thk = tmp()
                            A(th[:, 0:2], ph[:, 0:2], AF.Copy, [phk], [thk])
                            Vtt(th[:, 2:4], th[:, 0:2], ph[:, 2:4], ALU.mult, [thk, phk], [thk])
                            Vsmul(uh[:, m, :], th[:, 2:4], flag_ap, [thk, "cst"], [("uh", m)])
                        t, tk = tmp()
                        A(t[:, :], pc[:, :], AF.Copy, [pck], [tk])
                        u, uk = ubuf()
                        A(u[:, 0:2], uh[:, m, :], AF.Copy, [("uh", m)], [uk])
                        Vtt(u[:, 2:514], t[:, :], px[:, :], ALU.mult, [tk, pxk, uk], [uk])
                        A(uh[:, m, :], u[:, 512:514], AF.Copy, [uk], [("uh", m)])
                        a_, ak = tmp()
                        Vsmul(a_[:, :], u[:, 2:514], cw_ap(m, 2), [uk, "cst"], [ak])
                        Vstt(a_[:, :], u[:, 1:513], cw_ap(m, 1), a_[:, :], ALU.mult, ALU.add, [uk, "cst", ak], [ak])
                        Vstt(a_[:, :], u[:, 0:512], cw_ap(m, 0), a_[:, :], ALU.mult, ALU.add, [uk, "cst", ak], [ak])
                        Vtt(cbuT[:, m, :], a_[:, :], pcb[:, :], ALU.mult, [ak, pcbk], [("cbuT", m)])
                    done_w()
                    done_w()
                    done_w()

            def merge_stage(xb, par):
                OGT_ALL = [("ogT", kc) for kc in range(8)]
                CBU_ALL = [("cbuT", kc) for kc in range(8)]
                for mg in range(2):
                    wa, wak = next_w(("w_pa", 0, mg * 512, 512))
                    wga_, wgak = next_w(("w_in", 0, GA0 + mg * 512, 512))
                    wb, wbk = next_w(("w_pb", 0, mg * 512, 512))
                    wgb, wgbk = next_w(("w_in", 0, GB0 + mg * 512, 512))
                    for mm in range(4):
                        m = mg * 4 + mm
                        ms = slice(mm * 128, (mm + 1) * 128)
                        pya, pyak = bank()
                        MM([(pya[:, :], wa[:, kc, ms], ogT[:, kc, :], kc == 0, kc == 7) for kc in range(8)], [wak] + OGT_ALL, [pyak])
                        pga, pgak = bank()
                        MM([(pga[:, :], wga_[:, kc, ms], hT[:, kc, :], kc == 0, kc == 7) for kc in range(8)], [wgak] + HT_ALL, [pgak])
                        ra, rak = sigmoid(pga, pgak)
                        Vtt(ra[:, :], ra[:, :], pya[:, :], ALU.mult, [rak, pyak], [rak])
                        pyb, pybk = bank()
                        MM([(pyb[:, :], wb[:, kc, ms], cbuT[:, kc, :], kc == 0, kc == 7) for kc in range(8)], [wbk] + CBU_ALL, [pybk])
                        pgb, pgbk = bank()
                        MM([(pgb[:, :], wgb[:, kc, ms], hT[:, kc, :], kc == 0, kc == 7) for kc in range(8)], [wgbk] + HT_ALL, [pgbk])
                        rb, rbk = sigmoid(pgb, pgbk)
                        Vtt(rb[:, :], rb[:, :], pyb[:, :], ALU.mult, [rbk, pybk], [rbk])
                        Vtt(zT[:, m, :], ra[:, :], rb[:, :], ALU.add, [rak, rbk], [("zT", m)])
                    for _ in range(4):
                        done_w()
                ZT_ALL = [("zT", kc) for kc in range(8)]
                for n in range(2):
                    wo, wok = next_w(("w_o", 0, n * 512, 512))
                    ns = slice(n * 512, (n + 1) * 512)
                    for s in range(4):
                        p_, pk = bank()
                        MM([(p_[:, :], zT[:, kc, s * 128:(s + 1) * 128], wo[:, kc, :], kc == 0, kc == 7) for kc in range(8)],
                           [wok] + ZT_ALL, [pk])
                        t, tk = tmp()
                        Vtt(t[:, :], p_[:, :], g1b[:, ns], ALU.mult, [pk, "g1b"], [tk])
                        xk = ("x", par, s)
                        Vtt(xb[:, s, ns], xb[:, s, ns], t[:, :], ALU.add, [xk, tk], [xk])
                    done_w()

            def mlp_stage(xb, par, g, hoist=None):
                stage_norm(xb, par, a2T, "a2T", 16)
                for cg in range(8):
                    w1_, w1k = next_w(("w1", 0, cg * 512, 512))
                    for jj in range(4):
                        j = cg * 4 + jj
                        p_, pk = bank()
                        MM([(p_[:, :], w1_[:, kc, jj * 128:(jj + 1) * 128], hT[:, kc, :], kc == 0, kc == 7) for kc in range(8)],
                           [w1k] + HT_ALL, [pk])
                        t, tk = tmp()
                        A(t[:, :], p_[:, :], AF.Relu, [pk], [tk])
                        Vtt(aT[:, j, :], t[:, :], t[:, :], ALU.mult, [tk], [("aT", j)])
                    done_w()
                for n in range(2):
                    ns = slice(n * 512, (n + 1) * 512)
                    bks = [bank() for _ in range(4)]
                    for jg in range(4):
                        w2_, w2k = next_w(("w2", jg * 1024, n * 512, 512))
                        for s in range(4):
                            MM([(bks[s][0][:, :], aT[:, jg * 8 + jj, s * 128:(s + 1) * 128], w2_[:, jj, :],
                                 jg == 0 and jj == 0, jg == 3 and jj == 7) for jj in range(8)],
                               [w2k] + [("aT", jg * 8 + jj) for jj in range(8)], [bks[s][1]])
                        done_w()
                    for s in range(4):
                        t, tk = tmp()
                        Vtt(t[:, :], bks[s][0][:, :], g2b[:, ns], ALU.mult, [bks[s][1], "g2b"], [tk])
                        xk = ("x", par, s)
                        Vtt(xb[:, s, ns], xb[:, s, ns], t[:, :], ALU.add, [xk, tk], [xk])
                    if n == 0 and hoist is not None:
                        hoist()
                for s in range(4):
                    xk = ("x", par, s)
                    jt, jtk = tmp()
                    A(jt[:, :].bitcast(BF16), xb[:, s, :], AF.Square, [xk], [jtk, ("ss2", s)], accum_out=ss2[:, s:s + 1])
                    A(lnv2[:, s:s + 1], ss2[:, s:s + 1], AF.Ln, [("ss2", s)], [("lnv2", s)], scale=1.0 / D, bias=EPS)
                    A(rstd2[:, s:s + 1], lnv2[:, s:s + 1], AF.Exp, [("lnv2", s)], [("rstd2", s)], scale=-0.5)
                    A(xb[:, s, :], xb[:, s, :], AF.Copy, [xk, ("rstd2", s)], [xk], scale=rstd2[:, s:s + 1])
                    Vtt(xb[:, s, :], xb[:, s, :], fnw[:, :], ALU.mult, [xk, "fnw"], [xk])
                t0 = (g % 4) * T
                dst = out_d[t0:t0 + T, :].rearrange("(s p) d -> p s d", p=128)
                P.op("sp", lambda e: e.dma_start(out=dst, in_=xb[:, :, :]), [("x", par, s) for s in range(4)],
                     [("out", g)], dma=f"o{par}")

            def special_prep(xb, par):
                Vsmul(xn32, xb[:, 0, :], rstd[:, 0:1], [("x", par, 0), ("rstd", 0)], ["xn32"])
                for half in range(2):
                    p_, pk = bank()
                    def fn(e, p_=p_, half=half):
                        ins = None
                        for j in range(4):
                            kc = half * 4 + j
                            ins = e.transpose(out=p_[:, j * 128:(j + 1) * 128], in_=xn32[:, kc * 128:(kc + 1) * 128],
                                              identity=identf[:, :])
                        return ins
                    P.op("pe", fn, ["xn32", "identf"], [pk])
                    for j in range(4):
                        kc = half * 4 + j
                        Vts(h32T[:, kc, :], p_[:, j * 128:(j + 1) * 128], a1T[:, kc:kc + 1], modT[:, kc:kc + 1],
                            ALU.mult, ALU.add, [pk, "a1T", ("modT", 0), ("modT", 4)], ["h32T"])

            for ci in range(4):
                ada_chunk(ci)
            mod_finish(1)
            rest = [4, 5, 6, 7, 8, 9, 10, 11]
            for g in range(8):
                par = g % 2
                xb = xbuf[par]
                if g >= 4:
                    P.wmode = "t4" if g == 4 else "t5" if g == 5 else "scr"
                    P.widx = 0
                if g == 0:
                    stage_norm(xb, par, a1T, "a1T", 0)
                    la_stage()
                hoist = None
                if g + 1 < 8:
                    def hoist(g1=g + 1, with_la=(g >= 4)):
                        stage_norm(xbuf[g1 % 2], g1 % 2, a1T, "a1T", 0)
                        if with_la:
                            la_stage()
                if g < 4:
                    gla_stage(False, g == 3, hoist=hoist)
                    for ci in rest[g * 2:g * 2 + 2]:
                        ada_chunk(ci)
                    if g == 0:
                        rec_convs(100)
                    if g == 3:
                        mod_finish(2)
                else:
                    if g == 4:
                        special_prep(xb, par)
                    gla_stage(True, False, special=(g == 4))
                    if g == 4:
                        rec_xload(5)
                    conv_stage(first=(g == 4))
                    merge_stage(xb, par)
                    mlp_stage(xb, par, g, hoist=hoist)
                if g + 2 < 8 and g != 3:
                    rec_xload(g + 2)
            P.op("sp", None, [("out", g) for g in range(4, 8)], [])

        wplan = []
        record(Prog(True, wplan))
        P = Prog(False, wplan)
        record(P)
        assert P.wi == len(wplan), (P.wi, len(wplan))

        sem_names = P.sems()
        S = {n: es.enter_context(nc.semaphore(n)) for n in sem_names}
        block = es.enter_context(nc.Block())

        def run_stream(eng_name):
            def body(e):
                for (waits, fn, sem, amt) in P.streams[eng_name]:
                    for (s_, v_) in waits:
                        e.wait_ge(S[s_], v_)
                    if fn is not None:
                        fn(e).then_inc(S[sem], amt)
            return body

        block.sync(run_stream("sp"))
        block.gpsimd(run_stream("pool"))
        block.tensor(run_stream("pe"))
        block.scalar(run_stream("act"))
        block.vector(run_stream("dve"))
    return nc


_NC = None


def kernel(x, c, w_ada, b_ada, norm1_w, w_in, w_gate_up, b_gate, gla_norm_w, conv_w,
           w_proj_a, w_proj_b, w_out, norm2_w, w_mlp1, w_mlp2, final_norm_w):
    global _NC
    f = lambda a: np.ascontiguousarray(np.asarray(a, dtype=np.float32))
    x = f(x)
    c = f(c)
    b_ada = f(b_ada)[0]
    shared = {
        "w_ada": f(w_ada)[0], "w_in": f(w_in)[0], "w_pa": f(w_proj_a)[0], "w_pb": f(w_proj_b)[0],
        "w_o": f(w_out)[0], "w1": f(w_mlp1)[0], "w2": f(w_mlp2)[0],
        "fnw_b": np.ascontiguousarray(np.broadcast_to(f(final_norm_w)[None, :], (128, D))),
        "bgate_b": np.ascontiguousarray(np.broadcast_to(
            np.concatenate([b_ada[2 * D:3 * D], b_ada[5 * D:6 * D]])[None, :], (128, 2 * D))),
        "wg_aug": np.ascontiguousarray(np.concatenate([f(w_gate_up)[0], f(b_gate)[0][None, :]], axis=0)),
    }
    colT = lambda v: np.ascontiguousarray(v.reshape(-1, 128).T)
    cbase = np.zeros((128, NCONST), np.float32)
    cbase[:, C_BADA:C_BADA + 32] = np.concatenate(
        [colT(b_ada[0:D]), colT(b_ada[D:2 * D]), colT(b_ada[3 * D:4 * D]), colT(b_ada[4 * D:5 * D])], axis=1)
    cbase[:, C_N1:C_N1 + 8] = colT(f(norm1_w)[0])
    cbase[:, C_N2:C_N2 + 8] = colT(f(norm2_w)[0])
    cwl = f(conv_w)[0]
    cbase[:, C_CW:C_CW + 24] = np.transpose(cwl.reshape(3, 8, 128), (2, 1, 0)).reshape(128, 24)
    cbase[:, C_GNW:C_GNW + 2] = colT(f(gla_norm_w)[0])
    if _NC is None:
        _NC = build_nc()
    in_maps = []
    for i in range(8):
        b, hf = i // 2, i % 2
        cs = cbase.copy()
        cs[:, C_CT:C_CT + 8] = colT(c[b])
        cs[:, C_FLAG] = float(hf)
        m = dict(shared)
        m["x_cur"] = np.ascontiguousarray(x[b, hf * TOK:(hf + 1) * TOK])
        m["x_prev"] = np.ascontiguousarray(x[b, 0:TOK])
        m["consts"] = cs
        in_maps.append(m)
    res = run_bass_kernel_spmd(_NC, in_maps, core_ids=list(range(8)))
    out = np.empty((4, 2 * TOK, D), np.float32)
    for i in range(8):
        b, hf = i // 2, i % 2
        out[b, hf * TOK:(hf + 1) * TOK] = np.asarray(res.results[i]["out"]).reshape(TOK, D)
    return out
```

```python
import numpy as np
from contextlib import ExitStack
import concourse.bass as bass
import concourse.mybir as mybir
from concourse.bass_utils import run_bass_kernel_spmd

F32 = mybir.dt.float32
BF16 = mybir.dt.bfloat16
AF = mybir.ActivationFunctionType
ALU = mybir.AluOpType

D = 1024
TOK = 2048
T = 512
NW = 5
NTMP = 8
NSCR = 40
PRECONV = ("w2",)
EPS = 1e-6
Q0, K0, V0, G0, LR0, CB0, CC0, CX0, GA0, GB0 = 0, 512, 1024, 2048, 3072, 3088, 4112, 5136, 6160, 7184
C_CT, C_BADA, C_N1, C_N2, C_CW, C_GNW, C_FLAG, NCONST = 0, 8, 40, 48, 56, 80, 82, 84


class Ev:
    __slots__ = ("eng", "sem", "val", "know", "dma")

    def __init__(self, eng, sem, val, know, dma):
        self.eng, self.sem, self.val, self.know, self.dma = eng, sem, val, know, dma


class Prog:
    ENGS = ("pe", "act", "dve", "pool", "sp")

    def __init__(self, dry, wplan):
        self.dry = dry
        self.wplan = wplan
        self.streams = {e: [] for e in self.ENGS}
        self.cnt = {e: 0 for e in self.ENGS}
        self.dcnt = {}
        self.know = {e: {} for e in self.ENGS}
        self.last_w = {}
        self.readers = {}
        self.groups = {}
        self.cur_view = {}
        self.barrier = {}
        self.name_keys = {}
        self.bi = 0
        self.bbi = 0
        self.ti = 0
        self.wi = 0
        self.wdone = 0
        self.wmode = "cast"
        self.widx = -1

    def op(self, eng, fn, reads=(), writes=(), dma=None):
        if self.dry:
            return
        deps = []
        for k in list(reads) + list(writes):
            name = k[0] if isinstance(k, tuple) else k
            self.name_keys.setdefault(name, set()).add(k)
            for g in self.groups.get(name, ()):
                if self.cur_view.get(g) != name:
                    old = self.cur_view.get(g)
                    evs = []
                    if old is not None:
                        for kk in self.name_keys.get(old, ()):
                            if kk in self.last_w:
                                evs.append(self.last_w[kk])
                            evs += self.readers.get(kk, [])
                    self.barrier[g] = evs
                    self.cur_view[g] = name
                deps += self.barrier.get(g, [])
        is_dma = dma is not None
        for k in reads:
            ev = self.last_w.get(k)
            if ev is not None:
                deps.append(ev)
        for k in writes:
            ev = self.last_w.get(k)
            if ev is not None and (is_dma or ev.dma or ev.eng != eng or eng != "pe"):
                deps.append(ev)
            for ev in self.readers.get(k, []):
                if is_dma or ev.dma or ev.eng != eng or eng != "pe":
                    deps.append(ev)
        kn = self.know[eng]
        waits = {}
        for ev in deps:
            if kn.get(ev.sem, 0) >= ev.val:
                continue
            waits[ev.sem] = max(waits.get(ev.sem, 0), ev.val)
            for s_, v_ in ev.know.items():
                if kn.get(s_, 0) < v_:
                    kn[s_] = v_
        if fn is None:
            self.streams[eng].append((list(waits.items()), None, None, 0))
            return
        if is_dma:
            sem = dma
            self.dcnt[sem] = self.dcnt.get(sem, 0) + 16
            val = self.dcnt[sem]
            amt = 16
        else:
            sem = "E_" + eng
            self.cnt[eng] += 1
            val = self.cnt[eng]
            amt = 1
        evk = dict(kn)
        evk[sem] = val
        ev = Ev(eng, sem, val, evk, is_dma)
        for k in reads:
            self.readers.setdefault(k, []).append(ev)
        for k in writes:
            self.last_w[k] = ev
            self.readers[k] = []
        self.streams[eng].append((list(waits.items()), fn, sem, amt))

    def sems(self):
        s = {"E_" + e for e in self.ENGS}
        s |= set(self.dcnt.keys())
        return sorted(s)


def build_nc():
    nc = bass.Bass("TRN2", target_bir_lowering=False)

    def din(name, shape):
        return nc.dram_tensor(name, shape, F32, kind="ExternalInput").ap()

    x_cur = din("x_cur", [TOK, D])
    x_prev = din("x_prev", [TOK, D])
    W = {
        "w_ada": din("w_ada", [D, 6 * D]),
        "w_in": din("w_in", [D, 8208]),
        "w_pa": din("w_pa", [D, D]),
        "w_pb": din("w_pb", [D, D]),
        "w_o": din("w_o", [D, D]),
        "w1": din("w1", [D, 4 * D]),
        "w2": din("w2", [4 * D, D]),
    }
    consts_d = din("consts", [128, NCONST])
    fnwb_d = din("fnw_b", [128, D])
    bgb_d = din("bgate_b", [128, 2 * D])
    wga_d = din("wg_aug", [17, 512])
    out_d = nc.dram_tensor("out", [TOK, D], F32, kind="ExternalOutput").ap()
    wscr = nc.dram_tensor("wscr", [NSCR, 128, 4096], BF16, kind="Internal").ap()

    with ExitStack() as es:
        def sb(name, shape, dt):
            return es.enter_context(nc.sbuf_tensor(name, shape, dt))

        xbuf = [sb(f"xbuf{i}", [128, 4, D], F32) for i in range(2)]
        xn = sb("xn", [128, 4, D], BF16)
        hT = sb("hT", [128, 8, T], BF16)
        wsl = [sb(f"wsl{i}", [128, 8, 512], BF16) for i in range(NW)]
        wlr = sb("wlr", [128, 8, 16], BF16)
        lrT = sb("lrT", [32, T], F32)
        wg = sb("wg", [32, 512], F32)
        spb = sb("spb", [128, 4, 512], F32)
        E1 = sb("E1", [128, 4, T], F32)
        spc = sb("spc", [128, 4, 512], F32)
        qdT = sb("qdT", [128, 4, T], BF16)
        kdT = sb("kdT", [128, 4, T], BF16)
        ke = sb("ke", [128, 4, T], BF16)
        big = sb("big", [128, 16384], BF16)
        S32 = sb("S32", [128, 4, 256], F32)
        S_bfs = [sb(f"S_bf{i}", [128, 4, 256], BF16) for i in range(2)]
        scb = [sb(f"scb{i}", [128, 512], BF16) for i in range(2)]
        cmask4 = sb("cmask4", [128, 512], BF16)
        ss2 = sb("ss2", [128, 4], F32)
        lnv2 = sb("lnv2", [128, 4], F32)
        rstd2 = sb("rstd2", [128, 4], F32)
        ub = [sb(f"ub{i}", [128, 514], F32) for i in range(2)]
        uh = sb("uh", [128, 8, 2], F32)
        hTh = sb("hTh", [128, 8, 2], BF16)
        tmps = [sb(f"tmp{i}", [128, 512], F32) for i in range(NTMP)]
        cst = sb("cst", [128, NCONST], F32)
        fnw = sb("fnw", [128, D], F32)
        g1b = sb("g1b", [128, D], F32)
        g2b = sb("g2b", [128, D], F32)
        identf = sb("identf", [128, 128], F32)
        ident = sb("ident", [128, 128], BF16)
        ones_bf = sb("ones_bf", [128, 128], BF16)
        uneg = sb("uneg", [128, 128], F32)
        ce = sb("ce", [128, 8], F32)
        cact = sb("cact", [128, 8], F32)
        cact_bf = sb("cact_bf", [128, 8], BF16)
        modT = sb("modT", [128, 32], F32)
        a1T = sb("a1T", [128, 8], F32)
        a2T = sb("a2T", [128, 8], F32)
        ss = sb("ss", [128, 4], F32)
        lnv = sb("lnv", [128, 4], F32)
        rstd = sb("rstd", [128, 4], F32)
        sso = sb("sso", [128, 16], F32)
        lno = sb("lno", [128, 16], F32)
        rso = sb("rso", [128, 16], F32)

        psf = [es.enter_context(nc.psum_tensor(f"psf{i}", [128, 512], F32)) for i in range(8)]
        psb = [p_[:, :].bitcast(BF16) for p_ in psf]

        v_sb = big[:, 0:4096].rearrange("p (s c) -> p s c", s=4)
        ogT = big[:, 0:4096].rearrange("p (k c) -> p k c", k=8)
        sg = big[:, 4096:8192].rearrange("p (s c) -> p s c", s=4)
        cbuT = big[:, 8192:12288].rearrange("p (k c) -> p k c", k=8)
        zT = big[:, 12288:16384].rearrange("p (k c) -> p k c", k=8)
        aT = big[:, :].rearrange("p (j c) -> p j c", j=32)
        wkp = big[:, 4096:8192].rearrange("p (k c) -> p k c", k=8)
        wvp = [big[:, 8192:12288].rearrange("p (k c) -> p k c", k=8),
               big[:, 12288:16384].rearrange("p (k c) -> p k c", k=8)]
        cbm = qdT[:, 0:2, :].rearrange("p a (k c) -> p (a k) c", k=4)
        w32q = xbuf[1][:, :, :].rearrange("p s (a c) -> p (s a) c", a=2)
        w32k = big[:, 8192:16384].bitcast(F32).rearrange("p (k c) -> p k c", k=8)
        h32T = spc[:, 0:2, :].rearrange("p a (k c) -> p (a k) c", k=4)
        xn32 = spc[:, 2:4, :].rearrange("p a c -> p (a c)")
        qd32 = spc[:, 2, :].rearrange("p (h c) -> p h c", h=4)
        kd32 = spc[:, 3, :].rearrange("p (h c) -> p h c", h=4)

        def record(P):
            P.groups = {"v": ["A"], "ogT": ["A"], "sg": ["B"], "cbuT": ["C"], "zT": ["Dg"],
                        "aT": ["A", "B", "C", "Dg"], "w32k": ["C", "Dg"],
                        "wkp": ["B"], "wvp0": ["C"], "wvp1": ["Dg"],
                        "cbm": ["Q"], "qdT": ["Q"]}

            def A(out, in_, func, r, w, **kw):
                P.op("act", lambda e: e.activation(out=out, in_=in_, func=func, **kw), r, w)

            def Vtt(out, a, b, op, r, w):
                P.op("dve", lambda e: e.tensor_tensor(out=out, in0=a, in1=b, op=op), r, w)

            def Vts(out, a, s1, s2, op0, op1, r, w):
                P.op("dve", lambda e: e.tensor_scalar(out=out, in0=a, scalar1=s1, scalar2=s2, op0=op0, op1=op1), r, w)

            def Vsmul(out, a, s, r, w):
                P.op("dve", lambda e: e.tensor_scalar_mul(out=out, in0=a, scalar1=s), r, w)

            def Vsadd(out, a, s, r, w):
                P.op("dve", lambda e: e.tensor_scalar_add(out=out, in0=a, scalar1=s), r, w)

            def Vstt(out, a, s, b, op0, op1, r, w):
                P.op("dve", lambda e: e.scalar_tensor_tensor(out=out, in0=a, scalar=s, in1=b, op0=op0, op1=op1), r, w)

            def Vcopy(out, a, r, w):
                P.op("dve", lambda e: e.tensor_copy(out=out, in_=a), r, w)

            def Vrecip(out, a, r, w):
                P.op("dve", lambda e: e.reciprocal(out=out, in_=a), r, w)

            def MM(mms, r, w):
                def fn(e):
                    ins = None
                    for (o, l, rh, st, sp_) in mms:
                        ins = e.matmul(out=o, lhsT=l, rhs=rh, start=st, stop=sp_)
                    return ins
                P.op("pe", fn, r, w)

            def TR(trs, r, w):
                def fn(e):
                    ins = None
                    for (o, i) in trs:
                        ins = e.transpose(out=o, in_=i, identity=ident[:])
                    return ins
                P.op("pe", fn, list(r) + ["ident"], w)

            def bank():
                i = P.bi % 8
                P.bi += 1
                return psf[i], ("ps", i)

            def bbank():
                i = P.bi % 8
                P.bi += 1
                return psb[i], ("ps", i)

            def tmp():
                i = P.ti % NTMP
                P.ti += 1
                return tmps[i], ("tmp", i)

            sci = [0]

            def scbuf():
                i = sci[0] % 2
                sci[0] += 1
                return scb[i], ("sc", i)

            ubi = [0]

            def ubuf():
                i = ubi[0] % 2
                ubi[0] += 1
                return ub[i], ("u", i)

            def rec_load(j):
                wname, r0, c0, ncols, mode, idx = P.wplan[j]
                slot = j % NW
                if mode == "scr":
                    src = wscr[idx]
                    dst = wsl[slot][:, :, :].rearrange("p k c -> p (k c)")
                    P.op("pool", lambda e: e.dma_start(out=dst, in_=src), [("scr", idx)], [("w", slot)], dma=f"w{slot}")
                else:
                    src = W[wname][r0:r0 + 1024, c0:c0 + ncols].rearrange("(kc p) c -> p kc c", p=128)
                    dst = wsl[slot][:, :, 0:ncols]
                    P.op("pool", lambda e: e.dma_start(out=dst, in_=src), [], [("w", slot)], dma=f"w{slot}")
                if mode == "castwb":
                    wsrc = wsl[slot][:, :, :].rearrange("p k c -> p (k c)")
                    wdst = wscr[idx]
                    P.op("sp", lambda e: e.dma_start(out=wdst, in_=wsrc), [("w", slot)], [("scr", idx)], dma=f"wb{slot}")

            def is_preconv(spec):
                return spec[0] in PRECONV

            def next_w(spec):
                mode = P.wmode
                if mode in ("t4", "t5"):
                    if is_preconv(spec):
                        mode = "scr"
                    elif P.widx % 2 == 0:
                        mode = "castwb" if mode == "t4" else "scr"
                    else:
                        mode = "cast4" if mode == "t4" else "castwb"
                full = tuple(spec) + (mode, P.widx)
                if P.wmode != "cast":
                    P.widx += 1
                if P.dry:
                    P.wplan.append(full)
                    return wsl[0], ("w", 0)
                i = P.wi
                P.wi += 1
                assert P.wplan[i] == full, (i, P.wplan[i], full)
                return wsl[i % NW], ("w", i % NW)

            def done_w():
                if P.dry:
                    return
                j = P.wdone + NW
                P.wdone += 1
                if j < len(P.wplan):
                    rec_load(j)

            HT_ALL = [("hT", kc) for kc in range(8)]

            conv_list = [] if P.dry else [e_ for e_ in P.wplan if e_[4] == "scr" and is_preconv(e_) and e_[5] >= 0]
            seen_cv = set()
            conv_todo = []
            for e_ in conv_list:
                if e_[5] not in seen_cv:
                    seen_cv.add(e_[5])
                    conv_todo.append(e_)

            def rec_convs(n):
                for _ in range(n):
                    if not conv_todo:
                        return
                    wname, r0, c0, ncols, _m, idx = conv_todo.pop(0)
                    src = W[wname][r0:r0 + 1024, c0:c0 + ncols].rearrange("(kc p) c -> p kc c", p=128)
                    dst = wscr[idx].rearrange("p (k c) -> p k c", k=8)
                    P.op("pool", lambda e, dst=dst, src=src: e.dma_start(out=dst, in_=src), [], [("scr", idx)], dma=f"cv{idx}")

            P.op("sp", lambda e: e.dma_start(out=cst[:, :], in_=consts_d), [], ["cst"], dma="c0")
            P.op("sp", lambda e: e.dma_start(out=wg[0:17, :], in_=wga_d), [], ["wg"], dma="c1")
            P.op("sp", lambda e: e.dma_start(out=g1b[:, :], in_=bgb_d[:, 0:D]), [], ["g1b"], dma="c2")
            P.op("sp", lambda e: e.dma_start(out=g2b[:, :], in_=bgb_d[:, D:2 * D]), [], ["g2b"], dma="c3")
            P.op("sp", lambda e: e.dma_start(out=fnw[:, :], in_=fnwb_d), [], ["fnw"], dma="c4")

            P.op("pool", lambda e: e.memset(identf[:, :], 0.0), [], ["identf"])
            P.op("pool", lambda e: e.affine_select(out=identf[:, :], in_=identf[:, :], pattern=[[-1, 128]],
                                                   compare_op=ALU.not_equal, fill=1.0, base=0, channel_multiplier=1),
                 ["identf"], ["identf"])
            P.op("pool", lambda e: e.memset(cmask4[:, :], 1.0), [], ["cmask4"])
            P.op("pool", lambda e: e.affine_select(out=cmask4[:, :], in_=cmask4[:, :], pattern=[[0, 4], [1, 128]],
                                                   compare_op=ALU.is_ge, fill=0.0, base=0, channel_multiplier=-1),
                 ["cmask4"], ["cmask4"])
            P.op("pool", lambda e: e.memset(uneg[:, :], -1.0 / 16.0), [], ["uneg"])
            P.op("pool", lambda e: e.affine_select(out=uneg[:, :], in_=uneg[:, :], pattern=[[1, 128]],
                                                   compare_op=ALU.is_ge, fill=0.0, base=0, channel_multiplier=-1),
                 ["uneg"], ["uneg"])
            P.op("pool", lambda e: e.memset(lrT[:, :], 1.0), [], ["lrT"])
            P.op("pool", lambda e: e.dma_start(out=wlr[:, :, :], in_=W["w_in"][:, LR0:LR0 + 16].rearrange("(kc p) c -> p kc c", p=128)),
                 [], ["wlr"], dma="c7")
            if not P.dry:
                for j in range(min(4, len(P.wplan))):
                    rec_load(j)
            P.op("pool", lambda e: e.dma_start(out=wkp, in_=W["w_in"][:, K0:K0 + 512].rearrange("(kc p) c -> p kc c", p=128)),
                 [], ["wkp"], dma="c8")
            for n_ in range(2):
                P.op("pool", lambda e, n_=n_: e.dma_start(out=wvp[n_], in_=W["w_in"][:, V0 + n_ * 512:V0 + (n_ + 1) * 512].rearrange("(kc p) c -> p kc c", p=128)),
                     [], [f"wvp{n_}"], dma=f"c{9 + n_}")
            if not P.dry:
                for j in range(4, min(NW, len(P.wplan))):
                    rec_load(j)

            rec_convs(8)

            Vcopy(ident[:, :], identf[:, :], ["identf"], ["ident"])
            P.op("dve", lambda e: e.memset(ones_bf[:, :], 1.0), [], ["ones"])
            P.op("dve", lambda e: e.memset(S32[:, :, :], 0.0), [], [("S32", h) for h in range(4)])
            P.op("dve", lambda e: e.memset(S_bfs[0][:, :, :], 0.0), [], [("Sbf", 0)])
            P.op("dve", lambda e: e.memset(S_bfs[1][:, :, :], 0.0), [], [("Sbf", 1)])
            P.op("dve", lambda e: e.memset(uh[:, :, :], 0.0), [], [("uh", m) for m in range(8)])

            def rec_xload(g):
                src_t = x_prev if g < 4 else x_cur
                t0 = (g % 4) * T
                par = g % 2
                src = src_t[t0:t0 + T, :].rearrange("(s p) d -> p s d", p=128)
                P.op("sp", lambda e: e.dma_start(out=xbuf[par][:, :, :], in_=src), [],
                     [("x", par, s) for s in range(4)], dma=f"x{par}")

            rec_xload(0)
            rec_xload(1)

            cT = cst[:, C_CT:C_CT + 8]
            A(ce[:, :], cT, AF.Exp, ["cst"], ["ce"], scale=-1.0)
            Vsadd(ce[:, :], ce[:, :], 1.0, ["ce"], ["ce"])
            Vrecip(ce[:, :], ce[:, :], ["ce"], ["ce"])
            Vtt(cact[:, :], ce[:, :], cT, ALU.mult, ["ce", "cst"], ["cact"])
            Vcopy(cact_bf[:, :], cact[:, :], ["cact"], ["cactbf"])
            for kc in range(8):
                Vsmul(cbm[:, kc, :], ones_bf[:, :], cact[:, kc:kc + 1], ["ones", "cact"], [("cbm", kc)])

            def ada_chunk(ci):
                wt, wk = next_w(("w_ada", 0, ci * 512, 512))
                fm = {0: 0, 1: 0, 2: 1, 3: 1, 6: 2, 7: 2, 8: 3, 9: 3}
                if ci in fm:
                    j0 = fm[ci] * 8 + (ci % 2) * 4
                    p_, pk = bank()
                    mms = []
                    for jj in range(4):
                        for kc in range(8):
                            mms.append((p_[:, jj:jj + 1], wt[:, kc, jj * 128:(jj + 1) * 128], cact_bf[:, kc:kc + 1],
                                        kc == 0, kc == 7))
                    MM(mms, [wk, "cactbf"], [pk])
                    Vtt(modT[:, j0:j0 + 4], p_[:, 0:4], cst[:, C_BADA + j0:C_BADA + j0 + 4], ALU.add,
                        [pk, "cst"], [("modT", j0)])
                else:
                    gb_ = g1b if ci in (4, 5) else g2b
                    gk = "g1b" if ci in (4, 5) else "g2b"
                    hs_ = slice((ci % 2) * 512, (ci % 2) * 512 + 512)
                    p_, pk = bank()
                    MM([(p_[:, :], cbm[:, kc, :], wt[:, kc, :], kc == 0, kc == 7) for kc in range(8)],
                       [wk] + [("cbm", kc) for kc in range(8)], [pk])
                    Vtt(gb_[:, hs_], p_[:, :], gb_[:, hs_], ALU.add, [pk, gk], [gk])
                done_w()

            def mod_finish(which):
                sc0 = 8 if which == 1 else 24
                nw0 = C_N1 if which == 1 else C_N2
                dst = a1T if which == 1 else a2T
                key = "a1T" if which == 1 else "a2T"
                Vsadd(dst[:, :], modT[:, sc0:sc0 + 8], 1.0, [("modT", sc0), ("modT", sc0 + 4)], [key])
                Vtt(dst[:, :], dst[:, :], cst[:, nw0:nw0 + 8], ALU.mult, [key, "cst"], [key])

            def stage_norm(xb, par, aT_, akey, sh0):
                norm_p1(xb, par)
                norm_p2(aT_, akey, sh0)

            def norm_p1(xb, par):
                for s in range(4):
                    xk = ("x", par, s)
                    A(xn[:, s, :], xb[:, s, :], AF.Square, [xk], [("xn", s), ("ss", s)], accum_out=ss[:, s:s + 1])
                    A(lnv[:, s:s + 1], ss[:, s:s + 1], AF.Ln, [("ss", s)], [("lnv", s)], scale=1.0 / D, bias=EPS)
                    A(rstd[:, s:s + 1], lnv[:, s:s + 1], AF.Exp, [("lnv", s)], [("rstd", s)], scale=-0.5)
                    Vsmul(xn[:, s, :], xb[:, s, :], rstd[:, s:s + 1], [xk, ("rstd", s)], [("xn", s)])

            def norm_p2(aT_, akey, sh0):
                shkeys = [("modT", sh0), ("modT", sh0 + 4)]
                for kc in range(8):
                    pb, pbk = bbank()
                    TR([(pb[:, s * 128:(s + 1) * 128], xn[:, s, kc * 128:(kc + 1) * 128]) for s in range(4)],
                       [("xn", s) for s in range(4)], [pbk])
                    a_ap = aT_[:, kc:kc + 1]
                    s_ap = modT[:, sh0 + kc:sh0 + kc + 1]
                    if kc % 2 == 0:
                        Vts(hT[:, kc, :], pb[:, 0:512], a_ap, s_ap, ALU.mult, ALU.add, [pbk, akey] + shkeys, [("hT", kc)])
                    else:
                        A(hT[:, kc, :], pb[:, 0:512], AF.Identity, [pbk, akey] + shkeys, [("hT", kc)], scale=a_ap, bias=s_ap)

            def sigmoid(p_, pk):
                t, tk = tmp()
                A(t[:, :], p_[:, :], AF.Exp, [pk], [tk], scale=-1.0)
                A(t[:, :], t[:, :], AF.Ln, [tk], [tk], bias=1.0)
                A(t[:, :], t[:, :], AF.Exp, [tk], [tk], scale=-1.0)
                return t, tk

            flag_ap = cst[:, C_FLAG:C_FLAG + 1]

            def la_stage():
                la_p1()
                la_p2()

            def la_p1():
                p_, pk = bank()
                MM([(p_[0:16, :], wlr[:, kc, 0:16], hT[:, kc, :], kc == 0, kc == 7) for kc in range(8)],
                   ["wlr"] + HT_ALL, [pk])
                A(lrT[0:16, :], p_[0:16, :], AF.Copy, [pk], ["lrT"])

            def la_p2():
                for s in range(4):
                    p_, pk = bank()
                    MM([(p_[:, :], lrT[0:17, s * 128:(s + 1) * 128], wg[0:17, :], True, True)], ["lrT", "wg"], [pk])
                    t, tk = tmp()
                    A(t[:, :], p_[:, :], AF.Exp, [pk], [tk], scale=-1.0)
                    A(spb[:, s, :], t[:, :], AF.Ln, [tk], [("sp", s)], bias=1.0)

            def gla_stage(main, last_prev, special=False, hoist=None, nxt=None):
                if last_prev:
                    P.op("sp", lambda e: e.dma_start(out=w32q, in_=W["w_in"][:, Q0:Q0 + 512].rearrange("(kc p) c -> p kc c", p=128)),
                         [], ["w32q"] + [("x", 1, s_) for s_ in range(4)], dma="c5")
                    Vcopy(hTh[:, :, :], hT[:, :, 510:512], HT_ALL, ["hTh"])
                if main:
                    wq, wqk = next_w(("w_in", 0, Q0, 512))
                if main:
                    wk_, wkk = next_w(("w_in", 0, K0, 512))
                else:
                    wk_, wkk = wkp, "wkp"
                e2s = []
                for h in range(4):
                    hs = slice(h * 128, (h + 1) * 128)
                    pbb, pbbk = bank()
                    MM([(pbb[:, s * 128:(s + 1) * 128], spb[:, s, hs], uneg[:, :], True, True) for s in range(4)],
                       [("sp", s) for s in range(4)] + ["uneg"], [pbbk])
                    A(E1[:, h, :], pbb[:, :], AF.Exp, [pbbk], [("E1", h)])
                    e2, e2k = tmp()
                    A(e2[:, :], pbb[:, :], AF.Exp, [pbbk], [e2k], scale=-1.0)
                    e2s.append((e2, e2k))
                if (not main) and nxt is not None:
                    norm_p1(xbuf[nxt % 2], nxt % 2)

                def head_front(h):
                    hs = slice(h * 128, (h + 1) * 128)
                    e2, e2k = e2s[h]
                    if main:
                        pq, pqk = bank()
                        MM([(pq[:, :], wq[:, kc, hs], hT[:, kc, :], kc == 0, kc == 7) for kc in range(8)],
                           [wqk] + HT_ALL, [pqk])
                    pkk_, pkkk = bank()
                    MM([(pkk_[:, :], wk_[:, kc, hs], hT[:, kc, :], kc == 0, kc == 7) for kc in range(8)],
                       [wkk] + HT_ALL, [pkkk])
                    if special:
                        MM([(pq[:, 0:128], w32q[:, kc, hs], h32T[:, kc, :], kc == 0, kc == 7) for kc in range(8)],
                           ["w32q", "h32T", ("x", 1, 0)], [pqk])
                        MM([(pkk_[:, 0:128], w32k[:, kc, hs], h32T[:, kc, :], kc == 0, kc == 7) for kc in range(8)],
                           ["w32k", "h32T"], [pkkk])
                    if main:
                        Vstt(qdT[:, h, :], pq[:, :], 128.0 ** -0.5, E1[:, h, :], ALU.mult, ALU.mult,
                             [pqk, ("E1", h)], [("qdT", h)])
                    Vtt(kdT[:, h, :], pkk_[:, :], e2[:, :], ALU.mult, [pkkk, e2k], [("kdT", h)])
                    if special:
                        Vstt(qd32[:, h, :], pq[:, 0:128], 128.0 ** -0.5, E1[:, h, 0:128], ALU.mult, ALU.mult,
                             [pqk, ("E1", h)], [("qd32", h), "xn32"])
                        Vtt(kd32[:, h, :], pkk_[:, 0:128], e2[:, 0:128], ALU.mult, [pkkk, e2k], [("kd32", h), "xn32"])

                def head_back(h):
                    pb, pbk = bbank()
                    TR([(pb[:, s * 128:(s + 1) * 128], kdT[:, h, s * 128:(s + 1) * 128]) for s in range(4)], [("kdT", h)], [pbk])
                    Vcopy(ke[:, h, :], pb[:, 0:512], [pbk], [("ke", h)])

                for h in range(4):
                    head_front(h)
                    if h >= 1:
                        head_back(h - 1)
                head_back(3)
                if main:
                    done_w()
                    done_w()
                for n in range(2):
                    if main:
                        wv, wvk = next_w(("w_in", 0, V0 + n * 512, 512))
                    else:
                        wv, wvk = wvp[n], f"wvp{n}"
                    for s in range(4):
                        p_, pk = bank()
                        MM([(p_[:, :], hT[:, kc, s * 128:(s + 1) * 128], wv[:, kc, :], kc == 0, kc == 7) for kc in range(8)],
                           [wvk] + HT_ALL, [pk])
                        if s % 2 == 0:
                            Vcopy(v_sb[:, s, n * 512:(n + 1) * 512], p_[:, :], [pk], [("v", s)])
                        else:
                            A(v_sb[:, s, n * 512:(n + 1) * 512], p_[:, :], AF.Copy, [pk], [("v", s)])
                    if main:
                        done_w()
                if last_prev:
                    P.op("sp", lambda e: e.dma_start(out=w32k, in_=W["w_in"][:, K0:K0 + 512].rearrange("(kc p) c -> p kc c", p=128)),
                         [], ["w32k"], dma="c6")
                if (not main) and nxt is not None:
                    norm_p2(a1T, "a1T", 0)
                    la_p1()
                if main:
                    for n in range(2):
                        wgg, wggk = next_w(("w_in", 0, G0 + n * 512, 512))
                        for s in range(4):
                            p_, pk = bank()
                            MM([(p_[:, :], hT[:, kc, s * 128:(s + 1) * 128], wgg[:, kc, :], kc == 0, kc == 7) for kc in range(8)],
                               [wggk] + HT_ALL, [pk])
                            r_, rk = sigmoid(p_, pk)
                            Vtt(sg[:, s, n * 512:(n + 1) * 512], p_[:, :], r_[:, :], ALU.mult, [pk, rk],
                                [("sg", s, 2 * n), ("sg", s, 2 * n + 1)])
                        done_w()
                def phaseA(s):
                    cs = slice(s * 128, (s + 1) * 128)
                    st = {}
                    if main:
                        psc, psck = bank()
                        if special and s == 0:
                            MM([(psc[:, h * 128:(h + 1) * 128], kd32[:, h, :], qd32[:, h, :], True, True) for h in range(4)],
                               [("kd32", h) for h in range(4)] + [("qd32", h) for h in range(4)], [psck])
                        else:
                            MM([(psc[:, h * 128:(h + 1) * 128], kdT[:, h, cs], qdT[:, h, cs], True, True) for h in range(4)],
                               [("kdT", h) for h in range(4)] + [("qdT", h) for h in range(4)], [psck])
                        sc, sck = scbuf()
                        Vtt(sc[:, :], psc[:, :], cmask4[:, :], ALU.mult, [psck, "cmask4"], [sck])
                        st["sc"] = (sc, sck)
                    pts = []
                    for hp in range(2):
                        pt, ptk = bank()
                        MM([(pt[:, j * 256:(j + 1) * 256], ke[:, hp * 2 + j, cs],
                             v_sb[:, s, (hp * 2 + j) * 256:(hp * 2 + j + 1) * 256], True, True) for j in range(2)],
                           [("ke", hp * 2), ("ke", hp * 2 + 1), ("v", s)], [ptk])
                        pts.append((pt, ptk))
                    st["pts"] = pts
                    return st

                def supd(s, st):
                    for h in range(4):
                        pt, ptk = st["pts"][h // 2]
                        Vtt(S32[:, h, :], S32[:, h, :], pt[:, (h % 2) * 256:(h % 2 + 1) * 256], ALU.add,
                            [("S32", h), ptk], [("S32", h)])
                        Vsmul(S32[:, h, :], S32[:, h, :], E1[:, h, s * 128 + 127:s * 128 + 128],
                              [("S32", h), ("E1", h)], [("S32", h)])

                def o_and_act(s, st):
                    cs = slice(s * 128, (s + 1) * 128)
                    pos = []
                    if main:
                        sc, sck = st["sc"]
                        Sb = S_bfs[s % 2]
                        for hp in range(2):
                            po, pok = bank()
                            mms = []
                            for j in range(2):
                                h = hp * 2 + j
                                mms.append((po[:, j * 256:(j + 1) * 256], sc[:, h * 128:(h + 1) * 128],
                                            v_sb[:, s, h * 256:(h + 1) * 256], True, False))
                                mms.append((po[:, j * 256:(j + 1) * 256], qdT[:, h, cs], Sb[:, h, :], False, True))
                            MM(mms, [sck, ("v", s), ("qdT", hp * 2), ("qdT", hp * 2 + 1), ("Sbf", s % 2)], [pok])
                            pos.append((po, pok))
                    if main:
                        A(S_bfs[(s + 1) % 2][:, :, :], S32[:, :, :], AF.Copy, [("S32", h) for h in range(4)], [("Sbf", (s + 1) % 2)])
                    if main:
                        for h in range(4):
                            po, pok = pos[h // 2]
                            i = s * 4 + h
                            jt, jtk = tmp()
                            A(jt[:, :].bitcast(BF16)[:, 0:256], po[:, (h % 2) * 256:(h % 2 + 1) * 256], AF.Square, [pok],
                              [("sso", i), jtk], accum_out=sso[:, i:i + 1])
                        A(lno[:, s * 4:(s + 1) * 4], sso[:, s * 4:(s + 1) * 4], AF.Ln, [("sso", s * 4 + h) for h in range(4)],
                          [("lno", s)], scale=1.0 / 256, bias=EPS)
                        A(rso[:, s * 4:(s + 1) * 4], lno[:, s * 4:(s + 1) * 4], AF.Exp, [("lno", s)], [("rso", s)], scale=-0.5)
                    return pos

                def og_stage(s, pos):
                    if main:
                        for h in range(4):
                            po, pok = pos[h // 2]
                            i = s * 4 + h
                            sgs = sg[:, s, h * 256:(h + 1) * 256]
                            Vstt(sgs, po[:, (h % 2) * 256:(h % 2 + 1) * 256], rso[:, i:i + 1], sgs, ALU.mult, ALU.mult,
                                 [pok, ("rso", s), ("sg", s, h)], [("sg", s, h)])

                sts = {0: phaseA(0)}
                pos_prev = None
                for s in range(4):
                    supd(s, sts[s])
                    if s + 1 < 4:
                        sts[s + 1] = phaseA(s + 1)
                    pos = o_and_act(s, sts[s])
                    if pos_prev is not None:
                        og_stage(s - 1, pos_prev)
                    pos_prev = pos
                og_stage(3, pos_prev)
                if last_prev:
                    for h in range(4):
                        Vsmul(S32[:, h, :], S32[:, h, :], flag_ap, [("S32", h), "cst"], [("S32", h)])
                    A(S_bfs[0][:, :, :], S32[:, :, :], AF.Copy, [("S32", h) for h in range(4)], [("Sbf", 0)])
                if not main:
                    if nxt is not None:
                        la_p2()
                    return
                for kc in range(8):
                    pb, pbk = bbank()
                    TR([(pb[:, s * 128:(s + 1) * 128], sg[:, s, kc * 128:(kc + 1) * 128]) for s in range(4)],
                       [("sg", s, kc // 2) for s in range(4)], [pbk])
                    g_ap = cst[:, C_GNW + kc % 2:C_GNW + kc % 2 + 1]
                    if kc % 2:
                        A(ogT[:, kc, :], pb[:, 0:512], AF.Copy, [pbk, "cst"], [("ogT", kc)], scale=g_ap)
                    else:
                        Vsmul(ogT[:, kc, :], pb[:, 0:512], g_ap, [pbk, "cst"], [("ogT", kc)])

            def cw_ap(m, j):
                c = C_CW + m * 3 + j
                return cst[:, c:c + 1]

            def conv_stage(first=False):
                for mg in range(2):
                    wcb, wcbk = next_w(("w_in", 0, CB0 + mg * 512, 512))
                    wcc, wcck = next_w(("w_in", 0, CC0 + mg * 512, 512))
                    wcx, wcxk = next_w(("w_in", 0, CX0 + mg * 512, 512))
                    for mm in range(4):
                        m = mg * 4 + mm
                        ms = slice(mm * 128, (mm + 1) * 128)
                        pc, pck = bank()
                        MM([(pc[:, :], wcc[:, kc, ms], hT[:, kc, :], kc == 0, kc == 7) for kc in range(8)], [wcck] + HT_ALL, [pck])
                        px, pxk = bank()
                        MM([(px[:, :], wcx[:, kc, ms], hT[:, kc, :], kc == 0, kc == 7) for kc in range(8)], [wcxk] + HT_ALL, [pxk])
                        pcb, pcbk = bank()
                        MM([(pcb[:, :], wcb[:, kc, ms], hT[:, kc, :], kc == 0, kc == 7) for kc in range(8)], [wcbk] + HT_ALL, [pcbk])
                        if first:
                            ph, phk = bank()
                            MM([(ph[:, 0:2], wcc[:, kc, ms], hTh[:, kc, :], kc == 0, kc == 7) for kc in range(8)]
                               + [(ph[:, 2:4], wcx[:, kc, ms], hTh[:, kc, :], kc == 0, kc == 7) for kc in range(8)],
                               [wcck, wcxk, "hTh"], [phk])
                            th, thk = tmp()
                            A(th[:, 0:2], ph[:, 0:2], AF.Copy, [phk], [thk])
                            Vtt(th[:, 2:4], th[:, 0:2], ph[:, 2:4], ALU.mult, [thk, phk], [thk])
                            Vsmul(uh[:, m, :], th[:, 2:4], flag_ap, [thk, "cst"], [("uh", m)])
                        t, tk = tmp()
                        A(t[:, :], pc[:, :], AF.Copy, [pck], [tk])
                        u, uk = ubuf()
                        A(u[:, 0:2], uh[:, m, :], AF.Copy, [("uh", m)], [uk])
                        Vtt(u[:, 2:514], t[:, :], px[:, :], ALU.mult, [tk, pxk, uk], [uk])
                        A(uh[:, m, :], u[:, 512:514], AF.Copy, [uk], [("uh", m)])
                        a_, ak = tmp()
                        Vsmul(a_[:, :], u[:, 2:514], cw_ap(m, 2), [uk, "cst"], [ak])
                        Vstt(a_[:, :], u[:, 1:513], cw_ap(m, 1), a_[:, :], ALU.mult, ALU.add, [uk, "cst", ak], [ak])
                        Vstt(a_[:, :], u[:, 0:512], cw_ap(m, 0), a_[:, :], ALU.mult, ALU.add, [uk, "cst", ak], [ak])
                        Vtt(cbuT[:, m, :], a_[:, :], pcb[:, :], ALU.mult, [ak, pcbk], [("cbuT", m)])
                    done_w()
                    done_w()
                    done_w()

            def merge_stage(xb, par):
                OGT_ALL = [("ogT", kc) for kc in range(8)]
                CBU_ALL = [("cbuT", kc) for kc in range(8)]
                for mg in range(2):
                    wa, wak = next_w(("w_pa", 0, mg * 512, 512))
                    wga_, wgak = next_w(("w_in", 0, GA0 + mg * 512, 512))
                    wb, wbk = next_w(("w_pb", 0, mg * 512, 512))
                    wgb, wgbk = next_w(("w_in", 0, GB0 + mg * 512, 512))
                    for mm in range(4):
                        m = mg * 4 + mm
                        ms = slice(mm * 128, (mm + 1) * 128)
                        pya, pyak = bank()
                        MM([(pya[:, :], wa[:, kc, ms], ogT[:, kc, :], kc == 0, kc == 7) for kc in range(8)], [wak] + OGT_ALL, [pyak])
                        pga, pgak = bank()
                        MM([(pga[:, :], wga_[:, kc, ms], hT[:, kc, :], kc == 0, kc == 7) for kc in range(8)], [wgak] + HT_ALL, [pgak])
                        ra, rak = sigmoid(pga, pgak)
                        Vtt(ra[:, :], ra[:, :], pya[:, :], ALU.mult, [rak, pyak], [rak])
                        pyb, pybk = bank()
                        MM([(pyb[:, :], wb[:, kc, ms], cbuT[:, kc, :], kc == 0, kc == 7) for kc in range(8)], [wbk] + CBU_ALL, [pybk])
                        pgb, pgbk = bank()
                        MM([(pgb[:, :], wgb[:, kc, ms], hT[:, kc, :], kc == 0, kc == 7) for kc in range(8)], [wgbk] + HT_ALL, [pgbk])
                        rb, rbk = sigmoid(pgb, pgbk)
                        Vtt(rb[:, :], rb[:, :], pyb[:, :], ALU.mult, [rbk, pybk], [rbk])
                        Vtt(zT[:, m, :], ra[:, :], rb[:, :], ALU.add, [rak, rbk], [("zT", m)])
                    for _ in range(4):
                        done_w()
                ZT_ALL = [("zT", kc) for kc in range(8)]
                for n in range(2):
                    wo, wok = next_w(("w_o", 0, n * 512, 512))
                    ns = slice(n * 512, (n + 1) * 512)
                    for s in range(4):
                        p_, pk = bank()
                        MM([(p_[:, :], zT[:, kc, s * 128:(s + 1) * 128], wo[:, kc, :], kc == 0, kc == 7) for kc in range(8)],
                           [wok] + ZT_ALL, [pk])
                        t, tk = tmp()
                        Vtt(t[:, :], p_[:, :], g1b[:, ns], ALU.mult, [pk, "g1b"], [tk])
                        xk = ("x", par, s)
                        Vtt(xb[:, s, ns], xb[:, s, ns], t[:, :], ALU.add, [xk, tk], [xk])
                    done_w()

            def mlp_stage(xb, par, g, hoist=None, nxt=None):
                stage_norm(xb, par, a2T, "a2T", 16)
                if nxt is not None:
                    norm_p1(xbuf[nxt % 2], nxt % 2)
                for cg in range(8):
                    w1_, w1k = next_w(("w1", 0, cg * 512, 512))
                    for jj in range(4):
                        j = cg * 4 + jj
                        p_, pk = bank()
                        MM([(p_[:, :], w1_[:, kc, jj * 128:(jj + 1) * 128], hT[:, kc, :], kc == 0, kc == 7) for kc in range(8)],
                           [w1k] + HT_ALL, [pk])
                        t, tk = tmp()
                        A(t[:, :], p_[:, :], AF.Relu, [pk], [tk])
                        Vtt(aT[:, j, :], t[:, :], t[:, :], ALU.mult, [tk], [("aT", j)])
                    done_w()
                for n in range(2):
                    ns = slice(n * 512, (n + 1) * 512)
                    bks = [bank() for _ in range(4)]
                    for jg in range(4):
                        w2_, w2k = next_w(("w2", jg * 1024, n * 512, 512))
                        for s in range(4):
                            MM([(bks[s][0][:, :], aT[:, jg * 8 + jj, s * 128:(s + 1) * 128], w2_[:, jj, :],
                                 jg == 0 and jj == 0, jg == 3 and jj == 7) for jj in range(8)],
                               [w2k] + [("aT", jg * 8 + jj) for jj in range(8)], [bks[s][1]])
                        done_w()
                    for s in range(4):
                        t, tk = tmp()
                        Vtt(t[:, :], bks[s][0][:, :], g2b[:, ns], ALU.mult, [bks[s][1], "g2b"], [tk])
                        xk = ("x", par, s)
                        Vtt(xb[:, s, ns], xb[:, s, ns], t[:, :], ALU.add, [xk, tk], [xk])
                    if n == 0 and nxt is not None:
                        norm_p2(a1T, "a1T", 0)
                        la_p1()
                        la_p2()
                for s in range(4):
                    xk = ("x", par, s)
                    jt, jtk = tmp()
                    A(jt[:, :].bitcast(BF16), xb[:, s, :], AF.Square, [xk], [jtk, ("ss2", s)], accum_out=ss2[:, s:s + 1])
                    A(lnv2[:, s:s + 1], ss2[:, s:s + 1], AF.Ln, [("ss2", s)], [("lnv2", s)], scale=1.0 / D, bias=EPS)
                    A(rstd2[:, s:s + 1], lnv2[:, s:s + 1], AF.Exp, [("lnv2", s)], [("rstd2", s)], scale=-0.5)
                    A(xb[:, s, :], xb[:, s, :], AF.Copy, [xk, ("rstd2", s)], [xk], scale=rstd2[:, s:s + 1])
                    Vtt(xb[:, s, :], xb[:, s, :], fnw[:, :], ALU.mult, [xk, "fnw"], [xk])
                t0 = (g % 4) * T
                dst = out_d[t0:t0 + T, :].rearrange("(s p) d -> p s d", p=128)
                P.op("sp", lambda e: e.dma_start(out=dst, in_=xb[:, :, :]), [("x", par, s) for s in range(4)],
                     [("out", g)], dma=f"o{par}")

            def special_prep(xb, par):
                Vsmul(xn32, xb[:, 0, :], rstd[:, 0:1], [("x", par, 0), ("rstd", 0)], ["xn32"])
                for half in range(2):
                    p_, pk = bank()
                    def fn(e, p_=p_, half=half):
                        ins = None
                        for j in range(4):
                            kc = half * 4 + j
                            ins = e.transpose(out=p_[:, j * 128:(j + 1) * 128], in_=xn32[:, kc * 128:(kc + 1) * 128],
                                              identity=identf[:, :])
                        return ins
                    P.op("pe", fn, ["xn32", "identf"], [pk])
                    for j in range(4):
                        kc = half * 4 + j
                        Vts(h32T[:, kc, :], p_[:, j * 128:(j + 1) * 128], a1T[:, kc:kc + 1], modT[:, kc:kc + 1],
                            ALU.mult, ALU.add, [pk, "a1T", ("modT", 0), ("modT", 4)], ["h32T"])

            for ci in range(4):
                ada_chunk(ci)
            mod_finish(1)
            rest = [4, 5, 6, 7, 8, 9, 10, 11]
            for g in range(8):
                par = g % 2
                xb = xbuf[par]
                if g >= 4:
                    P.wmode = "t4" if g == 4 else "t5" if g == 5 else "scr"
                    P.widx = 0
                if g == 0:
                    stage_norm(xb, par, a1T, "a1T", 0)
                    la_stage()
                nxt = g + 1 if g + 1 < 8 else None
                if g < 4:
                    gla_stage(False, g == 3, nxt=nxt)
                    for ci in rest[g * 2:g * 2 + 2]:
                        ada_chunk(ci)
                    if g == 0:
                        rec_convs(100)
                    if g == 3:
                        mod_finish(2)
                else:
                    if g == 4:
                        special_prep(xb, par)
                    gla_stage(True, False, special=(g == 4))
                    if g == 4:
                        rec_xload(5)
                    conv_stage(first=(g == 4))
                    merge_stage(xb, par)
                    mlp_stage(xb, par, g, nxt=nxt)
                if g + 2 < 8 and g != 3:
                    rec_xload(g + 2)
            P.op("sp", None, [("out", g) for g in range(4, 8)], [])

        wplan = []
        record(Prog(True, wplan))
        P = Prog(False, wplan)
        record(P)
        assert P.wi == len(wplan), (P.wi, len(wplan))

        sem_names = P.sems()
        S = {n: es.enter_context(nc.semaphore(n)) for n in sem_names}
        block = es.enter_context(nc.Block())

        def run_stream(eng_name):
            def body(e):
                for (waits, fn, sem, amt) in P.streams[eng_name]:
                    for (s_, v_) in waits:
                        e.wait_ge(S[s_], v_)
                    if fn is not None:
                        fn(e).then_inc(S[sem], amt)
            return body

        block.sync(run_stream("sp"))
        block.gpsimd(run_stream("pool"))
        block.tensor(run_stream("pe"))
        block.scalar(run_stream("act"))
        block.vector(run_stream("dve"))
    return nc


_NC = None


def kernel(x, c, w_ada, b_ada, norm1_w, w_in, w_gate_up, b_gate, gla_norm_w, conv_w,
           w_proj_a, w_proj_b, w_out, norm2_w, w_mlp1, w_mlp2, final_norm_w):
    global _NC
    f = lambda a: np.ascontiguousarray(np.asarray(a, dtype=np.float32))
    x = f(x)
    c = f(c)
    b_ada = f(b_ada)[0]
    shared = {
        "w_ada": f(w_ada)[0], "w_in": f(w_in)[0], "w_pa": f(w_proj_a)[0], "w_pb": f(w_proj_b)[0],
        "w_o": f(w_out)[0], "w1": f(w_mlp1)[0], "w2": f(w_mlp2)[0],
        "fnw_b": np.ascontiguousarray(np.broadcast_to(f(final_norm_w)[None, :], (128, D))),
        "bgate_b": np.ascontiguousarray(np.broadcast_to(
            np.concatenate([b_ada[2 * D:3 * D], b_ada[5 * D:6 * D]])[None, :], (128, 2 * D))),
        "wg_aug": np.ascontiguousarray(np.concatenate([f(w_gate_up)[0], f(b_gate)[0][None, :]], axis=0)),
    }
    colT = lambda v: np.ascontiguousarray(v.reshape(-1, 128).T)
    cbase = np.zeros((128, NCONST), np.float32)
    cbase[:, C_BADA:C_BADA + 32] = np.concatenate(
        [colT(b_ada[0:D]), colT(b_ada[D:2 * D]), colT(b_ada[3 * D:4 * D]), colT(b_ada[4 * D:5 * D])], axis=1)
    cbase[:, C_N1:C_N1 + 8] = colT(f(norm1_w)[0])
    cbase[:, C_N2:C_N2 + 8] = colT(f(norm2_w)[0])
    cwl = f(conv_w)[0]
    cbase[:, C_CW:C_CW + 24] = np.transpose(cwl.reshape(3, 8, 128), (2, 1, 0)).reshape(128, 24)
    cbase[:, C_GNW:C_GNW + 2] = colT(f(gla_norm_w)[0])
    if _NC is None:
        _NC = build_nc()
    in_maps = []
    for i in range(8):
        b, hf = i // 2, i % 2
        cs = cbase.copy()
        cs[:, C_CT:C_CT + 8] = colT(c[b])
        cs[:, C_FLAG] = float(hf)
        m = dict(shared)
        m["x_cur"] = np.ascontiguousarray(x[b, hf * TOK:(hf + 1) * TOK])
        m["x_prev"] = np.ascontiguousarray(x[b, 0:TOK])
        m["consts"] = cs
        in_maps.append(m)
    res = run_bass_kernel_spmd(_NC, in_maps, core_ids=list(range(8)))
    out = np.empty((4, 2 * TOK, D), np.float32)
    for i in range(8):
        b, hf = i // 2, i % 2
        out[b, hf * TOK:(hf + 1) * TOK] = np.asarray(res.results[i]["out"]).reshape(TOK, D)
    return out
```

```python
import numpy as np
from contextlib import ExitStack
import concourse.bass as bass
import concourse.mybir as mybir
from concourse.bass_utils import run_bass_kernel_spmd

F32 = mybir.dt.float32
BF16 = mybir.dt.bfloat16
AF = mybir.ActivationFunctionType
ALU = mybir.AluOpType

D = 1024
TOK = 2048
T = 512
NW = 5
NTMP = 8
NSCR = 40
PRECONV = ("w2",)
EPS = 1e-6
Q0, K0, V0, G0, LR0, CB0, CC0, CX0, GA0, GB0 = 0, 512, 1024, 2048, 3072, 3088, 4112, 5136, 6160, 7184
C_CT, C_BADA, C_N1, C_N2, C_CW, C_GNW, C_FLAG, NCONST = 0, 8, 40, 48, 56, 80, 82, 84


class Ev:
    __slots__ = ("eng", "sem", "val", "know", "dma")

    def __init__(self, eng, sem, val, know, dma):
        self.eng, self.sem, self.val, self.know, self.dma = eng, sem, val, know, dma


class Prog:
    ENGS = ("pe", "act", "dve", "pool", "sp")

    def __init__(self, dry, wplan):
        self.dry = dry
        self.wplan = wplan
        self.streams = {e: [] for e in self.ENGS}
        self.cnt = {e: 0 for e in self.ENGS}
        self.dcnt = {}
        self.know = {e: {} for e in self.ENGS}
        self.last_w = {}
        self.readers = {}
        self.groups = {}
        self.cur_view = {}
        self.barrier = {}
        self.name_keys = {}
        self.bi = 0
        self.bbi = 0
        self.ti = 0
        self.wi = 0
        self.wdone = 0
        self.wmode = "cast"
        self.widx = -1

    def op(self, eng, fn, reads=(), writes=(), dma=None):
        if self.dry:
            return
        deps = []
        for k in list(reads) + list(writes):
            name = k[0] if isinstance(k, tuple) else k
            self.name_keys.setdefault(name, set()).add(k)
            for g in self.groups.get(name, ()):
                if self.cur_view.get(g) != name:
                    old = self.cur_view.get(g)
                    evs = []
                    if old is not None:
                        for kk in self.name_keys.get(old, ()):
                            if kk in self.last_w:
                                evs.append(self.last_w[kk])
                            evs += self.readers.get(kk, [])
                    self.barrier[g] = evs
                    self.cur_view[g] = name
                deps += self.barrier.get(g, [])
        is_dma = dma is not None
        for k in reads:
            ev = self.last_w.get(k)
            if ev is not None:
                deps.append(ev)
        for k in writes:
            ev = self.last_w.get(k)
            if ev is not None and (is_dma or ev.dma or ev.eng != eng or eng != "pe"):
                deps.append(ev)
            for ev in self.readers.get(k, []):
                if is_dma or ev.dma or ev.eng != eng or eng != "pe":
                    deps.append(ev)
        kn = self.know[eng]
        waits = {}
        for ev in deps:
            if kn.get(ev.sem, 0) >= ev.val:
                continue
            waits[ev.sem] = max(waits.get(ev.sem, 0), ev.val)
            for s_, v_ in ev.know.items():
                if kn.get(s_, 0) < v_:
                    kn[s_] = v_
        if fn is None:
            self.streams[eng].append((list(waits.items()), None, None, 0))
            return
        if is_dma:
            sem = dma
            self.dcnt[sem] = self.dcnt.get(sem, 0) + 16
            val = self.dcnt[sem]
            amt = 16
        else:
            sem = "E_" + eng
            self.cnt[eng] += 1
            val = self.cnt[eng]
            amt = 1
        evk = dict(kn)
        evk[sem] = val
        ev = Ev(eng, sem, val, evk, is_dma)
        for k in reads:
            self.readers.setdefault(k, []).append(ev)
        for k in writes:
            self.last_w[k] = ev
            self.readers[k] = []
        self.streams[eng].append((list(waits.items()), fn, sem, amt))

    def sems(self):
        s = {"E_" + e for e in self.ENGS}
        s |= set(self.dcnt.keys())
        return sorted(s)


def build_nc():
    nc = bass.Bass("TRN2", target_bir_lowering=False)

    def din(name, shape):
        return nc.dram_tensor(name, shape, F32, kind="ExternalInput").ap()

    x_cur = din("x_cur", [TOK, D])
    x_prev = din("x_prev", [TOK, D])
    W = {
        "w_ada": din("w_ada", [D, 6 * D]),
        "w_in": din("w_in", [D, 8208]),
        "w_pa": din("w_pa", [D, D]),
        "w_pb": din("w_pb", [D, D]),
        "w_o": din("w_o", [D, D]),
        "w1": din("w1", [D, 4 * D]),
        "w2": din("w2", [4 * D, D]),
    }
    consts_d = din("consts", [128, NCONST])
    fnwb_d = din("fnw_b", [128, D])
    bgb_d = din("bgate_b", [128, 2 * D])
    wga_d = din("wg_aug", [17, 512])
    out_d = nc.dram_tensor("out", [TOK, D], F32, kind="ExternalOutput").ap()
    wscr = nc.dram_tensor("wscr", [NSCR, 128, 4096], BF16, kind="Internal").ap()

    with ExitStack() as es:
        def sb(name, shape, dt):
            return es.enter_context(nc.sbuf_tensor(name, shape, dt))

        xbuf = [sb(f"xbuf{i}", [128, 4, D], F32) for i in range(2)]
        xn = sb("xn", [128, 4, D], BF16)
        hT = sb("hT", [128, 8, T], BF16)
        wsl = [sb(f"wsl{i}", [128, 8, 512], BF16) for i in range(NW)]
        wlr = sb("wlr", [128, 8, 16], BF16)
        lrT = sb("lrT", [32, T], F32)
        wg = sb("wg", [32, 512], F32)
        spb = sb("spb", [128, 4, 512], F32)
        E1 = sb("E1", [128, 4, T], F32)
        spc = sb("spc", [128, 4, 512], F32)
        qdT = sb("qdT", [128, 4, T], BF16)
        kdT = sb("kdT", [128, 4, T], BF16)
        ke = sb("ke", [128, 4, T], BF16)
        big = sb("big", [128, 16384], BF16)
        S32 = sb("S32", [128, 4, 256], F32)
        S_bfs = [sb(f"S_bf{i}", [128, 4, 256], BF16) for i in range(2)]
        scb = [sb(f"scb{i}", [128, 512], BF16) for i in range(2)]
        cmask4 = sb("cmask4", [128, 512], BF16)
        ss2 = sb("ss2", [128, 4], F32)
        lnv2 = sb("lnv2", [128, 4], F32)
        rstd2 = sb("rstd2", [128, 4], F32)
        ub = [sb(f"ub{i}", [128, 514], F32) for i in range(2)]
        uh = sb("uh", [128, 8, 2], F32)
        hTh = sb("hTh", [128, 8, 2], BF16)
        tmps = [sb(f"tmp{i}", [128, 512], F32) for i in range(NTMP)]
        cst = sb("cst", [128, NCONST], F32)
        fnw = sb("fnw", [128, D], F32)
        g1b = sb("g1b", [128, D], F32)
        g2b = sb("g2b", [128, D], F32)
        identf = sb("identf", [128, 128], F32)
        ident = sb("ident", [128, 128], BF16)
        ones_bf = sb("ones_bf", [128, 128], BF16)
        uneg = sb("uneg", [128, 128], F32)
        ce = sb("ce", [128, 8], F32)
        cact = sb("cact", [128, 8], F32)
        cact_bf = sb("cact_bf", [128, 8], BF16)
        modT = sb("modT", [128, 32], F32)
        a1T = sb("a1T", [128, 8], F32)
        a2T = sb("a2T", [128, 8], F32)
        ss = sb("ss", [128, 4], F32)
        lnv = sb("lnv", [128, 4], F32)
        rstd = sb("rstd", [128, 4], F32)
        sso = sb("sso", [128, 16], F32)
        lno = sb("lno", [128, 16], F32)
        rso = sb("rso", [128, 16], F32)

        psf = [es.enter_context(nc.psum_tensor(f"psf{i}", [128, 512], F32)) for i in range(8)]
        psb = [p_[:, :].bitcast(BF16) for p_ in psf]

        v_sb = big[:, 0:4096].rearrange("p (s c) -> p s c", s=4)
        ogT = big[:, 0:4096].rearrange("p (k c) -> p k c", k=8)
        sg = big[:, 4096:8192].rearrange("p (s c) -> p s c", s=4)
        cbuT = big[:, 8192:12288].rearrange("p (k c) -> p k c", k=8)
        zT = big[:, 12288:16384].rearrange("p (k c) -> p k c", k=8)
        aT = big[:, :].rearrange("p (j c) -> p j c", j=32)
        wkp = big[:, 4096:8192].rearrange("p (k c) -> p k c", k=8)
        wvp = [big[:, 8192:12288].rearrange("p (k c) -> p k c", k=8),
               big[:, 12288:16384].rearrange("p (k c) -> p k c", k=8)]
        cbm = qdT[:, 0:2, :].rearrange("p a (k c) -> p (a k) c", k=4)
        w32q = xbuf[1][:, :, :].rearrange("p s (a c) -> p (s a) c", a=2)
        w32k = big[:, 8192:16384].bitcast(F32).rearrange("p (k c) -> p k c", k=8)
        h32T = spc[:, 0:2, :].rearrange("p a (k c) -> p (a k) c", k=4)
        xn32 = spc[:, 2:4, :].rearrange("p a c -> p (a c)")
        qd32 = spc[:, 2, :].rearrange("p (h c) -> p h c", h=4)
        kd32 = spc[:, 3, :].rearrange("p (h c) -> p h c", h=4)

        def record(P):
            P.groups = {"v": ["A"], "ogT": ["A"], "sg": ["B"], "cbuT": ["C"], "zT": ["Dg"],
                        "aT": ["A", "B", "C", "Dg"], "w32k": ["C", "Dg"],
                        "wkp": ["B"], "wvp0": ["C"], "wvp1": ["Dg"],
                        "cbm": ["Q"], "qdT": ["Q"]}

            def A(out, in_, func, r, w, **kw):
                P.op("act", lambda e: e.activation(out=out, in_=in_, func=func, **kw), r, w)

            def Vtt(out, a, b, op, r, w):
                P.op("dve", lambda e: e.tensor_tensor(out=out, in0=a, in1=b, op=op), r, w)

            def Vts(out, a, s1, s2, op0, op1, r, w):
                P.op("dve", lambda e: e.tensor_scalar(out=out, in0=a, scalar1=s1, scalar2=s2, op0=op0, op1=op1), r, w)

            def Vsmul(out, a, s, r, w):
                P.op("dve", lambda e: e.tensor_scalar_mul(out=out, in0=a, scalar1=s), r, w)

            def Vsadd(out, a, s, r, w):
                P.op("dve", lambda e: e.tensor_scalar_add(out=out, in0=a, scalar1=s), r, w)

            def Vstt(out, a, s, b, op0, op1, r, w):
                P.op("dve", lambda e: e.scalar_tensor_tensor(out=out, in0=a, scalar=s, in1=b, op0=op0, op1=op1), r, w)

            def Vcopy(out, a, r, w):
                P.op("dve", lambda e: e.tensor_copy(out=out, in_=a), r, w)

            def Vrecip(out, a, r, w):
                P.op("dve", lambda e: e.reciprocal(out=out, in_=a), r, w)

            def MM(mms, r, w):
                def fn(e):
                    ins = None
                    for (o, l, rh, st, sp_) in mms:
                        ins = e.matmul(out=o, lhsT=l, rhs=rh, start=st, stop=sp_)
                    return ins
                P.op("pe", fn, r, w)

            def TR(trs, r, w):
                def fn(e):
                    ins = None
                    for (o, i) in trs:
                        ins = e.transpose(out=o, in_=i, identity=ident[:])
                    return ins
                P.op("pe", fn, list(r) + ["ident"], w)

            def bank():
                i = P.bi % 8
                P.bi += 1
                return psf[i], ("ps", i)

            def bbank():
                i = P.bi % 8
                P.bi += 1
                return psb[i], ("ps", i)

            def tmp():
                i = P.ti % NTMP
                P.ti += 1
                return tmps[i], ("tmp", i)

            sci = [0]

            def scbuf():
                i = sci[0] % 2
                sci[0] += 1
                return scb[i], ("sc", i)

            ubi = [0]

            def ubuf():
                i = ubi[0] % 2
                ubi[0] += 1
                return ub[i], ("u", i)

            def rec_load(j):
                wname, r0, c0, ncols, mode, idx = P.wplan[j]
                slot = j % NW
                if mode == "scr":
                    src = wscr[idx]
                    dst = wsl[slot][:, :, :].rearrange("p k c -> p (k c)")
                    P.op("pool", lambda e: e.dma_start(out=dst, in_=src), [("scr", idx)], [("w", slot)], dma=f"w{slot}")
                else:
                    src = W[wname][r0:r0 + 1024, c0:c0 + ncols].rearrange("(kc p) c -> p kc c", p=128)
                    dst = wsl[slot][:, :, 0:ncols]
                    P.op("pool", lambda e: e.dma_start(out=dst, in_=src), [], [("w", slot)], dma=f"w{slot}")
                if mode == "castwb":
                    wsrc = wsl[slot][:, :, :].rearrange("p k c -> p (k c)")
                    wdst = wscr[idx]
                    P.op("sp", lambda e: e.dma_start(out=wdst, in_=wsrc), [("w", slot)], [("scr", idx)], dma=f"wb{slot}")

            def is_preconv(spec):
                return spec[0] in PRECONV

            def next_w(spec):
                mode = P.wmode
                if mode in ("t4", "t5"):
                    if is_preconv(spec):
                        mode = "scr"
                    elif P.widx % 2 == 0:
                        mode = "castwb" if mode == "t4" else "scr"
                    else:
                        mode = "cast4" if mode == "t4" else "castwb"
                full = tuple(spec) + (mode, P.widx)
                if P.wmode != "cast":
                    P.widx += 1
                if P.dry:
                    P.wplan.append(full)
                    return wsl[0], ("w", 0)
                i = P.wi
                P.wi += 1
                assert P.wplan[i] == full, (i, P.wplan[i], full)
                return wsl[i % NW], ("w", i % NW)

            def done_w():
                if P.dry:
                    return
                j = P.wdone + NW
                P.wdone += 1
                if j < len(P.wplan):
                    rec_load(j)

            HT_ALL = [("hT", kc) for kc in range(8)]

            conv_list = [] if P.dry else [e_ for e_ in P.wplan if e_[4] == "scr" and is_preconv(e_) and e_[5] >= 0]
            seen_cv = set()
            conv_todo = []
            for e_ in conv_list:
                if e_[5] not in seen_cv:
                    seen_cv.add(e_[5])
                    conv_todo.append(e_)

            def rec_convs(n):
                for _ in range(n):
                    if not conv_todo:
                        return
                    wname, r0, c0, ncols, _m, idx = conv_todo.pop(0)
                    src = W[wname][r0:r0 + 1024, c0:c0 + ncols].rearrange("(kc p) c -> p kc c", p=128)
                    dst = wscr[idx].rearrange("p (k c) -> p k c", k=8)
                    P.op("pool", lambda e, dst=dst, src=src: e.dma_start(out=dst, in_=src), [], [("scr", idx)], dma=f"cv{idx}")

            P.op("sp", lambda e: e.dma_start(out=cst[:, :], in_=consts_d), [], ["cst"], dma="c0")
            P.op("sp", lambda e: e.dma_start(out=wg[0:17, :], in_=wga_d), [], ["wg"], dma="c1")
            P.op("sp", lambda e: e.dma_start(out=g1b[:, :], in_=bgb_d[:, 0:D]), [], ["g1b"], dma="c2")
            P.op("sp", lambda e: e.dma_start(out=g2b[:, :], in_=bgb_d[:, D:2 * D]), [], ["g2b"], dma="c3")
            P.op("sp", lambda e: e.dma_start(out=fnw[:, :], in_=fnwb_d), [], ["fnw"], dma="c4")

            P.op("pool", lambda e: e.memset(identf[:, :], 0.0), [], ["identf"])
            P.op("pool", lambda e: e.affine_select(out=identf[:, :], in_=identf[:, :], pattern=[[-1, 128]],
                                                   compare_op=ALU.not_equal, fill=1.0, base=0, channel_multiplier=1),
                 ["identf"], ["identf"])
            P.op("pool", lambda e: e.memset(cmask4[:, :], 1.0), [], ["cmask4"])
            P.op("pool", lambda e: e.affine_select(out=cmask4[:, :], in_=cmask4[:, :], pattern=[[0, 4], [1, 128]],
                                                   compare_op=ALU.is_ge, fill=0.0, base=0, channel_multiplier=-1),
                 ["cmask4"], ["cmask4"])
            P.op("pool", lambda e: e.memset(uneg[:, :], -1.0 / 16.0), [], ["uneg"])
            P.op("pool", lambda e: e.affine_select(out=uneg[:, :], in_=uneg[:, :], pattern=[[1, 128]],
                                                   compare_op=ALU.is_ge, fill=0.0, base=0, channel_multiplier=-1),
                 ["uneg"], ["uneg"])
            P.op("pool", lambda e: e.memset(lrT[:, :], 1.0), [], ["lrT"])
            P.op("pool", lambda e: e.dma_start(out=wlr[:, :, :], in_=W["w_in"][:, LR0:LR0 + 16].rearrange("(kc p) c -> p kc c", p=128)),
                 [], ["wlr"], dma="c7")
            if not P.dry:
                for j in range(min(4, len(P.wplan))):
                    rec_load(j)
            P.op("pool", lambda e: e.dma_start(out=wkp, in_=W["w_in"][:, K0:K0 + 512].rearrange("(kc p) c -> p kc c", p=128)),
                 [], ["wkp"], dma="c8")
            for n_ in range(2):
                P.op("pool", lambda e, n_=n_: e.dma_start(out=wvp[n_], in_=W["w_in"][:, V0 + n_ * 512:V0 + (n_ + 1) * 512].rearrange("(kc p) c -> p kc c", p=128)),
                     [], [f"wvp{n_}"], dma=f"c{9 + n_}")
            if not P.dry:
                for j in range(4, min(NW, len(P.wplan))):
                    rec_load(j)

            rec_convs(8)

            Vcopy(ident[:, :], identf[:, :], ["identf"], ["ident"])
            P.op("dve", lambda e: e.memset(ones_bf[:, :], 1.0), [], ["ones"])
            P.op("dve", lambda e: e.memset(S32[:, :, :], 0.0), [], [("S32", h) for h in range(4)])
            P.op("dve", lambda e: e.memset(S_bfs[0][:, :, :], 0.0), [], [("Sbf", 0)])
            P.op("dve", lambda e: e.memset(S_bfs[1][:, :, :], 0.0), [], [("Sbf", 1)])
            P.op("dve", lambda e: e.memset(uh[:, :, :], 0.0), [], [("uh", m) for m in range(8)])

            def rec_xload(g):
                src_t = x_prev if g < 4 else x_cur
                t0 = (g % 4) * T
                par = g % 2
                src = src_t[t0:t0 + T, :].rearrange("(s p) d -> p s d", p=128)
                P.op("sp", lambda e: e.dma_start(out=xbuf[par][:, :, :], in_=src), [],
                     [("x", par, s) for s in range(4)], dma=f"x{par}")

            rec_xload(0)
            rec_xload(1)

            cT = cst[:, C_CT:C_CT + 8]
            A(ce[:, :], cT, AF.Exp, ["cst"], ["ce"], scale=-1.0)
            Vsadd(ce[:, :], ce[:, :], 1.0, ["ce"], ["ce"])
            Vrecip(ce[:, :], ce[:, :], ["ce"], ["ce"])
            Vtt(cact[:, :], ce[:, :], cT, ALU.mult, ["ce", "cst"], ["cact"])
            Vcopy(cact_bf[:, :], cact[:, :], ["cact"], ["cactbf"])
            for kc in range(8):
                Vsmul(cbm[:, kc, :], ones_bf[:, :], cact[:, kc:kc + 1], ["ones", "cact"], [("cbm", kc)])

            def ada_chunk(ci):
                wt, wk = next_w(("w_ada", 0, ci * 512, 512))
                fm = {0: 0, 1: 0, 2: 1, 3: 1, 6: 2, 7: 2, 8: 3, 9: 3}
                if ci in fm:
                    j0 = fm[ci] * 8 + (ci % 2) * 4
                    p_, pk = bank()
                    mms = []
                    for jj in range(4):
                        for kc in range(8):
                            mms.append((p_[:, jj:jj + 1], wt[:, kc, jj * 128:(jj + 1) * 128], cact_bf[:, kc:kc + 1],
                                        kc == 0, kc == 7))
                    MM(mms, [wk, "cactbf"], [pk])
                    Vtt(modT[:, j0:j0 + 4], p_[:, 0:4], cst[:, C_BADA + j0:C_BADA + j0 + 4], ALU.add,
                        [pk, "cst"], [("modT", j0)])
                else:
                    gb_ = g1b if ci in (4, 5) else g2b
                    gk = "g1b" if ci in (4, 5) else "g2b"
                    hs_ = slice((ci % 2) * 512, (ci % 2) * 512 + 512)
                    p_, pk = bank()
                    MM([(p_[:, :], cbm[:, kc, :], wt[:, kc, :], kc == 0, kc == 7) for kc in range(8)],
                       [wk] + [("cbm", kc) for kc in range(8)], [pk])
                    Vtt(gb_[:, hs_], p_[:, :], gb_[:, hs_], ALU.add, [pk, gk], [gk])
                done_w()

            def mod_finish(which):
                sc0 = 8 if which == 1 else 24
                nw0 = C_N1 if which == 1 else C_N2
                dst = a1T if which == 1 else a2T
                key = "a1T" if which == 1 else "a2T"
                Vsadd(dst[:, :], modT[:, sc0:sc0 + 8], 1.0, [("modT", sc0), ("modT", sc0 + 4)], [key])
                Vtt(dst[:, :], dst[:, :], cst[:, nw0:nw0 + 8], ALU.mult, [key, "cst"], [key])

            def stage_norm(xb, par, aT_, akey, sh0):
                norm_p1(xb, par)
                norm_p2(aT_, akey, sh0)

            def norm_p1(xb, par):
                for s in range(4):
                    xk = ("x", par, s)
                    A(xn[:, s, :], xb[:, s, :], AF.Square, [xk], [("xn", s), ("ss", s)], accum_out=ss[:, s:s + 1])
                    A(lnv[:, s:s + 1], ss[:, s:s + 1], AF.Ln, [("ss", s)], [("lnv", s)], scale=1.0 / D, bias=EPS)
                    A(rstd[:, s:s + 1], lnv[:, s:s + 1], AF.Exp, [("lnv", s)], [("rstd", s)], scale=-0.5)
                    Vsmul(xn[:, s, :], xb[:, s, :], rstd[:, s:s + 1], [xk, ("rstd", s)], [("xn", s)])

            def norm_p2(aT_, akey, sh0):
                shkeys = [("modT", sh0), ("modT", sh0 + 4)]
                for kc in range(8):
                    pb, pbk = bbank()
                    TR([(pb[:, s * 128:(s + 1) * 128], xn[:, s, kc * 128:(kc + 1) * 128]) for s in range(4)],
                       [("xn", s) for s in range(4)], [pbk])
                    a_ap = aT_[:, kc:kc + 1]
                    s_ap = modT[:, sh0 + kc:sh0 + kc + 1]
                    if kc % 2 == 0:
                        Vts(hT[:, kc, :], pb[:, 0:512], a_ap, s_ap, ALU.mult, ALU.add, [pbk, akey] + shkeys, [("hT", kc)])
                    else:
                        A(hT[:, kc, :], pb[:, 0:512], AF.Identity, [pbk, akey] + shkeys, [("hT", kc)], scale=a_ap, bias=s_ap)

            def sigmoid(p_, pk):
                t, tk = tmp()
                A(t[:, :], p_[:, :], AF.Exp, [pk], [tk], scale=-1.0)
                A(t[:, :], t[:, :], AF.Ln, [tk], [tk], bias=1.0)
                A(t[:, :], t[:, :], AF.Exp, [tk], [tk], scale=-1.0)
                return t, tk

            flag_ap = cst[:, C_FLAG:C_FLAG + 1]

            def la_stage():
                la_p1()
                la_p2()

            def la_p1():
                p_, pk = bank()
                MM([(p_[0:16, :], wlr[:, kc, 0:16], hT[:, kc, :], kc == 0, kc == 7) for kc in range(8)],
                   ["wlr"] + HT_ALL, [pk])
                A(lrT[0:16, :], p_[0:16, :], AF.Copy, [pk], ["lrT"])

            def la_p2():
                for s in range(4):
                    p_, pk = bank()
                    MM([(p_[:, :], lrT[0:17, s * 128:(s + 1) * 128], wg[0:17, :], True, True)], ["lrT", "wg"], [pk])
                    t, tk = tmp()
                    A(t[:, :], p_[:, :], AF.Exp, [pk], [tk], scale=-1.0)
                    A(spb[:, s, :], t[:, :], AF.Ln, [tk], [("sp", s)], bias=1.0)

            def gla_stage(main, last_prev, special=False, hoist=None, nxt=None):
                if last_prev:
                    P.op("sp", lambda e: e.dma_start(out=w32q, in_=W["w_in"][:, Q0:Q0 + 512].rearrange("(kc p) c -> p kc c", p=128)),
                         [], ["w32q"] + [("x", 1, s_) for s_ in range(4)], dma="c5")
                    Vcopy(hTh[:, :, :], hT[:, :, 510:512], HT_ALL, ["hTh"])
                if main:
                    wq, wqk = next_w(("w_in", 0, Q0, 512))
                if main:
                    wk_, wkk = next_w(("w_in", 0, K0, 512))
                else:
                    wk_, wkk = wkp, "wkp"
                e2s = []
                for h in range(4):
                    hs = slice(h * 128, (h + 1) * 128)
                    pbb, pbbk = bank()
                    MM([(pbb[:, s * 128:(s + 1) * 128], spb[:, s, hs], uneg[:, :], True, True) for s in range(4)],
                       [("sp", s) for s in range(4)] + ["uneg"], [pbbk])
                    A(E1[:, h, :], pbb[:, :], AF.Exp, [pbbk], [("E1", h)])
                    e2, e2k = tmp()
                    A(e2[:, :], pbb[:, :], AF.Exp, [pbbk], [e2k], scale=-1.0)
                    e2s.append((e2, e2k))
                if (not main) and nxt is not None:
                    norm_p1(xbuf[nxt % 2], nxt % 2)

                def head_front(h):
                    hs = slice(h * 128, (h + 1) * 128)
                    e2, e2k = e2s[h]
                    if main:
                        pq, pqk = bank()
                        MM([(pq[:, :], wq[:, kc, hs], hT[:, kc, :], kc == 0, kc == 7) for kc in range(8)],
                           [wqk] + HT_ALL, [pqk])
                    pkk_, pkkk = bank()
                    MM([(pkk_[:, :], wk_[:, kc, hs], hT[:, kc, :], kc == 0, kc == 7) for kc in range(8)],
                       [wkk] + HT_ALL, [pkkk])
                    if special:
                        MM([(pq[:, 0:128], w32q[:, kc, hs], h32T[:, kc, :], kc == 0, kc == 7) for kc in range(8)],
                           ["w32q", "h32T", ("x", 1, 0)], [pqk])
                        MM([(pkk_[:, 0:128], w32k[:, kc, hs], h32T[:, kc, :], kc == 0, kc == 7) for kc in range(8)],
                           ["w32k", "h32T"], [pkkk])
                    if main:
                        Vstt(qdT[:, h, :], pq[:, :], 128.0 ** -0.5, E1[:, h, :], ALU.mult, ALU.mult,
                             [pqk, ("E1", h)], [("qdT", h)])
                    Vtt(kdT[:, h, :], pkk_[:, :], e2[:, :], ALU.mult, [pkkk, e2k], [("kdT", h)])
                    if special:
                        Vstt(qd32[:, h, :], pq[:, 0:128], 128.0 ** -0.5, E1[:, h, 0:128], ALU.mult, ALU.mult,
                             [pqk, ("E1", h)], [("qd32", h), "xn32"])
                        Vtt(kd32[:, h, :], pkk_[:, 0:128], e2[:, 0:128], ALU.mult, [pkkk, e2k], [("kd32", h), "xn32"])

                def head_back(h):
                    pb, pbk = bbank()
                    TR([(pb[:, s * 128:(s + 1) * 128], kdT[:, h, s * 128:(s + 1) * 128]) for s in range(4)], [("kdT", h)], [pbk])
                    Vcopy(ke[:, h, :], pb[:, 0:512], [pbk], [("ke", h)])

                for h in range(4):
                    head_front(h)
                    if h >= 1:
                        head_back(h - 1)
                head_back(3)
                if main:
                    done_w()
                    done_w()
                for n in range(2):
                    if main:
                        wv, wvk = next_w(("w_in", 0, V0 + n * 512, 512))
                    else:
                        wv, wvk = wvp[n], f"wvp{n}"
                    for s in range(4):
                        p_, pk = bank()
                        MM([(p_[:, :], hT[:, kc, s * 128:(s + 1) * 128], wv[:, kc, :], kc == 0, kc == 7) for kc in range(8)],
                           [wvk] + HT_ALL, [pk])
                        if s % 2 == 0:
                            Vcopy(v_sb[:, s, n * 512:(n + 1) * 512], p_[:, :], [pk], [("v", s)])
                        else:
                            A(v_sb[:, s, n * 512:(n + 1) * 512], p_[:, :], AF.Copy, [pk], [("v", s)])
                    if main:
                        done_w()
                if last_prev:
                    P.op("sp", lambda e: e.dma_start(out=w32k, in_=W["w_in"][:, K0:K0 + 512].rearrange("(kc p) c -> p kc c", p=128)),
                         [], ["w32k"], dma="c6")
                if (not main) and nxt is not None:
                    norm_p2(a1T, "a1T", 0)
                    la_p1()
                if main:
                    for n in range(2):
                        wgg, wggk = next_w(("w_in", 0, G0 + n * 512, 512))
                        for s in range(4):
                            p_, pk = bank()
                            MM([(p_[:, :], hT[:, kc, s * 128:(s + 1) * 128], wgg[:, kc, :], kc == 0, kc == 7) for kc in range(8)],
                               [wggk] + HT_ALL, [pk])
                            r_, rk = sigmoid(p_, pk)
                            Vtt(sg[:, s, n * 512:(n + 1) * 512], p_[:, :], r_[:, :], ALU.mult, [pk, rk],
                                [("sg", s, 2 * n), ("sg", s, 2 * n + 1)])
                        done_w()
                def phaseA(s):
                    cs = slice(s * 128, (s + 1) * 128)
                    st = {}
                    if main:
                        psc, psck = bank()
                        if special and s == 0:
                            MM([(psc[:, h * 128:(h + 1) * 128], kd32[:, h, :], qd32[:, h, :], True, True) for h in range(4)],
                               [("kd32", h) for h in range(4)] + [("qd32", h) for h in range(4)], [psck])
                        else:
                            MM([(psc[:, h * 128:(h + 1) * 128], kdT[:, h, cs], qdT[:, h, cs], True, True) for h in range(4)],
                               [("kdT", h) for h in range(4)] + [("qdT", h) for h in range(4)], [psck])
                        sc, sck = scbuf()
                        Vtt(sc[:, :], psc[:, :], cmask4[:, :], ALU.mult, [psck, "cmask4"], [sck])
                        st["sc"] = (sc, sck)
                    pts = []
                    for hp in range(2):
                        pt, ptk = bank()
                        MM([(pt[:, j * 256:(j + 1) * 256], ke[:, hp * 2 + j, cs],
                             v_sb[:, s, (hp * 2 + j) * 256:(hp * 2 + j + 1) * 256], True, True) for j in range(2)],
                           [("ke", hp * 2), ("ke", hp * 2 + 1), ("v", s)], [ptk])
                        pts.append((pt, ptk))
                    st["pts"] = pts
                    return st

                def supd(s, st):
                    for h in range(4):
                        pt, ptk = st["pts"][h // 2]
                        Vtt(S32[:, h, :], S32[:, h, :], pt[:, (h % 2) * 256:(h % 2 + 1) * 256], ALU.add,
                            [("S32", h), ptk], [("S32", h)])
                        Vsmul(S32[:, h, :], S32[:, h, :], E1[:, h, s * 128 + 127:s * 128 + 128],
                              [("S32", h), ("E1", h)], [("S32", h)])

                def o_and_act(s, st):
                    cs = slice(s * 128, (s + 1) * 128)
                    pos = []
                    if main:
                        sc, sck = st["sc"]
                        Sb = S_bfs[s % 2]
                        for hp in range(2):
                            po, pok = bank()
                            mms = []
                            for j in range(2):
                                h = hp * 2 + j
                                mms.append((po[:, j * 256:(j + 1) * 256], sc[:, h * 128:(h + 1) * 128],
                                            v_sb[:, s, h * 256:(h + 1) * 256], True, False))
                                mms.append((po[:, j * 256:(j + 1) * 256], qdT[:, h, cs], Sb[:, h, :], False, True))
                            MM(mms, [sck, ("v", s), ("qdT", hp * 2), ("qdT", hp * 2 + 1), ("Sbf", s % 2)], [pok])
                            pos.append((po, pok))
                    if main:
                        A(S_bfs[(s + 1) % 2][:, :, :], S32[:, :, :], AF.Copy, [("S32", h) for h in range(4)], [("Sbf", (s + 1) % 2)])
                    if main:
                        for h in range(4):
                            po, pok = pos[h // 2]
                            i = s * 4 + h
                            jt, jtk = tmp()
                            A(jt[:, :].bitcast(BF16)[:, 0:256], po[:, (h % 2) * 256:(h % 2 + 1) * 256], AF.Square, [pok],
                              [("sso", i), jtk], accum_out=sso[:, i:i + 1])
                        A(lno[:, s * 4:(s + 1) * 4], sso[:, s * 4:(s + 1) * 4], AF.Ln, [("sso", s * 4 + h) for h in range(4)],
                          [("lno", s)], scale=1.0 / 256, bias=EPS)
                        A(rso[:, s * 4:(s + 1) * 4], lno[:, s * 4:(s + 1) * 4], AF.Exp, [("lno", s)], [("rso", s)], scale=-0.5)
                    return pos

                def og_stage(s, pos):
                    if main:
                        for h in range(4):
                            po, pok = pos[h // 2]
                            i = s * 4 + h
                            sgs = sg[:, s, h * 256:(h + 1) * 256]
                            Vstt(sgs, po[:, (h % 2) * 256:(h % 2 + 1) * 256], rso[:, i:i + 1], sgs, ALU.mult, ALU.mult,
                                 [pok, ("rso", s), ("sg", s, h)], [("sg", s, h)])

                sts = {0: phaseA(0)}
                pos_prev = None
                for s in range(4):
                    supd(s, sts[s])
                    if s + 1 < 4:
                        sts[s + 1] = phaseA(s + 1)
                    pos = o_and_act(s, sts[s])
                    if pos_prev is not None:
                        og_stage(s - 1, pos_prev)
                    pos_prev = pos
                og_stage(3, pos_prev)
                if last_prev:
                    for h in range(4):
                        Vsmul(S32[:, h, :], S32[:, h, :], flag_ap, [("S32", h), "cst"], [("S32", h)])
                    A(S_bfs[0][:, :, :], S32[:, :, :], AF.Copy, [("S32", h) for h in range(4)], [("Sbf", 0)])
                if not main:
                    if nxt is not None:
                        la_p2()
                    return
                for kc in range(8):
                    pb, pbk = bbank()
                    TR([(pb[:, s * 128:(s + 1) * 128], sg[:, s, kc * 128:(kc + 1) * 128]) for s in range(4)],
                       [("sg", s, kc // 2) for s in range(4)], [pbk])
                    g_ap = cst[:, C_GNW + kc % 2:C_GNW + kc % 2 + 1]
                    if kc % 2:
                        A(ogT[:, kc, :], pb[:, 0:512], AF.Copy, [pbk, "cst"], [("ogT", kc)], scale=g_ap)
                    else:
                        Vsmul(ogT[:, kc, :], pb[:, 0:512], g_ap, [pbk, "cst"], [("ogT", kc)])

            def cw_ap(m, j):
                c = C_CW + m * 3 + j
                return cst[:, c:c + 1]

            def conv_stage(first=False):
                for mg in range(2):
                    wcb, wcbk = next_w(("w_in", 0, CB0 + mg * 512, 512))
                    wcc, wcck = next_w(("w_in", 0, CC0 + mg * 512, 512))
                    wcx, wcxk = next_w(("w_in", 0, CX0 + mg * 512, 512))
                    for mm in range(4):
                        m = mg * 4 + mm
                        ms = slice(mm * 128, (mm + 1) * 128)
                        pc, pck = bank()
                        MM([(pc[:, :], wcc[:, kc, ms], hT[:, kc, :], kc == 0, kc == 7) for kc in range(8)], [wcck] + HT_ALL, [pck])
                        px, pxk = bank()
                        MM([(px[:, :], wcx[:, kc, ms], hT[:, kc, :], kc == 0, kc == 7) for kc in range(8)], [wcxk] + HT_ALL, [pxk])
                        pcb, pcbk = bank()
                        MM([(pcb[:, :], wcb[:, kc, ms], hT[:, kc, :], kc == 0, kc == 7) for kc in range(8)], [wcbk] + HT_ALL, [pcbk])
                        if first:
                            ph, phk = bank()
                            MM([(ph[:, 0:2], wcc[:, kc, ms], hTh[:, kc, :], kc == 0, kc == 7) for kc in range(8)]
                               + [(ph[:, 2:4], wcx[:, kc, ms], hTh[:, kc, :], kc == 0, kc == 7) for kc in range(8)],
                               [wcck, wcxk, "hTh"], [phk])
                            th, thk = tmp()
                            A(th[:, 0:2], ph[:, 0:2], AF.Copy, [phk], [thk])
                            Vtt(th[:, 2:4], th[:, 0:2], ph[:, 2:4], ALU.mult, [thk, phk], [thk])
                            Vsmul(uh[:, m, :], th[:, 2:4], flag_ap, [thk, "cst"], [("uh", m)])
                        t, tk = tmp()
                        A(t[:, :], pc[:, :], AF.Copy, [pck], [tk])
                        u, uk = ubuf()
                        A(u[:, 0:2], uh[:, m, :], AF.Copy, [("uh", m)], [uk])
                        Vtt(u[:, 2:514], t[:, :], px[:, :], ALU.mult, [tk, pxk, uk], [uk])
                        A(uh[:, m, :], u[:, 512:514], AF.Copy, [uk], [("uh", m)])
                        a_, ak = tmp()
                        Vsmul(a_[:, :], u[:, 2:514], cw_ap(m, 2), [uk, "cst"], [ak])
                        Vstt(a_[:, :], u[:, 1:513], cw_ap(m, 1), a_[:, :], ALU.mult, ALU.add, [uk, "cst", ak], [ak])
                        Vstt(a_[:, :], u[:, 0:512], cw_ap(m, 0), a_[:, :], ALU.mult, ALU.add, [uk, "cst", ak], [ak])
                        Vtt(cbuT[:, m, :], a_[:, :], pcb[:, :], ALU.mult, [ak, pcbk], [("cbuT", m)])
                    done_w()
                    done_w()
                    done_w()

            def merge_stage(xb, par):
                OGT_ALL = [("ogT", kc) for kc in range(8)]
                CBU_ALL = [("cbuT", kc) for kc in range(8)]
                for mg in range(2):
                    wa, wak = next_w(("w_pa", 0, mg * 512, 512))
                    wga_, wgak = next_w(("w_in", 0, GA0 + mg * 512, 512))
                    wb, wbk = next_w(("w_pb", 0, mg * 512, 512))
                    wgb, wgbk = next_w(("w_in", 0, GB0 + mg * 512, 512))
                    for mm in range(4):
                        m = mg * 4 + mm
                        ms = slice(mm * 128, (mm + 1) * 128)
                        pya, pyak = bank()
                        MM([(pya[:, :], wa[:, kc, ms], ogT[:, kc, :], kc == 0, kc == 7) for kc in range(8)], [wak] + OGT_ALL, [pyak])
                        pga, pgak = bank()
                        MM([(pga[:, :], wga_[:, kc, ms], hT[:, kc, :], kc == 0, kc == 7) for kc in range(8)], [wgak] + HT_ALL, [pgak])
                        ra, rak = sigmoid(pga, pgak)
                        Vtt(ra[:, :], ra[:, :], pya[:, :], ALU.mult, [rak, pyak], [rak])
                        pyb, pybk = bank()
                        MM([(pyb[:, :], wb[:, kc, ms], cbuT[:, kc, :], kc == 0, kc == 7) for kc in range(8)], [wbk] + CBU_ALL, [pybk])
                        pgb, pgbk = bank()
                        MM([(pgb[:, :], wgb[:, kc, ms], hT[:, kc, :], kc == 0, kc == 7) for kc in range(8)], [wgbk] + HT_ALL, [pgbk])
                        rb, rbk = sigmoid(pgb, pgbk)
                        Vtt(rb[:, :], rb[:, :], pyb[:, :], ALU.mult, [rbk, pybk], [rbk])
                        Vtt(zT[:, m, :], ra[:, :], rb[:, :], ALU.add, [rak, rbk], [("zT", m)])
                    for _ in range(4):
                        done_w()
                ZT_ALL = [("zT", kc) for kc in range(8)]
                for n in range(2):
                    wo, wok = next_w(("w_o", 0, n * 512, 512))
                    ns = slice(n * 512, (n + 1) * 512)
                    for s in range(4):
                        p_, pk = bank()
                        MM([(p_[:, :], zT[:, kc, s * 128:(s + 1) * 128], wo[:, kc, :], kc == 0, kc == 7) for kc in range(8)],
                           [wok] + ZT_ALL, [pk])
                        t, tk = tmp()
                        Vtt(t[:, :], p_[:, :], g1b[:, ns], ALU.mult, [pk, "g1b"], [tk])
                        xk = ("x", par, s)
                        Vtt(xb[:, s, ns], xb[:, s, ns], t[:, :], ALU.add, [xk, tk], [xk])
                    done_w()

            def mlp_stage(xb, par, g, hoist=None, nxt=None):
                stage_norm(xb, par, a2T, "a2T", 16)
                if nxt is not None:
                    norm_p1(xbuf[nxt % 2], nxt % 2)
                for cg in range(8):
                    w1_, w1k = next_w(("w1", 0, cg * 512, 512))
                    for jj in range(4):
                        j = cg * 4 + jj
                        p_, pk = bank()
                        MM([(p_[:, :], w1_[:, kc, jj * 128:(jj + 1) * 128], hT[:, kc, :], kc == 0, kc == 7) for kc in range(8)],
                           [w1k] + HT_ALL, [pk])
                        t, tk = tmp()
                        A(t[:, :], p_[:, :], AF.Relu, [pk], [tk])
                        Vtt(aT[:, j, :], t[:, :], t[:, :], ALU.mult, [tk], [("aT", j)])
                    done_w()
                for n in range(2):
                    ns = slice(n * 512, (n + 1) * 512)
                    bks = [bank() for _ in range(4)]
                    for jg in range(4):
                        w2_, w2k = next_w(("w2", jg * 1024, n * 512, 512))
                        for s in range(4):
                            MM([(bks[s][0][:, :], aT[:, jg * 8 + jj, s * 128:(s + 1) * 128], w2_[:, jj, :],
                                 jg == 0 and jj == 0, jg == 3 and jj == 7) for jj in range(8)],
                               [w2k] + [("aT", jg * 8 + jj) for jj in range(8)], [bks[s][1]])
                        done_w()
                    for s in range(4):
                        t, tk = tmp()
                        Vtt(t[:, :], bks[s][0][:, :], g2b[:, ns], ALU.mult, [bks[s][1], "g2b"], [tk])
                        xk = ("x", par, s)
                        Vtt(xb[:, s, ns], xb[:, s, ns], t[:, :], ALU.add, [xk, tk], [xk])
                    if n == 0 and nxt is not None:
                        norm_p2(a1T, "a1T", 0)
                        la_p1()
                        la_p2()
                for s in range(4):
                    xk = ("x", par, s)
                    jt, jtk = tmp()
                    A(jt[:, :].bitcast(BF16), xb[:, s, :], AF.Square, [xk], [jtk, ("ss2", s)], accum_out=ss2[:, s:s + 1])
                    A(lnv2[:, s:s + 1], ss2[:, s:s + 1], AF.Ln, [("ss2", s)], [("lnv2", s)], scale=1.0 / D, bias=EPS)
                    A(rstd2[:, s:s + 1], lnv2[:, s:s + 1], AF.Exp, [("lnv2", s)], [("rstd2", s)], scale=-0.5)
                    A(xb[:, s, :], xb[:, s, :], AF.Copy, [xk, ("rstd2", s)], [xk], scale=rstd2[:, s:s + 1])
                    Vtt(xb[:, s, :], xb[:, s, :], fnw[:, :], ALU.mult, [xk, "fnw"], [xk])
                t0 = (g % 4) * T
                dst = out_d[t0:t0 + T, :].rearrange("(s p) d -> p s d", p=128)
                P.op("sp", lambda e: e.dma_start(out=dst, in_=xb[:, :, :]), [("x", par, s) for s in range(4)],
                     [("out", g)], dma=f"o{par}")

            def special_prep(xb, par):
                Vsmul(xn32, xb[:, 0, :], rstd[:, 0:1], [("x", par, 0), ("rstd", 0)], ["xn32"])
                for half in range(2):
                    p_, pk = bank()
                    def fn(e, p_=p_, half=half):
                        ins = None
                        for j in range(4):
                            kc = half * 4 + j
                            ins = e.transpose(out=p_[:, j * 128:(j + 1) * 128], in_=xn32[:, kc * 128:(kc + 1) * 128],
                                              identity=identf[:, :])
                        return ins
                    P.op("pe", fn, ["xn32", "identf"], [pk])
                    for j in range(4):
                        kc = half * 4 + j
                        Vts(h32T[:, kc, :], p_[:, j * 128:(j + 1) * 128], a1T[:, kc:kc + 1], modT[:, kc:kc + 1],
                            ALU.mult, ALU.add, [pk, "a1T", ("modT", 0), ("modT", 4)], ["h32T"])

            for ci in range(4):
                ada_chunk(ci)
            mod_finish(1)
            rest = [4, 5, 6, 7, 8, 9, 10, 11]
            for g in range(8):
                par = g % 2
                xb = xbuf[par]
                if g >= 4:
                    P.wmode = "t4" if g == 4 else "t5" if g == 5 else "scr"
                    P.widx = 0
                if g == 0:
                    stage_norm(xb, par, a1T, "a1T", 0)
                    la_stage()
                nxt = g + 1 if g + 1 < 8 else None
                if g < 3:
                    rec_xload(g + 2)
                if g < 4:
                    gla_stage(False, g == 3, nxt=nxt)
                    for ci in rest[g * 2:g * 2 + 2]:
                        ada_chunk(ci)
                    if g == 0:
                        rec_convs(100)
                    if g == 3:
                        mod_finish(2)
                else:
                    if g == 4:
                        special_prep(xb, par)
                    gla_stage(True, False, special=(g == 4))
                    if g == 4:
                        rec_xload(5)
                    conv_stage(first=(g == 4))
                    merge_stage(xb, par)
                    mlp_stage(xb, par, g, nxt=nxt)
                if g + 2 < 8 and g >= 4:
                    rec_xload(g + 2)
            P.op("sp", None, [("out", g) for g in range(4, 8)], [])

        wplan = []
        record(Prog(True, wplan))
        P = Prog(False, wplan)
        record(P)
        assert P.wi == len(wplan), (P.wi, len(wplan))

        sem_names = P.sems()
        S = {n: es.enter_context(nc.semaphore(n)) for n in sem_names}
        block = es.enter_context(nc.Block())

        def run_stream(eng_name):
            def body(e):
                for (waits, fn, sem, amt) in P.streams[eng_name]:
                    for (s_, v_) in waits:
                        e.wait_ge(S[s_], v_)
                    if fn is not None:
                        fn(e).then_inc(S[sem], amt)
            return body

        block.sync(run_stream("sp"))
        block.gpsimd(run_stream("pool"))
        block.tensor(run_stream("pe"))
        block.scalar(run_stream("act"))
        block.vector(run_stream("dve"))
    return nc


_NC = None


def kernel(x, c, w_ada, b_ada, norm1_w, w_in, w_gate_up, b_gate, gla_norm_w, conv_w,
           w_proj_a, w_proj_b, w_out, norm2_w, w_mlp1, w_mlp2, final_norm_w):
    global _NC
    f = lambda a: np.ascontiguousarray(np.asarray(a, dtype=np.float32))
    x = f(x)
    c = f(c)
    b_ada = f(b_ada)[0]
    shared = {
        "w_ada": f(w_ada)[0], "w_in": f(w_in)[0], "w_pa": f(w_proj_a)[0], "w_pb": f(w_proj_b)[0],
        "w_o": f(w_out)[0], "w1": f(w_mlp1)[0], "w2": f(w_mlp2)[0],
        "fnw_b": np.ascontiguousarray(np.broadcast_to(f(final_norm_w)[None, :], (128, D))),
        "bgate_b": np.ascontiguousarray(np.broadcast_to(
            np.concatenate([b_ada[2 * D:3 * D], b_ada[5 * D:6 * D]])[None, :], (128, 2 * D))),
        "wg_aug": np.ascontiguousarray(np.concatenate([f(w_gate_up)[0], f(b_gate)[0][None, :]], axis=0)),
    }
    colT = lambda v: np.ascontiguousarray(v.reshape(-1, 128).T)
    cbase = np.zeros((128, NCONST), np.float32)
    cbase[:, C_BADA:C_BADA + 32] = np.concatenate(
        [colT(b_ada[0:D]), colT(b_ada[D:2 * D]), colT(b_ada[3 * D:4 * D]), colT(b_ada[4 * D:5 * D])], axis=1)
    cbase[:, C_N1:C_N1 + 8] = colT(f(norm1_w)[0])
    cbase[:, C_N2:C_N2 + 8] = colT(f(norm2_w)[0])
    cwl = f(conv_w)[0]
    cbase[:, C_CW:C_CW + 24] = np.transpose(cwl.reshape(3, 8, 128), (2, 1, 0)).reshape(128, 24)
    cbase[:, C_GNW:C_GNW + 2] = colT(f(gla_norm_w)[0])
    if _NC is None:
        _NC = build_nc()
    in_maps = []
    for i in range(8):
        b, hf = i // 2, i % 2
        cs = cbase.copy()
        cs[:, C_CT:C_CT + 8] = colT(c[b])
        cs[:, C_FLAG] = float(hf)
        m = dict(shared)
        m["x_cur"] = np.ascontiguousarray(x[b, hf * TOK:(hf + 1) * TOK])
        m["x_prev"] = np.ascontiguousarray(x[b, 0:TOK])
        m["consts"] = cs
        in_maps.append(m)
    res = run_bass_kernel_spmd(_NC, in_maps, core_ids=list(range(8)))
    out = np.empty((4, 2 * TOK, D), np.float32)
    for i in range(8):
        b, hf = i // 2, i % 2
        out[b, hf * TOK:(hf + 1) * TOK] = np.asarray(res.results[i]["out"]).reshape(TOK, D)
    return out
```

```python
import numpy as np
from contextlib import ExitStack
import concourse.bass as bass
import concourse.mybir as mybir
from concourse.bass_utils import run_bass_kernel_spmd

F32 = mybir.dt.float32
BF16 = mybir.dt.bfloat16
AF = mybir.ActivationFunctionType
ALU = mybir.AluOpType

D = 1024
TOK = 2048
T = 512
NW = 5
NTMP = 8
NSCR = 40
PRECONV = ("w2",)
EPS = 1e-6
Q0, K0, V0, G0, LR0, CB0, CC0, CX0, GA0, GB0 = 0, 512, 1024, 2048, 3072, 3088, 4112, 5136, 6160, 7184
C_CT, C_BADA, C_N1, C_N2, C_CW, C_GNW, C_FLAG, NCONST = 0, 8, 40, 48, 56, 80, 82, 84


class Ev:
    __slots__ = ("eng", "sem", "val", "know", "dma")

    def __init__(self, eng, sem, val, know, dma):
        self.eng, self.sem, self.val, self.know, self.dma = eng, sem, val, know, dma


class Prog:
    ENGS = ("pe", "act", "dve", "pool", "sp")

    def __init__(self, dry, wplan):
        self.dry = dry
        self.wplan = wplan
        self.streams = {e: [] for e in self.ENGS}
        self.cnt = {e: 0 for e in self.ENGS}
        self.dcnt = {}
        self.know = {e: {} for e in self.ENGS}
        self.last_w = {}
        self.readers = {}
        self.groups = {}
        self.cur_view = {}
        self.barrier = {}
        self.name_keys = {}
        self.bi = 0
        self.bbi = 0
        self.ti = 0
        self.wi = 0
        self.wdone = 0
        self.wmode = "cast"
        self.widx = -1

    def op(self, eng, fn, reads=(), writes=(), dma=None):
        if self.dry:
            return
        deps = []
        for k in list(reads) + list(writes):
            name = k[0] if isinstance(k, tuple) else k
            self.name_keys.setdefault(name, set()).add(k)
            for g in self.groups.get(name, ()):
                if self.cur_view.get(g) != name:
                    old = self.cur_view.get(g)
                    evs = []
                    if old is not None:
                        for kk in self.name_keys.get(old, ()):
                            if kk in self.last_w:
                                evs.append(self.last_w[kk])
                            evs += self.readers.get(kk, [])
                    self.barrier[g] = evs
                    self.cur_view[g] = name
                deps += self.barrier.get(g, [])
        is_dma = dma is not None
        for k in reads:
            ev = self.last_w.get(k)
            if ev is not None:
                deps.append(ev)
        for k in writes:
            ev = self.last_w.get(k)
            if ev is not None and (is_dma or ev.dma or ev.eng != eng or eng != "pe"):
                deps.append(ev)
            for ev in self.readers.get(k, []):
                if is_dma or ev.dma or ev.eng != eng or eng != "pe":
                    deps.append(ev)
        kn = self.know[eng]
        waits = {}
        for ev in deps:
            if kn.get(ev.sem, 0) >= ev.val:
                continue
            waits[ev.sem] = max(waits.get(ev.sem, 0), ev.val)
            for s_, v_ in ev.know.items():
                if kn.get(s_, 0) < v_:
                    kn[s_] = v_
        if fn is None:
            self.streams[eng].append((list(waits.items()), None, None, 0))
            return
        if is_dma:
            sem = dma
            self.dcnt[sem] = self.dcnt.get(sem, 0) + 16
            val = self.dcnt[sem]
            amt = 16
        else:
            sem = "E_" + eng
            self.cnt[eng] += 1
            val = self.cnt[eng]
            amt = 1
        evk = dict(kn)
        evk[sem] = val
        ev = Ev(eng, sem, val, evk, is_dma)
        for k in reads:
            self.readers.setdefault(k, []).append(ev)
        for k in writes:
            self.last_w[k] = ev
            self.readers[k] = []
        self.streams[eng].append((list(waits.items()), fn, sem, amt))

    def sems(self):
        s = {"E_" + e for e in self.ENGS}
        s |= set(self.dcnt.keys())
        return sorted(s)


def build_nc():
    nc = bass.Bass("TRN2", target_bir_lowering=False)

    def din(name, shape):
        return nc.dram_tensor(name, shape, F32, kind="ExternalInput").ap()

    x_cur = din("x_cur", [TOK, D])
    x_prev = din("x_prev", [TOK, D])
    W = {
        "w_ada": din("w_ada", [D, 6 * D]),
        "w_in": din("w_in", [D, 8208]),
        "w_pa": din("w_pa", [D, D]),
        "w_pb": din("w_pb", [D, D]),
        "w_o": din("w_o", [D, D]),
        "w1": din("w1", [D, 4 * D]),
        "w2": din("w2", [4 * D, D]),
    }
    consts_d = din("consts", [128, NCONST])
    fnwb_d = din("fnw_b", [128, D])
    bgb_d = din("bgate_b", [128, 2 * D])
    wga_d = din("wg_aug", [17, 512])
    out_d = nc.dram_tensor("out", [TOK, D], F32, kind="ExternalOutput").ap()
    wscr = nc.dram_tensor("wscr", [NSCR, 128, 4096], BF16, kind="Internal").ap()

    with ExitStack() as es:
        def sb(name, shape, dt):
            return es.enter_context(nc.sbuf_tensor(name, shape, dt))

        xbuf = [sb(f"xbuf{i}", [128, 4, D], F32) for i in range(2)]
        xn = sb("xn", [128, 4, D], BF16)
        hT = sb("hT", [128, 8, T], BF16)
        wsl = [sb(f"wsl{i}", [128, 8, 512], BF16) for i in range(NW)]
        wlr = sb("wlr", [128, 8, 16], BF16)
        lrT = sb("lrT", [32, T], F32)
        wg = sb("wg", [32, 512], F32)
        spb = sb("spb", [128, 4, 512], F32)
        E1 = sb("E1", [128, 4, T], F32)
        spc = sb("spc", [128, 4, 512], F32)
        qdT = sb("qdT", [128, 4, T], BF16)
        kdT = sb("kdT", [128, 4, T], BF16)
        ke = sb("ke", [128, 4, T], BF16)
        big = sb("big", [128, 16384], BF16)
        S32 = sb("S32", [128, 4, 256], F32)
        S_bfs = [sb(f"S_bf{i}", [128, 4, 256], BF16) for i in range(2)]
        scb = [sb(f"scb{i}", [128, 512], BF16) for i in range(2)]
        cmask4 = sb("cmask4", [128, 512], BF16)
        ss2 = sb("ss2", [128, 4], F32)
        lnv2 = sb("lnv2", [128, 4], F32)
        rstd2 = sb("rstd2", [128, 4], F32)
        ub = [sb(f"ub{i}", [128, 514], F32) for i in range(2)]
        uh = sb("uh", [128, 8, 2], F32)
        hTh = sb("hTh", [128, 8, 2], BF16)
        tmps = [sb(f"tmp{i}", [128, 512], F32) for i in range(NTMP)]
        cst = sb("cst", [128, NCONST], F32)
        fnw = sb("fnw", [128, D], F32)
        g1b = sb("g1b", [128, D], F32)
        g2b = sb("g2b", [128, D], F32)
        identf = sb("identf", [128, 128], F32)
        ident = sb("ident", [128, 128], BF16)
        ones_bf = sb("ones_bf", [128, 128], BF16)
        uneg = sb("uneg", [128, 128], F32)
        ce = sb("ce", [128, 8], F32)
        cact = sb("cact", [128, 8], F32)
        cact_bf = sb("cact_bf", [128, 8], BF16)
        modT = sb("modT", [128, 32], F32)
        a1T = sb("a1T", [128, 8], F32)
        a2T = sb("a2T", [128, 8], F32)
        ss = sb("ss", [128, 4], F32)
        lnv = sb("lnv", [128, 4], F32)
        rstd = sb("rstd", [128, 4], F32)
        sso = sb("sso", [128, 16], F32)
        lno = sb("lno", [128, 16], F32)
        rso = sb("rso", [128, 16], F32)

        psf = [es.enter_context(nc.psum_tensor(f"psf{i}", [128, 512], F32)) for i in range(8)]
        psb = [p_[:, :].bitcast(BF16) for p_ in psf]

        v_sb = big[:, 0:4096].rearrange("p (s c) -> p s c", s=4)
        ogT = big[:, 0:4096].rearrange("p (k c) -> p k c", k=8)
        sg = big[:, 4096:8192].rearrange("p (s c) -> p s c", s=4)
        cbuT = big[:, 8192:12288].rearrange("p (k c) -> p k c", k=8)
        zT = big[:, 12288:16384].rearrange("p (k c) -> p k c", k=8)
        aT = big[:, :].rearrange("p (j c) -> p j c", j=32)
        wkp = big[:, 4096:8192].rearrange("p (k c) -> p k c", k=8)
        wvp = [big[:, 8192:12288].rearrange("p (k c) -> p k c", k=8),
               big[:, 12288:16384].rearrange("p (k c) -> p k c", k=8)]
        cbm = qdT[:, 0:2, :].rearrange("p a (k c) -> p (a k) c", k=4)
        w32q = xbuf[1][:, :, :].rearrange("p s (a c) -> p (s a) c", a=2)
        w32k = big[:, 8192:16384].bitcast(F32).rearrange("p (k c) -> p k c", k=8)
        h32T = spc[:, 0:2, :].rearrange("p a (k c) -> p (a k) c", k=4)
        xn32 = spc[:, 2:4, :].rearrange("p a c -> p (a c)")
        qd32 = spc[:, 2, :].rearrange("p (h c) -> p h c", h=4)
        kd32 = spc[:, 3, :].rearrange("p (h c) -> p h c", h=4)

        def record(P):
            P.groups = {"v": ["A"], "ogT": ["A"], "sg": ["B"], "cbuT": ["C"], "zT": ["Dg"],
                        "aT": ["A", "B", "C", "Dg"], "w32k": ["C", "Dg"],
                        "wkp": ["B"], "wvp0": ["C"], "wvp1": ["Dg"],
                        "cbm": ["Q"], "qdT": ["Q"]}

            def A(out, in_, func, r, w, **kw):
                P.op("act", lambda e: e.activation(out=out, in_=in_, func=func, **kw), r, w)

            def Vtt(out, a, b, op, r, w):
                P.op("dve", lambda e: e.tensor_tensor(out=out, in0=a, in1=b, op=op), r, w)

            def Vts(out, a, s1, s2, op0, op1, r, w):
                P.op("dve", lambda e: e.tensor_scalar(out=out, in0=a, scalar1=s1, scalar2=s2, op0=op0, op1=op1), r, w)

            def Vsmul(out, a, s, r, w):
                P.op("dve", lambda e: e.tensor_scalar_mul(out=out, in0=a, scalar1=s), r, w)

            def Vsadd(out, a, s, r, w):
                P.op("dve", lambda e: e.tensor_scalar_add(out=out, in0=a, scalar1=s), r, w)

            def Vstt(out, a, s, b, op0, op1, r, w):
                P.op("dve", lambda e: e.scalar_tensor_tensor(out=out, in0=a, scalar=s, in1=b, op0=op0, op1=op1), r, w)

            def Vcopy(out, a, r, w):
                P.op("dve", lambda e: e.tensor_copy(out=out, in_=a), r, w)

            def Vrecip(out, a, r, w):
                P.op("dve", lambda e: e.reciprocal(out=out, in_=a), r, w)

            def MM(mms, r, w):
                def fn(e):
                    ins = None
                    for (o, l, rh, st, sp_) in mms:
                        ins = e.matmul(out=o, lhsT=l, rhs=rh, start=st, stop=sp_)
                    return ins
                P.op("pe", fn, r, w)

            def TR(trs, r, w):
                def fn(e):
                    ins = None
                    for (o, i) in trs:
                        ins = e.transpose(out=o, in_=i, identity=ident[:])
                    return ins
                P.op("pe", fn, list(r) + ["ident"], w)

            def bank():
                i = P.bi % 8
                P.bi += 1
                return psf[i], ("ps", i)

            def bbank():
                i = P.bi % 8
                P.bi += 1
                return psb[i], ("ps", i)

            def tmp():
                i = P.ti % NTMP
                P.ti += 1
                return tmps[i], ("tmp", i)

            sci = [0]

            def scbuf():
                i = sci[0] % 2
                sci[0] += 1
                return scb[i], ("sc", i)

            ubi = [0]

            def ubuf():
                i = ubi[0] % 2
                ubi[0] += 1
                return ub[i], ("u", i)

            def rec_load(j):
                wname, r0, c0, ncols, mode, idx = P.wplan[j]
                slot = j % NW
                if mode == "scr":
                    src = wscr[idx]
                    dst = wsl[slot][:, :, :].rearrange("p k c -> p (k c)")
                    P.op("pool", lambda e: e.dma_start(out=dst, in_=src), [("scr", idx)], [("w", slot)], dma=f"w{slot}")
                else:
                    src = W[wname][r0:r0 + 1024, c0:c0 + ncols].rearrange("(kc p) c -> p kc c", p=128)
                    dst = wsl[slot][:, :, 0:ncols]
                    P.op("pool", lambda e: e.dma_start(out=dst, in_=src), [], [("w", slot)], dma=f"w{slot}")
                if mode == "castwb":
                    wsrc = wsl[slot][:, :, :].rearrange("p k c -> p (k c)")
                    wdst = wscr[idx]
                    P.op("sp", lambda e: e.dma_start(out=wdst, in_=wsrc), [("w", slot)], [("scr", idx)], dma=f"wb{slot}")

            def is_preconv(spec):
                return spec[0] in PRECONV

            def next_w(spec):
                mode = P.wmode
                if mode in ("t4", "t5"):
                    if is_preconv(spec):
                        mode = "scr"
                    elif P.widx % 2 == 0:
                        mode = "castwb" if mode == "t4" else "scr"
                    else:
                        mode = "cast4" if mode == "t4" else "castwb"
                full = tuple(spec) + (mode, P.widx)
                if P.wmode != "cast":
                    P.widx += 1
                if P.dry:
                    P.wplan.append(full)
                    return wsl[0], ("w", 0)
                i = P.wi
                P.wi += 1
                assert P.wplan[i] == full, (i, P.wplan[i], full)
                return wsl[i % NW], ("w", i % NW)

            def done_w():
                if P.dry:
                    return
                j = P.wdone + NW
                P.wdone += 1
                if j < len(P.wplan):
                    rec_load(j)

            HT_ALL = [("hT", kc) for kc in range(8)]

            conv_list = [] if P.dry else [e_ for e_ in P.wplan if e_[4] == "scr" and is_preconv(e_) and e_[5] >= 0]
            seen_cv = set()
            conv_todo = []
            for e_ in conv_list:
                if e_[5] not in seen_cv:
                    seen_cv.add(e_[5])
                    conv_todo.append(e_)

            def rec_convs(n):
                for _ in range(n):
                    if not conv_todo:
                        return
                    wname, r0, c0, ncols, _m, idx = conv_todo.pop(0)
                    src = W[wname][r0:r0 + 1024, c0:c0 + ncols].rearrange("(kc p) c -> p kc c", p=128)
                    dst = wscr[idx].rearrange("p (k c) -> p k c", k=8)
                    P.op("pool", lambda e, dst=dst, src=src: e.dma_start(out=dst, in_=src), [], [("scr", idx)], dma=f"cv{idx}")

            P.op("sp", lambda e: e.dma_start(out=cst[:, :], in_=consts_d), [], ["cst"], dma="c0")
            P.op("sp", lambda e: e.dma_start(out=wg[0:17, :], in_=wga_d), [], ["wg"], dma="c1")
            P.op("sp", lambda e: e.dma_start(out=g1b[:, :], in_=bgb_d[:, 0:D]), [], ["g1b"], dma="c2")
            P.op("sp", lambda e: e.dma_start(out=g2b[:, :], in_=bgb_d[:, D:2 * D]), [], ["g2b"], dma="c3")
            P.op("sp", lambda e: e.dma_start(out=fnw[:, :], in_=fnwb_d), [], ["fnw"], dma="c4")

            P.op("pool", lambda e: e.memset(identf[:, :], 0.0), [], ["identf"])
            P.op("pool", lambda e: e.affine_select(out=identf[:, :], in_=identf[:, :], pattern=[[-1, 128]],
                                                   compare_op=ALU.not_equal, fill=1.0, base=0, channel_multiplier=1),
                 ["identf"], ["identf"])
            P.op("pool", lambda e: e.memset(cmask4[:, :], 1.0), [], ["cmask4"])
            P.op("pool", lambda e: e.affine_select(out=cmask4[:, :], in_=cmask4[:, :], pattern=[[0, 4], [1, 128]],
                                                   compare_op=ALU.is_ge, fill=0.0, base=0, channel_multiplier=-1),
                 ["cmask4"], ["cmask4"])
            P.op("pool", lambda e: e.memset(uneg[:, :], -1.0 / 16.0), [], ["uneg"])
            P.op("pool", lambda e: e.affine_select(out=uneg[:, :], in_=uneg[:, :], pattern=[[1, 128]],
                                                   compare_op=ALU.is_ge, fill=0.0, base=0, channel_multiplier=-1),
                 ["uneg"], ["uneg"])
            P.op("pool", lambda e: e.memset(lrT[:, :], 1.0), [], ["lrT"])
            P.op("pool", lambda e: e.dma_start(out=wlr[:, :, :], in_=W["w_in"][:, LR0:LR0 + 16].rearrange("(kc p) c -> p kc c", p=128)),
                 [], ["wlr"], dma="c7")
            if not P.dry:
                for j in range(min(4, len(P.wplan))):
                    rec_load(j)
            P.op("pool", lambda e: e.dma_start(out=wkp, in_=W["w_in"][:, K0:K0 + 512].rearrange("(kc p) c -> p kc c", p=128)),
                 [], ["wkp"], dma="c8")
            for n_ in range(2):
                P.op("pool", lambda e, n_=n_: e.dma_start(out=wvp[n_], in_=W["w_in"][:, V0 + n_ * 512:V0 + (n_ + 1) * 512].rearrange("(kc p) c -> p kc c", p=128)),
                     [], [f"wvp{n_}"], dma=f"c{9 + n_}")
            if not P.dry:
                for j in range(4, min(NW, len(P.wplan))):
                    rec_load(j)

            Vcopy(ident[:, :], identf[:, :], ["identf"], ["ident"])
            P.op("dve", lambda e: e.memset(ones_bf[:, :], 1.0), [], ["ones"])
            P.op("dve", lambda e: e.memset(S32[:, :, :], 0.0), [], [("S32", h) for h in range(4)])
            P.op("dve", lambda e: e.memset(S_bfs[0][:, :, :], 0.0), [], [("Sbf", 0)])
            P.op("dve", lambda e: e.memset(S_bfs[1][:, :, :], 0.0), [], [("Sbf", 1)])
            P.op("dve", lambda e: e.memset(uh[:, :, :], 0.0), [], [("uh", m) for m in range(8)])

            def rec_xload(g):
                src_t = x_prev if g < 4 else x_cur
                t0 = (g % 4) * T
                par = g % 2
                src = src_t[t0:t0 + T, :].rearrange("(s p) d -> p s d", p=128)
                P.op("sp", lambda e: e.dma_start(out=xbuf[par][:, :, :], in_=src), [],
                     [("x", par, s) for s in range(4)], dma=f"x{par}")

            rec_xload(0)
            rec_xload(1)

            cT = cst[:, C_CT:C_CT + 8]
            A(ce[:, :], cT, AF.Exp, ["cst"], ["ce"], scale=-1.0)
            Vsadd(ce[:, :], ce[:, :], 1.0, ["ce"], ["ce"])
            Vrecip(ce[:, :], ce[:, :], ["ce"], ["ce"])
            Vtt(cact[:, :], ce[:, :], cT, ALU.mult, ["ce", "cst"], ["cact"])
            Vcopy(cact_bf[:, :], cact[:, :], ["cact"], ["cactbf"])
            for kc in range(8):
                Vsmul(cbm[:, kc, :], ones_bf[:, :], cact[:, kc:kc + 1], ["ones", "cact"], [("cbm", kc)])

            def ada_chunk(ci):
                wt, wk = next_w(("w_ada", 0, ci * 512, 512))
                fm = {0: 0, 1: 0, 2: 1, 3: 1, 6: 2, 7: 2, 8: 3, 9: 3}
                if ci in fm:
                    j0 = fm[ci] * 8 + (ci % 2) * 4
                    p_, pk = bank()
                    mms = []
                    for jj in range(4):
                        for kc in range(8):
                            mms.append((p_[:, jj:jj + 1], wt[:, kc, jj * 128:(jj + 1) * 128], cact_bf[:, kc:kc + 1],
                                        kc == 0, kc == 7))
                    MM(mms, [wk, "cactbf"], [pk])
                    Vtt(modT[:, j0:j0 + 4], p_[:, 0:4], cst[:, C_BADA + j0:C_BADA + j0 + 4], ALU.add,
                        [pk, "cst"], [("modT", j0)])
                else:
                    gb_ = g1b if ci in (4, 5) else g2b
                    gk = "g1b" if ci in (4, 5) else "g2b"
                    hs_ = slice((ci % 2) * 512, (ci % 2) * 512 + 512)
                    p_, pk = bank()
                    MM([(p_[:, :], cbm[:, kc, :], wt[:, kc, :], kc == 0, kc == 7) for kc in range(8)],
                       [wk] + [("cbm", kc) for kc in range(8)], [pk])
                    Vtt(gb_[:, hs_], p_[:, :], gb_[:, hs_], ALU.add, [pk, gk], [gk])
                done_w()

            def mod_finish(which):
                sc0 = 8 if which == 1 else 24
                nw0 = C_N1 if which == 1 else C_N2
                dst = a1T if which == 1 else a2T
                key = "a1T" if which == 1 else "a2T"
                Vsadd(dst[:, :], modT[:, sc0:sc0 + 8], 1.0, [("modT", sc0), ("modT", sc0 + 4)], [key])
                Vtt(dst[:, :], dst[:, :], cst[:, nw0:nw0 + 8], ALU.mult, [key, "cst"], [key])

            def stage_norm(xb, par, aT_, akey, sh0):
                norm_p1(xb, par)
                norm_p2(aT_, akey, sh0)

            def norm_p1(xb, par):
                for s in range(4):
                    xk = ("x", par, s)
                    A(xn[:, s, :], xb[:, s, :], AF.Square, [xk], [("xn", s), ("ss", s)], accum_out=ss[:, s:s + 1])
                    A(lnv[:, s:s + 1], ss[:, s:s + 1], AF.Ln, [("ss", s)], [("lnv", s)], scale=1.0 / D, bias=EPS)
                    A(rstd[:, s:s + 1], lnv[:, s:s + 1], AF.Exp, [("lnv", s)], [("rstd", s)], scale=-0.5)
                    Vsmul(xn[:, s, :], xb[:, s, :], rstd[:, s:s + 1], [xk, ("rstd", s)], [("xn", s)])

            def norm_p2(aT_, akey, sh0):
                shkeys = [("modT", sh0), ("modT", sh0 + 4)]
                for kc in range(8):
                    pb, pbk = bbank()
                    TR([(pb[:, s * 128:(s + 1) * 128], xn[:, s, kc * 128:(kc + 1) * 128]) for s in range(4)],
                       [("xn", s) for s in range(4)], [pbk])
                    a_ap = aT_[:, kc:kc + 1]
                    s_ap = modT[:, sh0 + kc:sh0 + kc + 1]
                    if kc % 2 == 0:
                        Vts(hT[:, kc, :], pb[:, 0:512], a_ap, s_ap, ALU.mult, ALU.add, [pbk, akey] + shkeys, [("hT", kc)])
                    else:
                        A(hT[:, kc, :], pb[:, 0:512], AF.Identity, [pbk, akey] + shkeys, [("hT", kc)], scale=a_ap, bias=s_ap)

            def sigmoid(p_, pk):
                t, tk = tmp()
                A(t[:, :], p_[:, :], AF.Exp, [pk], [tk], scale=-1.0)
                A(t[:, :], t[:, :], AF.Ln, [tk], [tk], bias=1.0)
                A(t[:, :], t[:, :], AF.Exp, [tk], [tk], scale=-1.0)
                return t, tk

            flag_ap = cst[:, C_FLAG:C_FLAG + 1]

            def la_stage():
                la_p1()
                la_p2()

            def la_p1():
                p_, pk = bank()
                MM([(p_[0:16, :], wlr[:, kc, 0:16], hT[:, kc, :], kc == 0, kc == 7) for kc in range(8)],
                   ["wlr"] + HT_ALL, [pk])
                A(lrT[0:16, :], p_[0:16, :], AF.Copy, [pk], ["lrT"])

            def la_p2():
                for s in range(4):
                    p_, pk = bank()
                    MM([(p_[:, :], lrT[0:17, s * 128:(s + 1) * 128], wg[0:17, :], True, True)], ["lrT", "wg"], [pk])
                    t, tk = tmp()
                    A(t[:, :], p_[:, :], AF.Exp, [pk], [tk], scale=-1.0)
                    A(spb[:, s, :], t[:, :], AF.Ln, [tk], [("sp", s)], bias=1.0)

            def gla_stage(main, last_prev, special=False, hoist=None, nxt=None):
                if last_prev:
                    P.op("sp", lambda e: e.dma_start(out=w32q, in_=W["w_in"][:, Q0:Q0 + 512].rearrange("(kc p) c -> p kc c", p=128)),
                         [], ["w32q"] + [("x", 1, s_) for s_ in range(4)], dma="c5")
                    Vcopy(hTh[:, :, :], hT[:, :, 510:512], HT_ALL, ["hTh"])
                if main:
                    wq, wqk = next_w(("w_in", 0, Q0, 512))
                if main:
                    wk_, wkk = next_w(("w_in", 0, K0, 512))
                else:
                    wk_, wkk = wkp, "wkp"
                e2s = []
                for h in range(4):
                    hs = slice(h * 128, (h + 1) * 128)
                    pbb, pbbk = bank()
                    MM([(pbb[:, s * 128:(s + 1) * 128], spb[:, s, hs], uneg[:, :], True, True) for s in range(4)],
                       [("sp", s) for s in range(4)] + ["uneg"], [pbbk])
                    A(E1[:, h, :], pbb[:, :], AF.Exp, [pbbk], [("E1", h)])
                    e2, e2k = tmp()
                    A(e2[:, :], pbb[:, :], AF.Exp, [pbbk], [e2k], scale=-1.0)
                    e2s.append((e2, e2k))
                if (not main) and nxt is not None:
                    norm_p1(xbuf[nxt % 2], nxt % 2)

                def head_front(h):
                    hs = slice(h * 128, (h + 1) * 128)
                    e2, e2k = e2s[h]
                    if main:
                        pq, pqk = bank()
                        MM([(pq[:, :], wq[:, kc, hs], hT[:, kc, :], kc == 0, kc == 7) for kc in range(8)],
                           [wqk] + HT_ALL, [pqk])
                    pkk_, pkkk = bank()
                    MM([(pkk_[:, :], wk_[:, kc, hs], hT[:, kc, :], kc == 0, kc == 7) for kc in range(8)],
                       [wkk] + HT_ALL, [pkkk])
                    if special:
                        MM([(pq[:, 0:128], w32q[:, kc, hs], h32T[:, kc, :], kc == 0, kc == 7) for kc in range(8)],
                           ["w32q", "h32T", ("x", 1, 0)], [pqk])
                        MM([(pkk_[:, 0:128], w32k[:, kc, hs], h32T[:, kc, :], kc == 0, kc == 7) for kc in range(8)],
                           ["w32k", "h32T"], [pkkk])
                    if main:
                        Vstt(qdT[:, h, :], pq[:, :], 128.0 ** -0.5, E1[:, h, :], ALU.mult, ALU.mult,
                             [pqk, ("E1", h)], [("qdT", h)])
                    Vtt(kdT[:, h, :], pkk_[:, :], e2[:, :], ALU.mult, [pkkk, e2k], [("kdT", h)])
                    if special:
                        Vstt(qd32[:, h, :], pq[:, 0:128], 128.0 ** -0.5, E1[:, h, 0:128], ALU.mult, ALU.mult,
                             [pqk, ("E1", h)], [("qd32", h), "xn32"])
                        Vtt(kd32[:, h, :], pkk_[:, 0:128], e2[:, 0:128], ALU.mult, [pkkk, e2k], [("kd32", h), "xn32"])

                def head_back(h):
                    pb, pbk = bbank()
                    TR([(pb[:, s * 128:(s + 1) * 128], kdT[:, h, s * 128:(s + 1) * 128]) for s in range(4)], [("kdT", h)], [pbk])
                    Vcopy(ke[:, h, :], pb[:, 0:512], [pbk], [("ke", h)])

                for h in range(4):
                    head_front(h)
                    if h >= 1:
                        head_back(h - 1)
                head_back(3)
                if main:
                    done_w()
                    done_w()
                for n in range(2):
                    if main:
                        wv, wvk = next_w(("w_in", 0, V0 + n * 512, 512))
                    else:
                        wv, wvk = wvp[n], f"wvp{n}"
                    for s in range(4):
                        p_, pk = bank()
                        MM([(p_[:, :], hT[:, kc, s * 128:(s + 1) * 128], wv[:, kc, :], kc == 0, kc == 7) for kc in range(8)],
                           [wvk] + HT_ALL, [pk])
                        if s % 2 == 0:
                            Vcopy(v_sb[:, s, n * 512:(n + 1) * 512], p_[:, :], [pk], [("v", s)])
                        else:
                            A(v_sb[:, s, n * 512:(n + 1) * 512], p_[:, :], AF.Copy, [pk], [("v", s)])
                    if main:
                        done_w()
                if last_prev:
                    P.op("sp", lambda e: e.dma_start(out=w32k, in_=W["w_in"][:, K0:K0 + 512].rearrange("(kc p) c -> p kc c", p=128)),
                         [], ["w32k"], dma="c6")
                if (not main) and nxt is not None:
                    norm_p2(a1T, "a1T", 0)
                    la_p1()
                if main:
                    for n in range(2):
                        wgg, wggk = next_w(("w_in", 0, G0 + n * 512, 512))
                        for s in range(4):
                            p_, pk = bank()
                            MM([(p_[:, :], hT[:, kc, s * 128:(s + 1) * 128], wgg[:, kc, :], kc == 0, kc == 7) for kc in range(8)],
                               [wggk] + HT_ALL, [pk])
                            r_, rk = sigmoid(p_, pk)
                            Vtt(sg[:, s, n * 512:(n + 1) * 512], p_[:, :], r_[:, :], ALU.mult, [pk, rk],
                                [("sg", s, 2 * n), ("sg", s, 2 * n + 1)])
                        done_w()
                def phaseA(s):
                    cs = slice(s * 128, (s + 1) * 128)
                    st = {}
                    if main:
                        psc, psck = bank()
                        if special and s == 0:
                            MM([(psc[:, h * 128:(h + 1) * 128], kd32[:, h, :], qd32[:, h, :], True, True) for h in range(4)],
                               [("kd32", h) for h in range(4)] + [("qd32", h) for h in range(4)], [psck])
                        else:
                            MM([(psc[:, h * 128:(h + 1) * 128], kdT[:, h, cs], qdT[:, h, cs], True, True) for h in range(4)],
                               [("kdT", h) for h in range(4)] + [("qdT", h) for h in range(4)], [psck])
                        sc, sck = scbuf()
                        Vtt(sc[:, :], psc[:, :], cmask4[:, :], ALU.mult, [psck, "cmask4"], [sck])
                        st["sc"] = (sc, sck)
                    pts = []
                    for hp in range(2):
                        pt, ptk = bank()
                        MM([(pt[:, j * 256:(j + 1) * 256], ke[:, hp * 2 + j, cs],
                             v_sb[:, s, (hp * 2 + j) * 256:(hp * 2 + j + 1) * 256], True, True) for j in range(2)],
                           [("ke", hp * 2), ("ke", hp * 2 + 1), ("v", s)], [ptk])
                        pts.append((pt, ptk))
                    st["pts"] = pts
                    return st

                def supd(s, st):
                    for h in range(4):
                        pt, ptk = st["pts"][h // 2]
                        Vtt(S32[:, h, :], S32[:, h, :], pt[:, (h % 2) * 256:(h % 2 + 1) * 256], ALU.add,
                            [("S32", h), ptk], [("S32", h)])
                        Vsmul(S32[:, h, :], S32[:, h, :], E1[:, h, s * 128 + 127:s * 128 + 128],
                              [("S32", h), ("E1", h)], [("S32", h)])

                def o_and_act(s, st):
                    cs = slice(s * 128, (s + 1) * 128)
                    pos = []
                    if main:
                        sc, sck = st["sc"]
                        Sb = S_bfs[s % 2]
                        for hp in range(2):
                            po, pok = bank()
                            mms = []
                            for j in range(2):
                                h = hp * 2 + j
                                mms.append((po[:, j * 256:(j + 1) * 256], sc[:, h * 128:(h + 1) * 128],
                                            v_sb[:, s, h * 256:(h + 1) * 256], True, False))
                                mms.append((po[:, j * 256:(j + 1) * 256], qdT[:, h, cs], Sb[:, h, :], False, True))
                            MM(mms, [sck, ("v", s), ("qdT", hp * 2), ("qdT", hp * 2 + 1), ("Sbf", s % 2)], [pok])
                            pos.append((po, pok))
                    if main:
                        A(S_bfs[(s + 1) % 2][:, :, :], S32[:, :, :], AF.Copy, [("S32", h) for h in range(4)], [("Sbf", (s + 1) % 2)])
                    if main:
                        for h in range(4):
                            po, pok = pos[h // 2]
                            i = s * 4 + h
                            jt, jtk = tmp()
                            A(jt[:, :].bitcast(BF16)[:, 0:256], po[:, (h % 2) * 256:(h % 2 + 1) * 256], AF.Square, [pok],
                              [("sso", i), jtk], accum_out=sso[:, i:i + 1])
                        A(lno[:, s * 4:(s + 1) * 4], sso[:, s * 4:(s + 1) * 4], AF.Ln, [("sso", s * 4 + h) for h in range(4)],
                          [("lno", s)], scale=1.0 / 256, bias=EPS)
                        A(rso[:, s * 4:(s + 1) * 4], lno[:, s * 4:(s + 1) * 4], AF.Exp, [("lno", s)], [("rso", s)], scale=-0.5)
                    return pos

                def og_stage(s, pos):
                    if main:
                        for h in range(4):
                            po, pok = pos[h // 2]
                            i = s * 4 + h
                            sgs = sg[:, s, h * 256:(h + 1) * 256]
                            Vstt(sgs, po[:, (h % 2) * 256:(h % 2 + 1) * 256], rso[:, i:i + 1], sgs, ALU.mult, ALU.mult,
                                 [pok, ("rso", s), ("sg", s, h)], [("sg", s, h)])

                sts = {0: phaseA(0)}
                pos_prev = None
                for s in range(4):
                    supd(s, sts[s])
                    if s + 1 < 4:
                        sts[s + 1] = phaseA(s + 1)
                    pos = o_and_act(s, sts[s])
                    if pos_prev is not None:
                        og_stage(s - 1, pos_prev)
                    pos_prev = pos
                og_stage(3, pos_prev)
                if last_prev:
                    for h in range(4):
                        Vsmul(S32[:, h, :], S32[:, h, :], flag_ap, [("S32", h), "cst"], [("S32", h)])
                    A(S_bfs[0][:, :, :], S32[:, :, :], AF.Copy, [("S32", h) for h in range(4)], [("Sbf", 0)])
                if not main:
                    if nxt is not None:
                        la_p2()
                    return
                for kc in range(8):
                    pb, pbk = bbank()
                    TR([(pb[:, s * 128:(s + 1) * 128], sg[:, s, kc * 128:(kc + 1) * 128]) for s in range(4)],
                       [("sg", s, kc // 2) for s in range(4)], [pbk])
                    g_ap = cst[:, C_GNW + kc % 2:C_GNW + kc % 2 + 1]
                    if kc % 2:
                        A(ogT[:, kc, :], pb[:, 0:512], AF.Copy, [pbk, "cst"], [("ogT", kc)], scale=g_ap)
                    else:
                        Vsmul(ogT[:, kc, :], pb[:, 0:512], g_ap, [pbk, "cst"], [("ogT", kc)])

            def cw_ap(m, j):
                c = C_CW + m * 3 + j
                return cst[:, c:c + 1]

            def conv_stage(first=False):
                for mg in range(2):
                    wcb, wcbk = next_w(("w_in", 0, CB0 + mg * 512, 512))
                    wcc, wcck = next_w(("w_in", 0, CC0 + mg * 512, 512))
                    wcx, wcxk = next_w(("w_in", 0, CX0 + mg * 512, 512))
                    for mm in range(4):
                        m = mg * 4 + mm
                        ms = slice(mm * 128, (mm + 1) * 128)
                        pc, pck = bank()
                        MM([(pc[:, :], wcc[:, kc, ms], hT[:, kc, :], kc == 0, kc == 7) for kc in range(8)], [wcck] + HT_ALL, [pck])
                        px, pxk = bank()
                        MM([(px[:, :], wcx[:, kc, ms], hT[:, kc, :], kc == 0, kc == 7) for kc in range(8)], [wcxk] + HT_ALL, [pxk])
                        pcb, pcbk = bank()
                        MM([(pcb[:, :], wcb[:, kc, ms], hT[:, kc, :], kc == 0, kc == 7) for kc in range(8)], [wcbk] + HT_ALL, [pcbk])
                        if first:
                            ph, phk = bank()
                            MM([(ph[:, 0:2], wcc[:, kc, ms], hTh[:, kc, :], kc == 0, kc == 7) for kc in range(8)]
                               + [(ph[:, 2:4], wcx[:, kc, ms], hTh[:, kc, :], kc == 0, kc == 7) for kc in range(8)],
                               [wcck, wcxk, "hTh"], [phk])
                            th, thk = tmp()
                            A(th[:, 0:2], ph[:, 0:2], AF.Copy, [phk], [thk])
                            Vtt(th[:, 2:4], th[:, 0:2], ph[:, 2:4], ALU.mult, [thk, phk], [thk])
                            Vsmul(uh[:, m, :], th[:, 2:4], flag_ap, [thk, "cst"], [("uh", m)])
                        t, tk = tmp()
                        A(t[:, :], pc[:, :], AF.Copy, [pck], [tk])
                        u, uk = ubuf()
                        A(u[:, 0:2], uh[:, m, :], AF.Copy, [("uh", m)], [uk])
                        Vtt(u[:, 2:514], t[:, :], px[:, :], ALU.mult, [tk, pxk, uk], [uk])
                        A(uh[:, m, :], u[:, 512:514], AF.Copy, [uk], [("uh", m)])
                        a_, ak = tmp()
                        Vsmul(a_[:, :], u[:, 2:514], cw_ap(m, 2), [uk, "cst"], [ak])
                        Vstt(a_[:, :], u[:, 1:513], cw_ap(m, 1), a_[:, :], ALU.mult, ALU.add, [uk, "cst", ak], [ak])
                        Vstt(a_[:, :], u[:, 0:512], cw_ap(m, 0), a_[:, :], ALU.mult, ALU.add, [uk, "cst", ak], [ak])
                        Vtt(cbuT[:, m, :], a_[:, :], pcb[:, :], ALU.mult, [ak, pcbk], [("cbuT", m)])
                    done_w()
                    done_w()
                    done_w()

            def merge_stage(xb, par):
                OGT_ALL = [("ogT", kc) for kc in range(8)]
                CBU_ALL = [("cbuT", kc) for kc in range(8)]
                for mg in range(2):
                    wa, wak = next_w(("w_pa", 0, mg * 512, 512))
                    wga_, wgak = next_w(("w_in", 0, GA0 + mg * 512, 512))
                    wb, wbk = next_w(("w_pb", 0, mg * 512, 512))
                    wgb, wgbk = next_w(("w_in", 0, GB0 + mg * 512, 512))
                    for mm in range(4):
                        m = mg * 4 + mm
                        ms = slice(mm * 128, (mm + 1) * 128)
                        pya, pyak = bank()
                        MM([(pya[:, :], wa[:, kc, ms], ogT[:, kc, :], kc == 0, kc == 7) for kc in range(8)], [wak] + OGT_ALL, [pyak])
                        pga, pgak = bank()
                        MM([(pga[:, :], wga_[:, kc, ms], hT[:, kc, :], kc == 0, kc == 7) for kc in range(8)], [wgak] + HT_ALL, [pgak])
                        ra, rak = sigmoid(pga, pgak)
                        Vtt(ra[:, :], ra[:, :], pya[:, :], ALU.mult, [rak, pyak], [rak])
                        pyb, pybk = bank()
                        MM([(pyb[:, :], wb[:, kc, ms], cbuT[:, kc, :], kc == 0, kc == 7) for kc in range(8)], [wbk] + CBU_ALL, [pybk])
                        pgb, pgbk = bank()
                        MM([(pgb[:, :], wgb[:, kc, ms], hT[:, kc, :], kc == 0, kc == 7) for kc in range(8)], [wgbk] + HT_ALL, [pgbk])
                        rb, rbk = sigmoid(pgb, pgbk)
                        Vtt(rb[:, :], rb[:, :], pyb[:, :], ALU.mult, [rbk, pybk], [rbk])
                        Vtt(zT[:, m, :], ra[:, :], rb[:, :], ALU.add, [rak, rbk], [("zT", m)])
                    for _ in range(4):
                        done_w()
                ZT_ALL = [("zT", kc) for kc in range(8)]
                for n in range(2):
                    wo, wok = next_w(("w_o", 0, n * 512, 512))
                    ns = slice(n * 512, (n + 1) * 512)
                    for s in range(4):
                        p_, pk = bank()
                        MM([(p_[:, :], zT[:, kc, s * 128:(s + 1) * 128], wo[:, kc, :], kc == 0, kc == 7) for kc in range(8)],
                           [wok] + ZT_ALL, [pk])
                        t, tk = tmp()
                        Vtt(t[:, :], p_[:, :], g1b[:, ns], ALU.mult, [pk, "g1b"], [tk])
                        xk = ("x", par, s)
                        Vtt(xb[:, s, ns], xb[:, s, ns], t[:, :], ALU.add, [xk, tk], [xk])
                    done_w()

            def mlp_stage(xb, par, g, hoist=None, nxt=None):
                stage_norm(xb, par, a2T, "a2T", 16)
                if nxt is not None:
                    norm_p1(xbuf[nxt % 2], nxt % 2)
                for cg in range(8):
                    w1_, w1k = next_w(("w1", 0, cg * 512, 512))
                    for jj in range(4):
                        j = cg * 4 + jj
                        p_, pk = bank()
                        MM([(p_[:, :], w1_[:, kc, jj * 128:(jj + 1) * 128], hT[:, kc, :], kc == 0, kc == 7) for kc in range(8)],
                           [w1k] + HT_ALL, [pk])
                        t, tk = tmp()
                        A(t[:, :], p_[:, :], AF.Relu, [pk], [tk])
                        Vtt(aT[:, j, :], t[:, :], t[:, :], ALU.mult, [tk], [("aT", j)])
                    done_w()
                for n in range(2):
                    ns = slice(n * 512, (n + 1) * 512)
                    bks = [bank() for _ in range(4)]
                    for jg in range(4):
                        w2_, w2k = next_w(("w2", jg * 1024, n * 512, 512))
                        for s in range(4):
                            MM([(bks[s][0][:, :], aT[:, jg * 8 + jj, s * 128:(s + 1) * 128], w2_[:, jj, :],
                                 jg == 0 and jj == 0, jg == 3 and jj == 7) for jj in range(8)],
                               [w2k] + [("aT", jg * 8 + jj) for jj in range(8)], [bks[s][1]])
                        done_w()
                    for s in range(4):
                        t, tk = tmp()
                        Vtt(t[:, :], bks[s][0][:, :], g2b[:, ns], ALU.mult, [bks[s][1], "g2b"], [tk])
                        xk = ("x", par, s)
                        Vtt(xb[:, s, ns], xb[:, s, ns], t[:, :], ALU.add, [xk, tk], [xk])
                    if n == 0 and nxt is not None:
                        norm_p2(a1T, "a1T", 0)
                        la_p1()
                        la_p2()
                for s in range(4):
                    xk = ("x", par, s)
                    jt, jtk = tmp()
                    A(jt[:, :].bitcast(BF16), xb[:, s, :], AF.Square, [xk], [jtk, ("ss2", s)], accum_out=ss2[:, s:s + 1])
                    A(lnv2[:, s:s + 1], ss2[:, s:s + 1], AF.Ln, [("ss2", s)], [("lnv2", s)], scale=1.0 / D, bias=EPS)
                    A(rstd2[:, s:s + 1], lnv2[:, s:s + 1], AF.Exp, [("lnv2", s)], [("rstd2", s)], scale=-0.5)
                    A(xb[:, s, :], xb[:, s, :], AF.Copy, [xk, ("rstd2", s)], [xk], scale=rstd2[:, s:s + 1])
                    Vtt(xb[:, s, :], xb[:, s, :], fnw[:, :], ALU.mult, [xk, "fnw"], [xk])
                    r0 = (g % 4) * T + s * 128
                    P.op("sp", lambda e, r0=r0, s=s: e.dma_start(out=out_d[r0:r0 + 128, :], in_=xb[:, s, :]), [xk],
                         [("out", g, s)], dma=f"o{par}")

            def special_prep(xb, par):
                Vsmul(xn32, xb[:, 0, :], rstd[:, 0:1], [("x", par, 0), ("rstd", 0)], ["xn32"])
                for half in range(2):
                    p_, pk = bank()
                    def fn(e, p_=p_, half=half):
                        ins = None
                        for j in range(4):
                            kc = half * 4 + j
                            ins = e.transpose(out=p_[:, j * 128:(j + 1) * 128], in_=xn32[:, kc * 128:(kc + 1) * 128],
                                              identity=identf[:, :])
                        return ins
                    P.op("pe", fn, ["xn32", "identf"], [pk])
                    for j in range(4):
                        kc = half * 4 + j
                        Vts(h32T[:, kc, :], p_[:, j * 128:(j + 1) * 128], a1T[:, kc:kc + 1], modT[:, kc:kc + 1],
                            ALU.mult, ALU.add, [pk, "a1T", ("modT", 0), ("modT", 4)], ["h32T"])

            for ci in range(4):
                ada_chunk(ci)
            mod_finish(1)
            rest = [4, 5, 6, 7, 8, 9, 10, 11]
            for g in range(8):
                par = g % 2
                xb = xbuf[par]
                if g >= 4:
                    P.wmode = "t4" if g == 4 else "t5" if g == 5 else "scr"
                    P.widx = 0
                if g == 0:
                    stage_norm(xb, par, a1T, "a1T", 0)
                    la_stage()
                nxt = g + 1 if g + 1 < 8 else None
                if g < 3:
                    rec_xload(g + 2)
                if g < 4:
                    gla_stage(False, g == 3, nxt=nxt)
                    for ci in rest[g * 2:g * 2 + 2]:
                        ada_chunk(ci)
                    rec_convs(2 if g < 3 else 100)
                    if g == 3:
                        mod_finish(2)
                else:
                    if g == 4:
                        special_prep(xb, par)
                    gla_stage(True, False, special=(g == 4))
                    if g == 4:
                        rec_xload(5)
                    conv_stage(first=(g == 4))
                    merge_stage(xb, par)
                    mlp_stage(xb, par, g, nxt=nxt)
                if g + 2 < 8 and g >= 4:
                    rec_xload(g + 2)
            P.op("sp", None, [("out", g, s_) for g in range(4, 8) for s_ in range(4)], [])

        wplan = []
        record(Prog(True, wplan))
        P = Prog(False, wplan)
        record(P)
        assert P.wi == len(wplan), (P.wi, len(wplan))

        sem_names = P.sems()
        S = {n: es.enter_context(nc.semaphore(n)) for n in sem_names}
        block = es.enter_context(nc.Block())

        def run_stream(eng_name):
            def body(e):
                for (waits, fn, sem, amt) in P.streams[eng_name]:
                    for (s_, v_) in waits:
                        e.wait_ge(S[s_], v_)
                    if fn is not None:
                        fn(e).then_inc(S[sem], amt)
            return body

        block.sync(run_stream("sp"))
        block.gpsimd(run_stream("pool"))
        block.tensor(run_stream("pe"))
        block.scalar(run_stream("act"))
        block.vector(run_stream("dve"))
    return nc


_NC = None


def kernel(x, c, w_ada, b_ada, norm1_w, w_in, w_gate_up, b_gate, gla_norm_w, conv_w,
           w_proj_a, w_proj_b, w_out, norm2_w, w_mlp1, w_mlp2, final_norm_w):
    global _NC
    f = lambda a: np.ascontiguousarray(np.asarray(a, dtype=np.float32))
    x = f(x)
    c = f(c)
    b_ada = f(b_ada)[0]
    shared = {
        "w_ada": f(w_ada)[0], "w_in": f(w_in)[0], "w_pa": f(w_proj_a)[0], "w_pb": f(w_proj_b)[0],
        "w_o": f(w_out)[0], "w1": f(w_mlp1)[0], "w2": f(w_mlp2)[0],
        "fnw_b": np.ascontiguousarray(np.broadcast_to(f(final_norm_w)[None, :], (128, D))),
        "bgate_b": np.ascontiguousarray(np.broadcast_to(
            np.concatenate([b_ada[2 * D:3 * D], b_ada[5 * D:6 * D]])[None, :], (128, 2 * D))),
        "wg_aug": np.ascontiguousarray(np.concatenate([f(w_gate_up)[0], f(b_gate)[0][None, :]], axis=0)),
    }
    colT = lambda v: np.ascontiguousarray(v.reshape(-1, 128).T)
    cbase = np.zeros((128, NCONST), np.float32)
    cbase[:, C_BADA:C_BADA + 32] = np.concatenate(
        [colT(b_ada[0:D]), colT(b_ada[D:2 * D]), colT(b_ada[3 * D:4 * D]), colT(b_ada[4 * D:5 * D])], axis=1)
    cbase[:, C_N1:C_N1 + 8] = colT(f(norm1_w)[0])
    cbase[:, C_N2:C_N2 + 8] = colT(f(norm2_w)[0])
    cwl = f(conv_w)[0]
    cbase[:, C_CW:C_CW + 24] = np.transpose(cwl.reshape(3, 8, 128), (2, 1, 0)).reshape(128, 24)
    cbase[:, C_GNW:C_GNW + 2] = colT(f(gla_norm_w)[0])
    if _NC is None:
        _NC = build_nc()
    in_maps = []
    for i in range(8):
        b, hf = i // 2, i % 2
        cs = cbase.copy()
        cs[:, C_CT:C_CT + 8] = colT(c[b])
        cs[:, C_FLAG] = float(hf)
        m = dict(shared)
        m["x_cur"] = np.ascontiguousarray(x[b, hf * TOK:(hf + 1) * TOK])
        m["x_prev"] = np.ascontiguousarray(x[b, 0:TOK])
        m["consts"] = cs
        in_maps.append(m)
    res = run_bass_kernel_spmd(_NC, in_maps, core_ids=list(range(8)))
    out = np.empty((4, 2 * TOK, D), np.float32)
    for i in range(8):
        b, hf = i // 2, i % 2
        out[b, hf * TOK:(hf + 1) * TOK] = np.asarray(res.results[i]["out"]).reshape(TOK, D)
    return out
```

```python
import numpy as np
from contextlib import ExitStack
import concourse.bass as bass
import concourse.mybir as mybir
from concourse.bass_utils import run_bass_kernel_spmd

F32 = mybir.dt.float32
BF16 = mybir.dt.bfloat16
AF = mybir.ActivationFunctionType
ALU = mybir.AluOpType

D = 1024
TOK = 2048
T = 512
NW = 6
NTMP = 8
NSCR = 40
PRECONV = ("w2",)
EPS = 1e-6
Q0, K0, V0, G0, LR0, CB0, CC0, CX0, GA0, GB0 = 0, 512, 1024, 2048, 3072, 3088, 4112, 5136, 6160, 7184
C_CT, C_BADA, C_N1, C_N2, C_CW, C_GNW, C_FLAG, NCONST = 0, 8, 40, 48, 56, 80, 82, 84


class Ev:
    __slots__ = ("eng", "sem", "val", "know", "dma")

    def __init__(self, eng, sem, val, know, dma):
        self.eng, self.sem, self.val, self.know, self.dma = eng, sem, val, know, dma


class Prog:
    ENGS = ("pe", "act", "dve", "pool", "sp")

    def __init__(self, dry, wplan):
        self.dry = dry
        self.wplan = wplan
        self.streams = {e: [] for e in self.ENGS}
        self.cnt = {e: 0 for e in self.ENGS}
        self.dcnt = {}
        self.know = {e: {} for e in self.ENGS}
        self.last_w = {}
        self.readers = {}
        self.groups = {}
        self.cur_view = {}
        self.barrier = {}
        self.name_keys = {}
        self.bi = 0
        self.bbi = 0
        self.ti = 0
        self.wi = 0
        self.wdone = 0
        self.wmode = "cast"
        self.widx = -1

    def op(self, eng, fn, reads=(), writes=(), dma=None):
        if self.dry:
            return
        deps = []
        for k in list(reads) + list(writes):
            name = k[0] if isinstance(k, tuple) else k
            self.name_keys.setdefault(name, set()).add(k)
            for g in self.groups.get(name, ()):
                if self.cur_view.get(g) != name:
                    old = self.cur_view.get(g)
                    evs = []
                    if old is not None:
                        for kk in self.name_keys.get(old, ()):
                            if kk in self.last_w:
                                evs.append(self.last_w[kk])
                            evs += self.readers.get(kk, [])
                    self.barrier[g] = evs
                    self.cur_view[g] = name
                deps += self.barrier.get(g, [])
        is_dma = dma is not None
        for k in reads:
            ev = self.last_w.get(k)
            if ev is not None:
                deps.append(ev)
        for k in writes:
            ev = self.last_w.get(k)
            if ev is not None and (is_dma or ev.dma or ev.eng != eng or eng != "pe"):
                deps.append(ev)
            for ev in self.readers.get(k, []):
                if is_dma or ev.dma or ev.eng != eng or eng != "pe":
                    deps.append(ev)
        kn = self.know[eng]
        waits = {}
        for ev in deps:
            if kn.get(ev.sem, 0) >= ev.val:
                continue
            waits[ev.sem] = max(waits.get(ev.sem, 0), ev.val)
            for s_, v_ in ev.know.items():
                if kn.get(s_, 0) < v_:
                    kn[s_] = v_
        if fn is None:
            self.streams[eng].append((list(waits.items()), None, None, 0))
            return
        if is_dma:
            sem = dma
            self.dcnt[sem] = self.dcnt.get(sem, 0) + 16
            val = self.dcnt[sem]
            amt = 16
        else:
            sem = "E_" + eng
            self.cnt[eng] += 1
            val = self.cnt[eng]
            amt = 1
        evk = dict(kn)
        evk[sem] = val
        ev = Ev(eng, sem, val, evk, is_dma)
        for k in reads:
            self.readers.setdefault(k, []).append(ev)
        for k in writes:
            self.last_w[k] = ev
            self.readers[k] = []
        self.streams[eng].append((list(waits.items()), fn, sem, amt))

    def sems(self):
        s = {"E_" + e for e in self.ENGS}
        s |= set(self.dcnt.keys())
        return sorted(s)


def build_nc():
    nc = bass.Bass("TRN2", target_bir_lowering=False)

    def din(name, shape):
        return nc.dram_tensor(name, shape, F32, kind="ExternalInput").ap()

    x_cur = din("x_cur", [TOK, D])
    x_prev = din("x_prev", [TOK, D])
    W = {
        "w_ada": din("w_ada", [D, 6 * D]),
        "w_in": din("w_in", [D, 8208]),
        "w_pa": din("w_pa", [D, D]),
        "w_pb": din("w_pb", [D, D]),
        "w_o": din("w_o", [D, D]),
        "w1": din("w1", [D, 4 * D]),
        "w2": din("w2", [4 * D, D]),
    }
    consts_d = din("consts", [128, NCONST])
    fnwb_d = din("fnw_b", [128, D])
    bgb_d = din("bgate_b", [128, 2 * D])
    wga_d = din("wg_aug", [17, 512])
    out_d = nc.dram_tensor("out", [TOK, D], F32, kind="ExternalOutput").ap()
    wscr = nc.dram_tensor("wscr", [NSCR, 128, 4096], BF16, kind="Internal").ap()

    with ExitStack() as es:
        def sb(name, shape, dt):
            return es.enter_context(nc.sbuf_tensor(name, shape, dt))

        xbuf = [sb(f"xbuf{i}", [128, 4, D], F32) for i in range(2)]
        xn = sb("xn", [128, 4, D], BF16)
        hT = sb("hT", [128, 8, T], BF16)
        wsl = [sb(f"wsl{i}", [128, 8, 512], BF16) for i in range(NW)]
        wlr = sb("wlr", [128, 8, 16], BF16)
        lrT = sb("lrT", [32, T], F32)
        wg = sb("wg", [32, 512], F32)
        spb = sb("spb", [128, 4, 512], F32)
        E1 = spb
        spc = sb("spc", [128, 4, 512], F32)
        qdT = sb("qdT", [128, 4, T], BF16)
        kdT = sb("kdT", [128, 4, T], BF16)
        ke = sb("ke", [128, 4, T], BF16)
        big = sb("big", [128, 16384], BF16)
        S32 = sb("S32", [128, 4, 256], F32)
        S_bfs = [sb(f"S_bf{i}", [128, 4, 256], BF16) for i in range(2)]
        scb = [sb(f"scb{i}", [128, 512], BF16) for i in range(2)]
        cmask4 = sb("cmask4", [128, 512], BF16)
        ss2 = sb("ss2", [128, 4], F32)
        lnv2 = sb("lnv2", [128, 4], F32)
        rstd2 = sb("rstd2", [128, 4], F32)
        ub = [sb(f"ub{i}", [128, 514], F32) for i in range(2)]
        uh = sb("uh", [128, 8, 2], F32)
        hTh = sb("hTh", [128, 8, 2], BF16)
        tmps = [sb(f"tmp{i}", [128, 512], F32) for i in range(NTMP)]
        cst = sb("cst", [128, NCONST], F32)
        fnw = sb("fnw", [128, D], F32)
        g1b = sb("g1b", [128, D], F32)
        g2b = sb("g2b", [128, D], F32)
        identf = sb("identf", [128, 128], F32)
        ident = sb("ident", [128, 128], BF16)
        ones_bf = sb("ones_bf", [128, 128], BF16)
        uneg = sb("uneg", [128, 128], F32)
        ce = sb("ce", [128, 8], F32)
        cact = sb("cact", [128, 8], F32)
        cact_bf = sb("cact_bf", [128, 8], BF16)
        modT = sb("modT", [128, 32], F32)
        a1T = sb("a1T", [128, 8], F32)
        a2T = sb("a2T", [128, 8], F32)
        ss = sb("ss", [128, 4], F32)
        lnv = sb("lnv", [128, 4], F32)
        rstd = sb("rstd", [128, 4], F32)
        sso = sb("sso", [128, 16], F32)
        lno = sb("lno", [128, 16], F32)
        rso = sb("rso", [128, 16], F32)

        psf = [es.enter_context(nc.psum_tensor(f"psf{i}", [128, 512], F32)) for i in range(8)]
        psb = [p_[:, :].bitcast(BF16) for p_ in psf]

        v_sb = big[:, 0:4096].rearrange("p (s c) -> p s c", s=4)
        ogT = big[:, 0:4096].rearrange("p (k c) -> p k c", k=8)
        sg = big[:, 4096:8192].rearrange("p (s c) -> p s c", s=4)
        cbuT = big[:, 8192:12288].rearrange("p (k c) -> p k c", k=8)
        zT = big[:, 12288:16384].rearrange("p (k c) -> p k c", k=8)
        aT = big[:, :].rearrange("p (j c) -> p j c", j=32)
        wkp = big[:, 4096:8192].rearrange("p (k c) -> p k c", k=8)
        wvp = [big[:, 8192:12288].rearrange("p (k c) -> p k c", k=8),
               big[:, 12288:16384].rearrange("p (k c) -> p k c", k=8)]
        cbm = qdT[:, 0:2, :].rearrange("p a (k c) -> p (a k) c", k=4)
        w32q = xbuf[1][:, :, :].rearrange("p s (a c) -> p (s a) c", a=2)
        w32k = big[:, 8192:16384].bitcast(F32).rearrange("p (k c) -> p k c", k=8)
        h32T = spc[:, 0:2, :].rearrange("p a (k c) -> p (a k) c", k=4)
        xn32 = spc[:, 2:4, :].rearrange("p a c -> p (a c)")
        qd32 = spc[:, 2, :].rearrange("p (h c) -> p h c", h=4)
        kd32 = spc[:, 3, :].rearrange("p (h c) -> p h c", h=4)

        def record(P):
            P.groups = {"v": ["A"], "ogT": ["A"], "sg": ["B"], "cbuT": ["C"], "zT": ["Dg"],
                        "aT": ["A", "B", "C", "Dg"], "w32k": ["C", "Dg"],
                        "wkp": ["B"], "wvp0": ["C"], "wvp1": ["Dg"],
                        "cbm": ["Q"], "qdT": ["Q"], "sp": ["SE"], "E1": ["SE"]}

            def A(out, in_, func, r, w, **kw):
                P.op("act", lambda e: e.activation(out=out, in_=in_, func=func, **kw), r, w)

            def Vtt(out, a, b, op, r, w):
                P.op("dve", lambda e: e.tensor_tensor(out=out, in0=a, in1=b, op=op), r, w)

            def Vts(out, a, s1, s2, op0, op1, r, w):
                P.op("dve", lambda e: e.tensor_scalar(out=out, in0=a, scalar1=s1, scalar2=s2, op0=op0, op1=op1), r, w)

            def Vsmul(out, a, s, r, w):
                P.op("dve", lambda e: e.tensor_scalar_mul(out=out, in0=a, scalar1=s), r, w)

            def Vsadd(out, a, s, r, w):
                P.op("dve", lambda e: e.tensor_scalar_add(out=out, in0=a, scalar1=s), r, w)

            def Vstt(out, a, s, b, op0, op1, r, w):
                P.op("dve", lambda e: e.scalar_tensor_tensor(out=out, in0=a, scalar=s, in1=b, op0=op0, op1=op1), r, w)

            def Vcopy(out, a, r, w):
                P.op("dve", lambda e: e.tensor_copy(out=out, in_=a), r, w)

            def Vrecip(out, a, r, w):
                P.op("dve", lambda e: e.reciprocal(out=out, in_=a), r, w)

            def MM(mms, r, w):
                def fn(e):
                    ins = None
                    for (o, l, rh, st, sp_) in mms:
                        ins = e.matmul(out=o, lhsT=l, rhs=rh, start=st, stop=sp_)
                    return ins
                P.op("pe", fn, r, w)

            def TR(trs, r, w):
                def fn(e):
                    ins = None
                    for (o, i) in trs:
                        ins = e.transpose(out=o, in_=i, identity=ident[:])
                    return ins
                P.op("pe", fn, list(r) + ["ident"], w)

            def bank():
                i = P.bi % 8
                P.bi += 1
                return psf[i], ("ps", i)

            def bbank():
                i = P.bi % 8
                P.bi += 1
                return psb[i], ("ps", i)

            def tmp():
                i = P.ti % NTMP
                P.ti += 1
                return tmps[i], ("tmp", i)

            sci = [0]

            def scbuf():
                i = sci[0] % 2
                sci[0] += 1
                return scb[i], ("sc", i)

            ubi = [0]

            def ubuf():
                i = ubi[0] % 2
                ubi[0] += 1
                return ub[i], ("u", i)

            def rec_load(j):
                wname, r0, c0, ncols, mode, idx = P.wplan[j]
                slot = j % NW
                if mode == "scr":
                    src = wscr[idx]
                    dst = wsl[slot][:, :, :].rearrange("p k c -> p (k c)")
                    P.op("pool", lambda e: e.dma_start(out=dst, in_=src), [("scr", idx)], [("w", slot)], dma=f"w{slot}")
                else:
                    src = W[wname][r0:r0 + 1024, c0:c0 + ncols].rearrange("(kc p) c -> p kc c", p=128)
                    dst = wsl[slot][:, :, 0:ncols]
                    P.op("pool", lambda e: e.dma_start(out=dst, in_=src), [], [("w", slot)], dma=f"w{slot}")
                if mode == "castwb":
                    wsrc = wsl[slot][:, :, :].rearrange("p k c -> p (k c)")
                    wdst = wscr[idx]
                    P.op("sp", lambda e: e.dma_start(out=wdst, in_=wsrc), [("w", slot)], [("scr", idx)], dma=f"wb{slot}")

            def is_preconv(spec):
                return spec[0] in PRECONV

            def next_w(spec):
                mode = P.wmode
                if mode in ("t4", "t5"):
                    if is_preconv(spec):
                        mode = "scr"
                    elif P.widx % 2 == 0:
                        mode = "castwb" if mode == "t4" else "scr"
                    else:
                        mode = "cast4" if mode == "t4" else "castwb"
                full = tuple(spec) + (mode, P.widx)
                if P.wmode != "cast":
                    P.widx += 1
                if P.dry:
                    P.wplan.append(full)
                    return wsl[0], ("w", 0)
                i = P.wi
                P.wi += 1
                assert P.wplan[i] == full, (i, P.wplan[i], full)
                return wsl[i % NW], ("w", i % NW)

            def done_w():
                if P.dry:
                    return
                j = P.wdone + NW
                P.wdone += 1
                if j < len(P.wplan):
                    rec_load(j)

            HT_ALL = [("hT", kc) for kc in range(8)]

            conv_list = [] if P.dry else [e_ for e_ in P.wplan if e_[4] == "scr" and is_preconv(e_) and e_[5] >= 0]
            seen_cv = set()
            conv_todo = []
            for e_ in conv_list:
                if e_[5] not in seen_cv:
                    seen_cv.add(e_[5])
                    conv_todo.append(e_)

            def rec_convs(n):
                for _ in range(n):
                    if not conv_todo:
                        return
                    wname, r0, c0, ncols, _m, idx = conv_todo.pop(0)
                    src = W[wname][r0:r0 + 1024, c0:c0 + ncols].rearrange("(kc p) c -> p kc c", p=128)
                    dst = wscr[idx].rearrange("p (k c) -> p k c", k=8)
                    P.op("pool", lambda e, dst=dst, src=src: e.dma_start(out=dst, in_=src), [], [("scr", idx)], dma=f"cv{idx}")

            P.op("sp", lambda e: e.dma_start(out=cst[:, :], in_=consts_d), [], ["cst"], dma="c0")
            P.op("sp", lambda e: e.dma_start(out=wg[0:17, :], in_=wga_d), [], ["wg"], dma="c1")
            P.op("sp", lambda e: e.dma_start(out=g1b[:, :], in_=bgb_d[:, 0:D]), [], ["g1b"], dma="c2")
            P.op("sp", lambda e: e.dma_start(out=g2b[:, :], in_=bgb_d[:, D:2 * D]), [], ["g2b"], dma="c3")
            P.op("sp", lambda e: e.dma_start(out=fnw[:, :], in_=fnwb_d), [], ["fnw"], dma="c4")

            P.op("pool", lambda e: e.memset(identf[:, :], 0.0), [], ["identf"])
            P.op("pool", lambda e: e.affine_select(out=identf[:, :], in_=identf[:, :], pattern=[[-1, 128]],
                                                   compare_op=ALU.not_equal, fill=1.0, base=0, channel_multiplier=1),
                 ["identf"], ["identf"])
            P.op("pool", lambda e: e.memset(cmask4[:, :], 1.0), [], ["cmask4"])
            P.op("pool", lambda e: e.affine_select(out=cmask4[:, :], in_=cmask4[:, :], pattern=[[0, 4], [1, 128]],
                                                   compare_op=ALU.is_ge, fill=0.0, base=0, channel_multiplier=-1),
                 ["cmask4"], ["cmask4"])
            P.op("pool", lambda e: e.memset(uneg[:, :], -1.0 / 16.0), [], ["uneg"])
            P.op("pool", lambda e: e.affine_select(out=uneg[:, :], in_=uneg[:, :], pattern=[[1, 128]],
                                                   compare_op=ALU.is_ge, fill=0.0, base=0, channel_multiplier=-1),
                 ["uneg"], ["uneg"])
            P.op("pool", lambda e: e.memset(lrT[:, :], 1.0), [], ["lrT"])
            P.op("pool", lambda e: e.dma_start(out=wlr[:, :, :], in_=W["w_in"][:, LR0:LR0 + 16].rearrange("(kc p) c -> p kc c", p=128)),
                 [], ["wlr"], dma="c7")
            if not P.dry:
                for j in range(min(4, len(P.wplan))):
                    rec_load(j)
            P.op("pool", lambda e: e.dma_start(out=wkp, in_=W["w_in"][:, K0:K0 + 512].rearrange("(kc p) c -> p kc c", p=128)),
                 [], ["wkp"], dma="c8")
            for n_ in range(2):
                P.op("pool", lambda e, n_=n_: e.dma_start(out=wvp[n_], in_=W["w_in"][:, V0 + n_ * 512:V0 + (n_ + 1) * 512].rearrange("(kc p) c -> p kc c", p=128)),
                     [], [f"wvp{n_}"], dma=f"c{9 + n_}")
            if not P.dry:
                for j in range(4, min(NW, len(P.wplan))):
                    rec_load(j)

            Vcopy(ident[:, :], identf[:, :], ["identf"], ["ident"])
            P.op("dve", lambda e: e.memset(ones_bf[:, :], 1.0), [], ["ones"])
            P.op("dve", lambda e: e.memset(S32[:, :, :], 0.0), [], [("S32", h) for h in range(4)])
            P.op("dve", lambda e: e.memset(S_bfs[0][:, :, :], 0.0), [], [("Sbf", 0)])
            P.op("dve", lambda e: e.memset(S_bfs[1][:, :, :], 0.0), [], [("Sbf", 1)])
            P.op("dve", lambda e: e.memset(uh[:, :, :], 0.0), [], [("uh", m) for m in range(8)])

            def rec_xload(g):
                src_t = x_prev if g < 4 else x_cur
                t0 = (g % 4) * T
                par = g % 2
                src = src_t[t0:t0 + T, :].rearrange("(s p) d -> p s d", p=128)
                P.op("sp", lambda e: e.dma_start(out=xbuf[par][:, :, :], in_=src), [],
                     [("x", par, s) for s in range(4)], dma=f"x{par}")

            rec_xload(0)
            rec_xload(1)

            cT = cst[:, C_CT:C_CT + 8]
            A(ce[:, :], cT, AF.Exp, ["cst"], ["ce"], scale=-1.0)
            Vsadd(ce[:, :], ce[:, :], 1.0, ["ce"], ["ce"])
            Vrecip(ce[:, :], ce[:, :], ["ce"], ["ce"])
            Vtt(cact[:, :], ce[:, :], cT, ALU.mult, ["ce", "cst"], ["cact"])
            Vcopy(cact_bf[:, :], cact[:, :], ["cact"], ["cactbf"])
            for kc in range(8):
                Vsmul(cbm[:, kc, :], ones_bf[:, :], cact[:, kc:kc + 1], ["ones", "cact"], [("cbm", kc)])

            def ada_chunk(ci):
                wt, wk = next_w(("w_ada", 0, ci * 512, 512))
                fm = {0: 0, 1: 0, 2: 1, 3: 1, 6: 2, 7: 2, 8: 3, 9: 3}
                if ci in fm:
                    j0 = fm[ci] * 8 + (ci % 2) * 4
                    p_, pk = bank()
                    mms = []
                    for jj in range(4):
                        for kc in range(8):
                            mms.append((p_[:, jj:jj + 1], wt[:, kc, jj * 128:(jj + 1) * 128], cact_bf[:, kc:kc + 1],
                                        kc == 0, kc == 7))
                    MM(mms, [wk, "cactbf"], [pk])
                    Vtt(modT[:, j0:j0 + 4], p_[:, 0:4], cst[:, C_BADA + j0:C_BADA + j0 + 4], ALU.add,
                        [pk, "cst"], [("modT", j0)])
                else:
                    gb_ = g1b if ci in (4, 5) else g2b
                    gk = "g1b" if ci in (4, 5) else "g2b"
                    hs_ = slice((ci % 2) * 512, (ci % 2) * 512 + 512)
                    p_, pk = bank()
                    MM([(p_[:, :], cbm[:, kc, :], wt[:, kc, :], kc == 0, kc == 7) for kc in range(8)],
                       [wk] + [("cbm", kc) for kc in range(8)], [pk])
                    Vtt(gb_[:, hs_], p_[:, :], gb_[:, hs_], ALU.add, [pk, gk], [gk])
                done_w()

            def mod_finish(which):
                sc0 = 8 if which == 1 else 24
                nw0 = C_N1 if which == 1 else C_N2
                dst = a1T if which == 1 else a2T
                key = "a1T" if which == 1 else "a2T"
                Vsadd(dst[:, :], modT[:, sc0:sc0 + 8], 1.0, [("modT", sc0), ("modT", sc0 + 4)], [key])
                Vtt(dst[:, :], dst[:, :], cst[:, nw0:nw0 + 8], ALU.mult, [key, "cst"], [key])

            def stage_norm(xb, par, aT_, akey, sh0):
                norm_p1(xb, par)
                norm_p2(aT_, akey, sh0)

            def norm_p1(xb, par):
                for s in range(4):
                    xk = ("x", par, s)
                    A(xn[:, s, :], xb[:, s, :], AF.Square, [xk], [("xn", s), ("ss", s)], accum_out=ss[:, s:s + 1])
                    A(lnv[:, s:s + 1], ss[:, s:s + 1], AF.Ln, [("ss", s)], [("lnv", s)], scale=1.0 / D, bias=EPS)
                    A(rstd[:, s:s + 1], lnv[:, s:s + 1], AF.Exp, [("lnv", s)], [("rstd", s)], scale=-0.5)
                    Vsmul(xn[:, s, :], xb[:, s, :], rstd[:, s:s + 1], [xk, ("rstd", s)], [("xn", s)])

            def norm_p2(aT_, akey, sh0):
                shkeys = [("modT", sh0), ("modT", sh0 + 4)]
                for kc in range(8):
                    pb, pbk = bbank()
                    TR([(pb[:, s * 128:(s + 1) * 128], xn[:, s, kc * 128:(kc + 1) * 128]) for s in range(4)],
                       [("xn", s) for s in range(4)], [pbk])
                    a_ap = aT_[:, kc:kc + 1]
                    s_ap = modT[:, sh0 + kc:sh0 + kc + 1]
                    if kc % 2 == 0:
                        Vts(hT[:, kc, :], pb[:, 0:512], a_ap, s_ap, ALU.mult, ALU.add, [pbk, akey] + shkeys, [("hT", kc)])
                    else:
                        A(hT[:, kc, :], pb[:, 0:512], AF.Identity, [pbk, akey] + shkeys, [("hT", kc)], scale=a_ap, bias=s_ap)

            def sigmoid(p_, pk):
                t, tk = tmp()
                A(t[:, :], p_[:, :], AF.Exp, [pk], [tk], scale=-1.0)
                A(t[:, :], t[:, :], AF.Ln, [tk], [tk], bias=1.0)
                A(t[:, :], t[:, :], AF.Exp, [tk], [tk], scale=-1.0)
                return t, tk

            flag_ap = cst[:, C_FLAG:C_FLAG + 1]

            def la_stage():
                la_p1()
                la_p2()

            def la_p1():
                p_, pk = bank()
                MM([(p_[0:16, :], wlr[:, kc, 0:16], hT[:, kc, :], kc == 0, kc == 7) for kc in range(8)],
                   ["wlr"] + HT_ALL, [pk])
                A(lrT[0:16, :], p_[0:16, :], AF.Copy, [pk], ["lrT"])

            def la_p2():
                for s in range(4):
                    p_, pk = bank()
                    MM([(p_[:, :], lrT[0:17, s * 128:(s + 1) * 128], wg[0:17, :], True, True)], ["lrT", "wg"], [pk])
                    t, tk = tmp()
                    A(t[:, :], p_[:, :], AF.Exp, [pk], [tk], scale=-1.0)
                    A(spb[:, s, :], t[:, :], AF.Ln, [tk], [("sp", s)], bias=1.0)

            def gla_stage(main, last_prev, special=False, hoist=None, nxt=None):
                if last_prev:
                    P.op("sp", lambda e: e.dma_start(out=w32q, in_=W["w_in"][:, Q0:Q0 + 512].rearrange("(kc p) c -> p kc c", p=128)),
                         [], ["w32q"] + [("x", 1, s_) for s_ in range(4)], dma="c5")
                    Vcopy(hTh[:, :, :], hT[:, :, 510:512], HT_ALL, ["hTh"])
                if main:
                    wq, wqk = next_w(("w_in", 0, Q0, 512))
                if main:
                    wk_, wkk = next_w(("w_in", 0, K0, 512))
                else:
                    wk_, wkk = wkp, "wkp"
                e2s = []
                pbbs = []
                for h in range(4):
                    hs = slice(h * 128, (h + 1) * 128)
                    pbb, pbbk = bank()
                    MM([(pbb[:, s * 128:(s + 1) * 128], spb[:, s, hs], uneg[:, :], True, True) for s in range(4)],
                       [("sp", s) for s in range(4)] + ["uneg"], [pbbk])
                    pbbs.append((pbb, pbbk))
                for h in range(4):
                    pbb, pbbk = pbbs[h]
                    A(E1[:, h, :], pbb[:, :], AF.Exp, [pbbk], [("E1", h)])
                    e2, e2k = tmp()
                    A(e2[:, :], pbb[:, :], AF.Exp, [pbbk], [e2k], scale=-1.0)
                    e2s.append((e2, e2k))
                if (not main) and nxt is not None:
                    norm_p1(xbuf[nxt % 2], nxt % 2)

                def head_front(h):
                    hs = slice(h * 128, (h + 1) * 128)
                    e2, e2k = e2s[h]
                    if main:
                        pq, pqk = bank()
                        MM([(pq[:, :], wq[:, kc, hs], hT[:, kc, :], kc == 0, kc == 7) for kc in range(8)],
                           [wqk] + HT_ALL, [pqk])
                    pkk_, pkkk = bank()
                    MM([(pkk_[:, :], wk_[:, kc, hs], hT[:, kc, :], kc == 0, kc == 7) for kc in range(8)],
                       [wkk] + HT_ALL, [pkkk])
                    if special:
                        MM([(pq[:, 0:128], w32q[:, kc, hs], h32T[:, kc, :], kc == 0, kc == 7) for kc in range(8)],
                           ["w32q", "h32T", ("x", 1, 0)], [pqk])
                        MM([(pkk_[:, 0:128], w32k[:, kc, hs], h32T[:, kc, :], kc == 0, kc == 7) for kc in range(8)],
                           ["w32k", "h32T"], [pkkk])
                    if main:
                        Vstt(qdT[:, h, :], pq[:, :], 128.0 ** -0.5, E1[:, h, :], ALU.mult, ALU.mult,
                             [pqk, ("E1", h)], [("qdT", h)])
                    Vtt(kdT[:, h, :], pkk_[:, :], e2[:, :], ALU.mult, [pkkk, e2k], [("kdT", h)])
                    if special:
                        Vstt(qd32[:, h, :], pq[:, 0:128], 128.0 ** -0.5, E1[:, h, 0:128], ALU.mult, ALU.mult,
                             [pqk, ("E1", h)], [("qd32", h), "xn32"])
                        Vtt(kd32[:, h, :], pkk_[:, 0:128], e2[:, 0:128], ALU.mult, [pkkk, e2k], [("kd32", h), "xn32"])

                def head_back(h):
                    pb, pbk = bbank()
                    TR([(pb[:, s * 128:(s + 1) * 128], kdT[:, h, s * 128:(s + 1) * 128]) for s in range(4)], [("kdT", h)], [pbk])
                    Vcopy(ke[:, h, :], pb[:, 0:512], [pbk], [("ke", h)])

                for h in range(4):
                    head_front(h)
                    if h >= 1:
                        head_back(h - 1)
                head_back(3)
                if main:
                    done_w()
                    done_w()
                for n in range(2):
                    if main:
                        wv, wvk = next_w(("w_in", 0, V0 + n * 512, 512))
                    else:
                        wv, wvk = wvp[n], f"wvp{n}"
                    for s in range(4):
                        p_, pk = bank()
                        MM([(p_[:, :], hT[:, kc, s * 128:(s + 1) * 128], wv[:, kc, :], kc == 0, kc == 7) for kc in range(8)],
                           [wvk] + HT_ALL, [pk])
                        if s % 2 == 0:
                            Vcopy(v_sb[:, s, n * 512:(n + 1) * 512], p_[:, :], [pk], [("v", s)])
                        else:
                            A(v_sb[:, s, n * 512:(n + 1) * 512], p_[:, :], AF.Copy, [pk], [("v", s)])
                    if main:
                        done_w()
                if last_prev:
                    P.op("sp", lambda e: e.dma_start(out=w32k, in_=W["w_in"][:, K0:K0 + 512].rearrange("(kc p) c -> p kc c", p=128)),
                         [], ["w32k"], dma="c6")
                if (not main) and nxt is not None:
                    norm_p2(a1T, "a1T", 0)
                    la_p1()
                if main:
                    for n in range(2):
                        wgg, wggk = next_w(("w_in", 0, G0 + n * 512, 512))
                        for s in range(4):
                            p_, pk = bank()
                            MM([(p_[:, :], hT[:, kc, s * 128:(s + 1) * 128], wgg[:, kc, :], kc == 0, kc == 7) for kc in range(8)],
                               [wggk] + HT_ALL, [pk])
                            r_, rk = sigmoid(p_, pk)
                            Vtt(sg[:, s, n * 512:(n + 1) * 512], p_[:, :], r_[:, :], ALU.mult, [pk, rk],
                                [("sg", s, 2 * n), ("sg", s, 2 * n + 1)])
                        done_w()
                def phaseA(s):
                    cs = slice(s * 128, (s + 1) * 128)
                    st = {}
                    if main:
                        psc, psck = bank()
                        if special and s == 0:
                            MM([(psc[:, h * 128:(h + 1) * 128], kd32[:, h, :], qd32[:, h, :], True, True) for h in range(4)],
                               [("kd32", h) for h in range(4)] + [("qd32", h) for h in range(4)], [psck])
                        else:
                            MM([(psc[:, h * 128:(h + 1) * 128], kdT[:, h, cs], qdT[:, h, cs], True, True) for h in range(4)],
                               [("kdT", h) for h in range(4)] + [("qdT", h) for h in range(4)], [psck])
                        sc, sck = scbuf()
                        Vtt(sc[:, :], psc[:, :], cmask4[:, :], ALU.mult, [psck, "cmask4"], [sck])
                        st["sc"] = (sc, sck)
                    pts = []
                    for hp in range(2):
                        pt, ptk = bank()
                        MM([(pt[:, j * 256:(j + 1) * 256], ke[:, hp * 2 + j, cs],
                             v_sb[:, s, (hp * 2 + j) * 256:(hp * 2 + j + 1) * 256], True, True) for j in range(2)],
                           [("ke", hp * 2), ("ke", hp * 2 + 1), ("v", s)], [ptk])
                        pts.append((pt, ptk))
                    st["pts"] = pts
                    return st

                def supd(s, st):
                    for h in range(4):
                        pt, ptk = st["pts"][h // 2]
                        Vtt(S32[:, h, :], S32[:, h, :], pt[:, (h % 2) * 256:(h % 2 + 1) * 256], ALU.add,
                            [("S32", h), ptk], [("S32", h)])
                        Vsmul(S32[:, h, :], S32[:, h, :], E1[:, h, s * 128 + 127:s * 128 + 128],
                              [("S32", h), ("E1", h)], [("S32", h)])

                def o_and_act(s, st):
                    cs = slice(s * 128, (s + 1) * 128)
                    pos = []
                    if main:
                        sc, sck = st["sc"]
                        Sb = S_bfs[s % 2]
                        for hp in range(2):
                            po, pok = bank()
                            mms = []
                            for j in range(2):
                                h = hp * 2 + j
                                mms.append((po[:, j * 256:(j + 1) * 256], sc[:, h * 128:(h + 1) * 128],
                                            v_sb[:, s, h * 256:(h + 1) * 256], True, False))
                                mms.append((po[:, j * 256:(j + 1) * 256], qdT[:, h, cs], Sb[:, h, :], False, True))
                            MM(mms, [sck, ("v", s), ("qdT", hp * 2), ("qdT", hp * 2 + 1), ("Sbf", s % 2)], [pok])
                            pos.append((po, pok))
                    if main:
                        A(S_bfs[(s + 1) % 2][:, :, :], S32[:, :, :], AF.Copy, [("S32", h) for h in range(4)], [("Sbf", (s + 1) % 2)])
                    if main:
                        for h in range(4):
                            po, pok = pos[h // 2]
                            i = s * 4 + h
                            jt, jtk = tmp()
                            A(jt[:, :].bitcast(BF16)[:, 0:256], po[:, (h % 2) * 256:(h % 2 + 1) * 256], AF.Square, [pok],
                              [("sso", i), jtk], accum_out=sso[:, i:i + 1])
                        A(lno[:, s * 4:(s + 1) * 4], sso[:, s * 4:(s + 1) * 4], AF.Ln, [("sso", s * 4 + h) for h in range(4)],
                          [("lno", s)], scale=1.0 / 256, bias=EPS)
                        A(rso[:, s * 4:(s + 1) * 4], lno[:, s * 4:(s + 1) * 4], AF.Exp, [("lno", s)], [("rso", s)], scale=-0.5)
                    return pos

                def og_stage(s, pos):
                    if main:
                        for h in range(4):
                            po, pok = pos[h // 2]
                            i = s * 4 + h
                            sgs = sg[:, s, h * 256:(h + 1) * 256]
                            Vstt(sgs, po[:, (h % 2) * 256:(h % 2 + 1) * 256], rso[:, i:i + 1], sgs, ALU.mult, ALU.mult,
                                 [pok, ("rso", s), ("sg", s, h)], [("sg", s, h)])

                sts = {0: phaseA(0)}
                pos_prev = None
                for s in range(4):
                    supd(s, sts[s])
                    if s + 1 < 4:
                        sts[s + 1] = phaseA(s + 1)
                    pos = o_and_act(s, sts[s])
                    if pos_prev is not None:
                        og_stage(s - 1, pos_prev)
                    pos_prev = pos
                og_stage(3, pos_prev)
                if last_prev:
                    for h in range(4):
                        Vsmul(S32[:, h, :], S32[:, h, :], flag_ap, [("S32", h), "cst"], [("S32", h)])
                    A(S_bfs[0][:, :, :], S32[:, :, :], AF.Copy, [("S32", h) for h in range(4)], [("Sbf", 0)])
                if not main:
                    if nxt is not None:
                        la_p2()
                    return
                for kc in range(8):
                    pb, pbk = bbank()
                    TR([(pb[:, s * 128:(s + 1) * 128], sg[:, s, kc * 128:(kc + 1) * 128]) for s in range(4)],
                       [("sg", s, kc // 2) for s in range(4)], [pbk])
                    g_ap = cst[:, C_GNW + kc % 2:C_GNW + kc % 2 + 1]
                    if kc % 2:
                        A(ogT[:, kc, :], pb[:, 0:512], AF.Copy, [pbk, "cst"], [("ogT", kc)], scale=g_ap)
                    else:
                        Vsmul(ogT[:, kc, :], pb[:, 0:512], g_ap, [pbk, "cst"], [("ogT", kc)])

            def cw_ap(m, j):
                c = C_CW + m * 3 + j
                return cst[:, c:c + 1]

            def conv_stage(first=False):
                for mg in range(2):
                    wcb, wcbk = next_w(("w_in", 0, CB0 + mg * 512, 512))
                    wcc, wcck = next_w(("w_in", 0, CC0 + mg * 512, 512))
                    wcx, wcxk = next_w(("w_in", 0, CX0 + mg * 512, 512))
                    for mm in range(4):
                        m = mg * 4 + mm
                        ms = slice(mm * 128, (mm + 1) * 128)
                        pc, pck = bank()
                        MM([(pc[:, :], wcc[:, kc, ms], hT[:, kc, :], kc == 0, kc == 7) for kc in range(8)], [wcck] + HT_ALL, [pck])
                        px, pxk = bank()
                        MM([(px[:, :], wcx[:, kc, ms], hT[:, kc, :], kc == 0, kc == 7) for kc in range(8)], [wcxk] + HT_ALL, [pxk])
                        pcb, pcbk = bank()
                        MM([(pcb[:, :], wcb[:, kc, ms], hT[:, kc, :], kc == 0, kc == 7) for kc in range(8)], [wcbk] + HT_ALL, [pcbk])
                        if first:
                            ph, phk = bank()
                            MM([(ph[:, 0:2], wcc[:, kc, ms], hTh[:, kc, :], kc == 0, kc == 7) for kc in range(8)]
                               + [(ph[:, 2:4], wcx[:, kc, ms], hTh[:, kc, :], kc == 0, kc == 7) for kc in range(8)],
                               [wcck, wcxk, "hTh"], [phk])
                            th, thk = tmp()
                            A(th[:, 0:2], ph[:, 0:2], AF.Copy, [phk], [thk])
                            Vtt(th[:, 2:4], th[:, 0:2], ph[:, 2:4], ALU.mult, [thk, phk], [thk])
                            Vsmul(uh[:, m, :], th[:, 2:4], flag_ap, [thk, "cst"], [("uh", m)])
                        t, tk = tmp()
                        A(t[:, :], pc[:, :], AF.Copy, [pck], [tk])
                        u, uk = ubuf()
                        A(u[:, 0:2], uh[:, m, :], AF.Copy, [("uh", m)], [uk])
                        Vtt(u[:, 2:514], t[:, :], px[:, :], ALU.mult, [tk, pxk, uk], [uk])
                        A(uh[:, m, :], u[:, 512:514], AF.Copy, [uk], [("uh", m)])
                        a_, ak = tmp()
                        Vsmul(a_[:, :], u[:, 2:514], cw_ap(m, 2), [uk, "cst"], [ak])
                        Vstt(a_[:, :], u[:, 1:513], cw_ap(m, 1), a_[:, :], ALU.mult, ALU.add, [uk, "cst", ak], [ak])
                        Vstt(a_[:, :], u[:, 0:512], cw_ap(m, 0), a_[:, :], ALU.mult, ALU.add, [uk, "cst", ak], [ak])
                        Vtt(cbuT[:, m, :], a_[:, :], pcb[:, :], ALU.mult, [ak, pcbk], [("cbuT", m)])
                    done_w()
                    done_w()
                    done_w()

            def merge_stage(xb, par):
                OGT_ALL = [("ogT", kc) for kc in range(8)]
                CBU_ALL = [("cbuT", kc) for kc in range(8)]
                for mg in range(2):
                    wa, wak = next_w(("w_pa", 0, mg * 512, 512))
                    wga_, wgak = next_w(("w_in", 0, GA0 + mg * 512, 512))
                    wb, wbk = next_w(("w_pb", 0, mg * 512, 512))
                    wgb, wgbk = next_w(("w_in", 0, GB0 + mg * 512, 512))
                    for mm in range(4):
                        m = mg * 4 + mm
                        ms = slice(mm * 128, (mm + 1) * 128)
                        pya, pyak = bank()
                        MM([(pya[:, :], wa[:, kc, ms], ogT[:, kc, :], kc == 0, kc == 7) for kc in range(8)], [wak] + OGT_ALL, [pyak])
                        pga, pgak = bank()
                        MM([(pga[:, :], wga_[:, kc, ms], hT[:, kc, :], kc == 0, kc == 7) for kc in range(8)], [wgak] + HT_ALL, [pgak])
                        ra, rak = sigmoid(pga, pgak)
                        Vtt(ra[:, :], ra[:, :], pya[:, :], ALU.mult, [rak, pyak], [rak])
                        pyb, pybk = bank()
                        MM([(pyb[:, :], wb[:, kc, ms], cbuT[:, kc, :], kc == 0, kc == 7) for kc in range(8)], [wbk] + CBU_ALL, [pybk])
                        pgb, pgbk = bank()
                        MM([(pgb[:, :], wgb[:, kc, ms], hT[:, kc, :], kc == 0, kc == 7) for kc in range(8)], [wgbk] + HT_ALL, [pgbk])
                        rb, rbk = sigmoid(pgb, pgbk)
                        Vtt(rb[:, :], rb[:, :], pyb[:, :], ALU.mult, [rbk, pybk], [rbk])
                        Vtt(zT[:, m, :], ra[:, :], rb[:, :], ALU.add, [rak, rbk], [("zT", m)])
                    for _ in range(4):
                        done_w()
                ZT_ALL = [("zT", kc) for kc in range(8)]
                for n in range(2):
                    wo, wok = next_w(("w_o", 0, n * 512, 512))
                    ns = slice(n * 512, (n + 1) * 512)
                    for s in range(4):
                        p_, pk = bank()
                        MM([(p_[:, :], zT[:, kc, s * 128:(s + 1) * 128], wo[:, kc, :], kc == 0, kc == 7) for kc in range(8)],
                           [wok] + ZT_ALL, [pk])
                        t, tk = tmp()
                        Vtt(t[:, :], p_[:, :], g1b[:, ns], ALU.mult, [pk, "g1b"], [tk])
                        xk = ("x", par, s)
                        Vtt(xb[:, s, ns], xb[:, s, ns], t[:, :], ALU.add, [xk, tk], [xk])
                    done_w()

            def mlp_stage(xb, par, g, hoist=None, nxt=None):
                stage_norm(xb, par, a2T, "a2T", 16)
                if nxt is not None:
                    norm_p1(xbuf[nxt % 2], nxt % 2)
                for cg in range(8):
                    w1_, w1k = next_w(("w1", 0, cg * 512, 512))
                    for jj in range(4):
                        j = cg * 4 + jj
                        p_, pk = bank()
                        MM([(p_[:, :], w1_[:, kc, jj * 128:(jj + 1) * 128], hT[:, kc, :], kc == 0, kc == 7) for kc in range(8)],
                           [w1k] + HT_ALL, [pk])
                        t, tk = tmp()
                        A(t[:, :], p_[:, :], AF.Relu, [pk], [tk])
                        Vtt(aT[:, j, :], t[:, :], t[:, :], ALU.mult, [tk], [("aT", j)])
                    done_w()
                for n in range(2):
                    ns = slice(n * 512, (n + 1) * 512)
                    bks = [bank() for _ in range(4)]
                    for jg in range(4):
                        w2_, w2k = next_w(("w2", jg * 1024, n * 512, 512))
                        for s in range(4):
                            MM([(bks[s][0][:, :], aT[:, jg * 8 + jj, s * 128:(s + 1) * 128], w2_[:, jj, :],
                                 jg == 0 and jj == 0, jg == 3 and jj == 7) for jj in range(8)],
                               [w2k] + [("aT", jg * 8 + jj) for jj in range(8)], [bks[s][1]])
                        done_w()
                    for s in range(4):
                        t, tk = tmp()
                        Vtt(t[:, :], bks[s][0][:, :], g2b[:, ns], ALU.mult, [bks[s][1], "g2b"], [tk])
                        xk = ("x", par, s)
                        Vtt(xb[:, s, ns], xb[:, s, ns], t[:, :], ALU.add, [xk, tk], [xk])
                    if n == 0 and nxt is not None:
                        norm_p2(a1T, "a1T", 0)
                        la_p1()
                        la_p2()
                for s in range(4):
                    xk = ("x", par, s)
                    jt, jtk = tmp()
                    A(jt[:, :].bitcast(BF16), xb[:, s, :], AF.Square, [xk], [jtk, ("ss2", s)], accum_out=ss2[:, s:s + 1])
                    A(lnv2[:, s:s + 1], ss2[:, s:s + 1], AF.Ln, [("ss2", s)], [("lnv2", s)], scale=1.0 / D, bias=EPS)
                    A(rstd2[:, s:s + 1], lnv2[:, s:s + 1], AF.Exp, [("lnv2", s)], [("rstd2", s)], scale=-0.5)
                    A(xb[:, s, :], xb[:, s, :], AF.Copy, [xk, ("rstd2", s)], [xk], scale=rstd2[:, s:s + 1])
                    Vtt(xb[:, s, :], xb[:, s, :], fnw[:, :], ALU.mult, [xk, "fnw"], [xk])
                    r0 = (g % 4) * T + s * 128
                    P.op("sp", lambda e, r0=r0, s=s: e.dma_start(out=out_d[r0:r0 + 128, :], in_=xb[:, s, :]), [xk],
                         [("out", g, s)], dma=f"o{par}")

            def special_prep(xb, par):
                Vsmul(xn32, xb[:, 0, :], rstd[:, 0:1], [("x", par, 0), ("rstd", 0)], ["xn32"])
                for half in range(2):
                    p_, pk = bank()
                    def fn(e, p_=p_, half=half):
                        ins = None
                        for j in range(4):
                            kc = half * 4 + j
                            ins = e.transpose(out=p_[:, j * 128:(j + 1) * 128], in_=xn32[:, kc * 128:(kc + 1) * 128],
                                              identity=identf[:, :])
                        return ins
                    P.op("pe", fn, ["xn32", "identf"], [pk])
                    for j in range(4):
                        kc = half * 4 + j
                        Vts(h32T[:, kc, :], p_[:, j * 128:(j + 1) * 128], a1T[:, kc:kc + 1], modT[:, kc:kc + 1],
                            ALU.mult, ALU.add, [pk, "a1T", ("modT", 0), ("modT", 4)], ["h32T"])

            for ci in range(4):
                ada_chunk(ci)
            mod_finish(1)
            rest = [4, 5, 6, 7, 8, 9, 10, 11]
            for g in range(8):
                par = g % 2
                xb = xbuf[par]
                if g >= 4:
                    P.wmode = "t4" if g == 4 else "t5" if g == 5 else "scr"
                    P.widx = 0
                if g == 0:
                    stage_norm(xb, par, a1T, "a1T", 0)
                    la_stage()
                nxt = g + 1 if g + 1 < 8 else None
                if g < 3:
                    rec_xload(g + 2)
                if g < 4:
                    gla_stage(False, g == 3, nxt=nxt)
                    for ci in rest[g * 2:g * 2 + 2]:
                        ada_chunk(ci)
                    rec_convs(2 if g < 3 else 100)
                    if g == 3:
                        mod_finish(2)
                else:
                    if g == 4:
                        special_prep(xb, par)
                    gla_stage(True, False, special=(g == 4))
                    if g == 4:
                        rec_xload(5)
                    conv_stage(first=(g == 4))
                    merge_stage(xb, par)
                    mlp_stage(xb, par, g, nxt=nxt)
                if g + 2 < 8 and g >= 4:
                    rec_xload(g + 2)
            P.op("sp", None, [("out", g, s_) for g in range(4, 8) for s_ in range(4)], [])

        wplan = []
        record(Prog(True, wplan))
        P = Prog(False, wplan)
        record(P)
        assert P.wi == len(wplan), (P.wi, len(wplan))

        sem_names = P.sems()
        S = {n: es.enter_context(nc.semaphore(n)) for n in sem_names}
        block = es.enter_context(nc.Block())

        def run_stream(eng_name):
            def body(e):
                for (waits, fn, sem, amt) in P.streams[eng_name]:
                    for (s_, v_) in waits:
                        e.wait_ge(S[s_], v_)
                    if fn is not None:
                        fn(e).then_inc(S[sem], amt)
            return body

        block.sync(run_stream("sp"))
        block.gpsimd(run_stream("pool"))
        block.tensor(run_stream("pe"))
        block.scalar(run_stream("act"))
        block.vector(run_stream("dve"))
    return nc


_NC = None


def kernel(x, c, w_ada, b_ada, norm1_w, w_in, w_gate_up, b_gate, gla_norm_w, conv_w,
           w_proj_a, w_proj_b, w_out, norm2_w, w_mlp1, w_mlp2, final_norm_w):
    global _NC
    f = lambda a: np.ascontiguousarray(np.asarray(a, dtype=np.float32))
    x = f(x)
    c = f(c)
    b_ada = f(b_ada)[0]
    shared = {
        "w_ada": f(w_ada)[0], "w_in": f(w_in)[0], "w_pa": f(w_proj_a)[0], "w_pb": f(w_proj_b)[0],
        "w_o": f(w_out)[0], "w1": f(w_mlp1)[0], "w2": f(w_mlp2)[0],
        "fnw_b": np.ascontiguousarray(np.broadcast_to(f(final_norm_w)[None, :], (128, D))),
        "bgate_b": np.ascontiguousarray(np.broadcast_to(
            np.concatenate([b_ada[2 * D:3 * D], b_ada[5 * D:6 * D]])[None, :], (128, 2 * D))),
        "wg_aug": np.ascontiguousarray(np.concatenate([f(w_gate_up)[0], f(b_gate)[0][None, :]], axis=0)),
    }
    colT = lambda v: np.ascontiguousarray(v.reshape(-1, 128).T)
    cbase = np.zeros((128, NCONST), np.float32)
    cbase[:, C_BADA:C_BADA + 32] = np.concatenate(
        [colT(b_ada[0:D]), colT(b_ada[D:2 * D]), colT(b_ada[3 * D:4 * D]), colT(b_ada[4 * D:5 * D])], axis=1)
    cbase[:, C_N1:C_N1 + 8] = colT(f(norm1_w)[0])
    cbase[:, C_N2:C_N2 + 8] = colT(f(norm2_w)[0])
    cwl = f(conv_w)[0]
    cbase[:, C_CW:C_CW + 24] = np.transpose(cwl.reshape(3, 8, 128), (2, 1, 0)).reshape(128, 24)
    cbase[:, C_GNW:C_GNW + 2] = colT(f(gla_norm_w)[0])
    if _NC is None:
        _NC = build_nc()
    in_maps = []
    for i in range(8):
        b, hf = i // 2, i % 2
        cs = cbase.copy()
        cs[:, C_CT:C_CT + 8] = colT(c[b])
        cs[:, C_FLAG] = float(hf)
        m = dict(shared)
        m["x_cur"] = np.ascontiguousarray(x[b, hf * TOK:(hf + 1) * TOK])
        m["x_prev"] = np.ascontiguousarray(x[b, 0:TOK])
        m["consts"] = cs
        in_maps.append(m)
    res = run_bass_kernel_spmd(_NC, in_maps, core_ids=list(range(8)))
    out = np.empty((4, 2 * TOK, D), np.float32)
    for i in range(8):
        b, hf = i // 2, i % 2
        out[b, hf * TOK:(hf + 1) * TOK] = np.asarray(res.results[i]["out"]).reshape(TOK, D)
    return out
```

```python
import numpy as np
from contextlib import ExitStack
import concourse.bass as bass
import concourse.mybir as mybir
from concourse.bass_utils import run_bass_kernel_spmd

F32 = mybir.dt.float32
BF16 = mybir.dt.bfloat16
AF = mybir.ActivationFunctionType
ALU = mybir.AluOpType

D = 1024
TOK = 2048
T = 512
NW = 6
NTMP = 8
NSCR = 40
PRECONV = ("w2",)
EPS = 1e-6
Q0, K0, V0, G0, LR0, CB0, CC0, CX0, GA0, GB0 = 0, 512, 1024, 2048, 3072, 3088, 4112, 5136, 6160, 7184
C_CT, C_BADA, C_N1, C_N2, C_CW, C_GNW, C_FLAG, NCONST = 0, 8, 40, 48, 56, 80, 82, 84


class Ev:
    __slots__ = ("eng", "sem", "val", "know", "dma")

    def __init__(self, eng, sem, val, know, dma):
        self.eng, self.sem, self.val, self.know, self.dma = eng, sem, val, know, dma


class Prog:
    ENGS = ("pe", "act", "dve", "pool", "sp")

    def __init__(self, dry, wplan):
        self.dry = dry
        self.wplan = wplan
        self.streams = {e: [] for e in self.ENGS}
        self.cnt = {e: 0 for e in self.ENGS}
        self.dcnt = {}
        self.know = {e: {} for e in self.ENGS}
        self.last_w = {}
        self.readers = {}
        self.groups = {}
        self.cur_view = {}
        self.barrier = {}
        self.name_keys = {}
        self.bi = 0
        self.bbi = 0
        self.ti = 0
        self.wi = 0
        self.wdone = 0
        self.wmode = "cast"
        self.widx = -1

    def op(self, eng, fn, reads=(), writes=(), dma=None):
        if self.dry:
            return
        deps = []
        for k in list(reads) + list(writes):
            name = k[0] if isinstance(k, tuple) else k
            self.name_keys.setdefault(name, set()).add(k)
            for g in self.groups.get(name, ()):
                if self.cur_view.get(g) != name:
                    old = self.cur_view.get(g)
                    evs = []
                    if old is not None:
                        for kk in self.name_keys.get(old, ()):
                            if kk in self.last_w:
                                evs.append(self.last_w[kk])
                            evs += self.readers.get(kk, [])
                    self.barrier[g] = evs
                    self.cur_view[g] = name
                deps += self.barrier.get(g, [])
        is_dma = dma is not None
        for k in reads:
            ev = self.last_w.get(k)
            if ev is not None:
                deps.append(ev)
        for k in writes:
            ev = self.last_w.get(k)
            if ev is not None and (is_dma or ev.dma or ev.eng != eng or eng != "pe"):
                deps.append(ev)
            for ev in self.readers.get(k, []):
                if is_dma or ev.dma or ev.eng != eng or eng != "pe":
                    deps.append(ev)
        kn = self.know[eng]
        waits = {}
        for ev in deps:
            if kn.get(ev.sem, 0) >= ev.val:
                continue
            waits[ev.sem] = max(waits.get(ev.sem, 0), ev.val)
            for s_, v_ in ev.know.items():
                if kn.get(s_, 0) < v_:
                    kn[s_] = v_
        if fn is None:
            self.streams[eng].append((list(waits.items()), None, None, 0))
            return
        if is_dma:
            sem = dma
            self.dcnt[sem] = self.dcnt.get(sem, 0) + 16
            val = self.dcnt[sem]
            amt = 16
        else:
            sem = "E_" + eng
            self.cnt[eng] += 1
            val = self.cnt[eng]
            amt = 1
        evk = dict(kn)
        evk[sem] = val
        ev = Ev(eng, sem, val, evk, is_dma)
        for k in reads:
            self.readers.setdefault(k, []).append(ev)
        for k in writes:
            self.last_w[k] = ev
            self.readers[k] = []
        self.streams[eng].append((list(waits.items()), fn, sem, amt))

    def sems(self):
        s = {"E_" + e for e in self.ENGS}
        s |= set(self.dcnt.keys())
        return sorted(s)


def build_nc():
    nc = bass.Bass("TRN2", target_bir_lowering=False)

    def din(name, shape):
        return nc.dram_tensor(name, shape, F32, kind="ExternalInput").ap()

    x_cur = din("x_cur", [TOK, D])
    x_prev = din("x_prev", [TOK, D])
    W = {
        "w_ada": din("w_ada", [D, 6 * D]),
        "w_in": din("w_in", [D, 8208]),
        "w_pa": din("w_pa", [D, D]),
        "w_pb": din("w_pb", [D, D]),
        "w_o": din("w_o", [D, D]),
        "w1": din("w1", [D, 4 * D]),
        "w2": din("w2", [4 * D, D]),
    }
    consts_d = din("consts", [128, NCONST])
    fnwb_d = din("fnw_b", [128, D])
    bgb_d = din("bgate_b", [128, 2 * D])
    wga_d = din("wg_aug", [17, 512])
    out_d = nc.dram_tensor("out", [TOK, D], F32, kind="ExternalOutput").ap()
    wscr = nc.dram_tensor("wscr", [NSCR, 128, 4096], BF16, kind="Internal").ap()

    with ExitStack() as es:
        def sb(name, shape, dt):
            return es.enter_context(nc.sbuf_tensor(name, shape, dt))

        xbuf = [sb(f"xbuf{i}", [128, 4, D], F32) for i in range(2)]
        xn = sb("xn", [128, 4, D], BF16)
        hT = sb("hT", [128, 8, T], BF16)
        wsl = [sb(f"wsl{i}", [128, 8, 512], BF16) for i in range(NW)]
        wlr = sb("wlr", [128, 8, 16], BF16)
        lrT = sb("lrT", [32, T], F32)
        wg = sb("wg", [32, 512], F32)
        spb = sb("spb", [128, 4, 512], F32)
        E1 = spb
        spc = sb("spc", [128, 4, 512], F32)
        qdT = sb("qdT", [128, 4, T], BF16)
        kdT = sb("kdT", [128, 4, T], BF16)
        ke = sb("ke", [128, 4, T], BF16)
        big = sb("big", [128, 16384], BF16)
        S32 = sb("S32", [128, 4, 256], F32)
        S_bfs = [sb(f"S_bf{i}", [128, 4, 256], BF16) for i in range(2)]
        scb = [sb(f"scb{i}", [128, 512], BF16) for i in range(2)]
        cmask4 = sb("cmask4", [128, 512], BF16)
        ss2 = sb("ss2", [128, 4], F32)
        lnv2 = sb("lnv2", [128, 4], F32)
        rstd2 = sb("rstd2", [128, 4], F32)
        ub = [sb(f"ub{i}", [128, 514], F32) for i in range(2)]
        uh = sb("uh", [128, 8, 2], F32)
        hTh = sb("hTh", [128, 8, 2], BF16)
        tmps = [sb(f"tmp{i}", [128, 512], F32) for i in range(NTMP)]
        cst = sb("cst", [128, NCONST], F32)
        g1b = sb("g1b", [128, D], F32)
        g2b = sb("g2b", [128, D], F32)
        identf = sb("identf", [128, 128], F32)
        ident = sb("ident", [128, 128], BF16)
        ones_bf = sb("ones_bf", [128, 128], BF16)
        uneg = sb("uneg", [128, 128], F32)
        ce = sb("ce", [128, 8], F32)
        cact = sb("cact", [128, 8], F32)
        cact_bf = sb("cact_bf", [128, 8], BF16)
        modT = sb("modT", [128, 32], F32)
        a1T = sb("a1T", [128, 8], F32)
        a2T = sb("a2T", [128, 8], F32)
        ss = sb("ss", [128, 4], F32)
        lnv = sb("lnv", [128, 4], F32)
        rstd = sb("rstd", [128, 4], F32)
        sso = sb("sso", [128, 16], F32)
        lno = sb("lno", [128, 16], F32)
        rso = sb("rso", [128, 16], F32)

        psf = [es.enter_context(nc.psum_tensor(f"psf{i}", [128, 512], F32)) for i in range(8)]
        psb = [p_[:, :].bitcast(BF16) for p_ in psf]

        v_sb = big[:, 0:4096].rearrange("p (s c) -> p s c", s=4)
        ogT = big[:, 0:4096].rearrange("p (k c) -> p k c", k=8)
        sg = big[:, 4096:8192].rearrange("p (s c) -> p s c", s=4)
        cbuT = big[:, 8192:12288].rearrange("p (k c) -> p k c", k=8)
        zT = big[:, 12288:16384].rearrange("p (k c) -> p k c", k=8)
        aT = big[:, :].rearrange("p (j c) -> p j c", j=32)
        wkp = big[:, 4096:8192].rearrange("p (k c) -> p k c", k=8)
        wvp = [big[:, 8192:12288].rearrange("p (k c) -> p k c", k=8),
               big[:, 12288:16384].rearrange("p (k c) -> p k c", k=8)]
        cbm = qdT[:, 0:2, :].rearrange("p a (k c) -> p (a k) c", k=4)
        w32q = xbuf[1][:, :, :].rearrange("p s (a c) -> p (s a) c", a=2)
        w32k = big[:, 8192:16384].bitcast(F32).rearrange("p (k c) -> p k c", k=8)
        fnw = xn[:, 0:2, :].rearrange("p a c -> p (a c)").bitcast(F32)
        h32T = spc[:, 0:2, :].rearrange("p a (k c) -> p (a k) c", k=4)
        xn32 = spc[:, 2:4, :].rearrange("p a c -> p (a c)")
        qd32 = spc[:, 2, :].rearrange("p (h c) -> p h c", h=4)
        kd32 = spc[:, 3, :].rearrange("p (h c) -> p h c", h=4)

        def record(P):
            P.groups = {"v": ["A"], "ogT": ["A"], "sg": ["B"], "cbuT": ["C"], "zT": ["Dg"],
                        "aT": ["A", "B", "C", "Dg"], "w32k": ["C", "Dg"],
                        "wkp": ["B"], "wvp0": ["C"], "wvp1": ["Dg"],
                        "cbm": ["Q"], "qdT": ["Q"], "sp": ["SE"], "E1": ["SE"]}

            def A(out, in_, func, r, w, **kw):
                P.op("act", lambda e: e.activation(out=out, in_=in_, func=func, **kw), r, w)

            def Vtt(out, a, b, op, r, w):
                P.op("dve", lambda e: e.tensor_tensor(out=out, in0=a, in1=b, op=op), r, w)

            def Vts(out, a, s1, s2, op0, op1, r, w):
                P.op("dve", lambda e: e.tensor_scalar(out=out, in0=a, scalar1=s1, scalar2=s2, op0=op0, op1=op1), r, w)

            def Vsmul(out, a, s, r, w):
                P.op("dve", lambda e: e.tensor_scalar_mul(out=out, in0=a, scalar1=s), r, w)

            def Vsadd(out, a, s, r, w):
                P.op("dve", lambda e: e.tensor_scalar_add(out=out, in0=a, scalar1=s), r, w)

            def Vstt(out, a, s, b, op0, op1, r, w):
                P.op("dve", lambda e: e.scalar_tensor_tensor(out=out, in0=a, scalar=s, in1=b, op0=op0, op1=op1), r, w)

            def Vcopy(out, a, r, w):
                P.op("dve", lambda e: e.tensor_copy(out=out, in_=a), r, w)

            def Vrecip(out, a, r, w):
                P.op("dve", lambda e: e.reciprocal(out=out, in_=a), r, w)

            def MM(mms, r, w):
                def fn(e):
                    ins = None
                    for (o, l, rh, st, sp_) in mms:
                        ins = e.matmul(out=o, lhsT=l, rhs=rh, start=st, stop=sp_)
                    return ins
                P.op("pe", fn, r, w)

            def TR(trs, r, w):
                def fn(e):
                    ins = None
                    for (o, i) in trs:
                        ins = e.transpose(out=o, in_=i, identity=ident[:])
                    return ins
                P.op("pe", fn, list(r) + ["ident"], w)

            def bank():
                i = P.bi % 8
                P.bi += 1
                return psf[i], ("ps", i)

            def bbank():
                i = P.bi % 8
                P.bi += 1
                return psb[i], ("ps", i)

            def tmp():
                i = P.ti % NTMP
                P.ti += 1
                return tmps[i], ("tmp", i)

            sci = [0]

            def scbuf():
                i = sci[0] % 2
                sci[0] += 1
                return scb[i], ("sc", i)

            ubi = [0]

            def ubuf():
                i = ubi[0] % 2
                ubi[0] += 1
                return ub[i], ("u", i)

            def rec_load(j):
                wname, r0, c0, ncols, mode, idx = P.wplan[j]
                slot = j % NW
                if mode == "scr":
                    src = wscr[idx]
                    dst = wsl[slot][:, :, :].rearrange("p k c -> p (k c)")
                    P.op("pool", lambda e: e.dma_start(out=dst, in_=src), [("scr", idx)], [("w", slot)], dma=f"w{slot}")
                else:
                    src = W[wname][r0:r0 + 1024, c0:c0 + ncols].rearrange("(kc p) c -> p kc c", p=128)
                    dst = wsl[slot][:, :, 0:ncols]
                    P.op("pool", lambda e: e.dma_start(out=dst, in_=src), [], [("w", slot)], dma=f"w{slot}")
                if mode == "castwb":
                    wsrc = wsl[slot][:, :, :].rearrange("p k c -> p (k c)")
                    wdst = wscr[idx]
                    P.op("sp", lambda e: e.dma_start(out=wdst, in_=wsrc), [("w", slot)], [("scr", idx)], dma=f"wb{slot}")

            def is_preconv(spec):
                return spec[0] in PRECONV

            def next_w(spec):
                mode = P.wmode
                if mode in ("t4", "t5"):
                    if is_preconv(spec):
                        mode = "scr"
                    elif P.widx % 2 == 0:
                        mode = "castwb" if mode == "t4" else "scr"
                    else:
                        mode = "cast4" if mode == "t4" else "castwb"
                full = tuple(spec) + (mode, P.widx)
                if P.wmode != "cast":
                    P.widx += 1
                if P.dry:
                    P.wplan.append(full)
                    return wsl[0], ("w", 0)
                i = P.wi
                P.wi += 1
                assert P.wplan[i] == full, (i, P.wplan[i], full)
                return wsl[i % NW], ("w", i % NW)

            def done_w():
                if P.dry:
                    return
                j = P.wdone + NW
                P.wdone += 1
                if j < len(P.wplan):
                    rec_load(j)

            HT_ALL = [("hT", kc) for kc in range(8)]

            conv_list = [] if P.dry else [e_ for e_ in P.wplan if e_[4] == "scr" and is_preconv(e_) and e_[5] >= 0]
            seen_cv = set()
            conv_todo = []
            for e_ in conv_list:
                if e_[5] not in seen_cv:
                    seen_cv.add(e_[5])
                    conv_todo.append(e_)

            def rec_convs(n):
                for _ in range(n):
                    if not conv_todo:
                        return
                    wname, r0, c0, ncols, _m, idx = conv_todo.pop(0)
                    src = W[wname][r0:r0 + 1024, c0:c0 + ncols].rearrange("(kc p) c -> p kc c", p=128)
                    dst = wscr[idx].rearrange("p (k c) -> p k c", k=8)
                    P.op("pool", lambda e, dst=dst, src=src: e.dma_start(out=dst, in_=src), [], [("scr", idx)], dma=f"cv{idx}")

            P.op("sp", lambda e: e.dma_start(out=cst[:, :], in_=consts_d), [], ["cst"], dma="c0")
            P.op("sp", lambda e: e.dma_start(out=wg[0:17, :], in_=wga_d), [], ["wg"], dma="c1")
            P.op("sp", lambda e: e.dma_start(out=g1b[:, :], in_=bgb_d[:, 0:D]), [], ["g1b"], dma="c2")
            P.op("sp", lambda e: e.dma_start(out=g2b[:, :], in_=bgb_d[:, D:2 * D]), [], ["g2b"], dma="c3")

            P.op("pool", lambda e: e.memset(identf[:, :], 0.0), [], ["identf"])
            P.op("pool", lambda e: e.affine_select(out=identf[:, :], in_=identf[:, :], pattern=[[-1, 128]],
                                                   compare_op=ALU.not_equal, fill=1.0, base=0, channel_multiplier=1),
                 ["identf"], ["identf"])
            P.op("pool", lambda e: e.memset(cmask4[:, :], 1.0), [], ["cmask4"])
            P.op("pool", lambda e: e.affine_select(out=cmask4[:, :], in_=cmask4[:, :], pattern=[[0, 4], [1, 128]],
                                                   compare_op=ALU.is_ge, fill=0.0, base=0, channel_multiplier=-1),
                 ["cmask4"], ["cmask4"])
            P.op("pool", lambda e: e.memset(uneg[:, :], -1.0 / 16.0), [], ["uneg"])
            P.op("pool", lambda e: e.affine_select(out=uneg[:, :], in_=uneg[:, :], pattern=[[1, 128]],
                                                   compare_op=ALU.is_ge, fill=0.0, base=0, channel_multiplier=-1),
                 ["uneg"], ["uneg"])
            P.op("pool", lambda e: e.memset(lrT[:, :], 1.0), [], ["lrT"])
            P.op("pool", lambda e: e.dma_start(out=wlr[:, :, :], in_=W["w_in"][:, LR0:LR0 + 16].rearrange("(kc p) c -> p kc c", p=128)),
                 [], ["wlr"], dma="c7")
            if not P.dry:
                for j in range(min(4, len(P.wplan))):
                    rec_load(j)
            P.op("pool", lambda e: e.dma_start(out=wkp, in_=W["w_in"][:, K0:K0 + 512].rearrange("(kc p) c -> p kc c", p=128)),
                 [], ["wkp"], dma="c8")
            for n_ in range(2):
                P.op("pool", lambda e, n_=n_: e.dma_start(out=wvp[n_], in_=W["w_in"][:, V0 + n_ * 512:V0 + (n_ + 1) * 512].rearrange("(kc p) c -> p kc c", p=128)),
                     [], [f"wvp{n_}"], dma=f"c{9 + n_}")
            if not P.dry:
                for j in range(4, min(NW, len(P.wplan))):
                    rec_load(j)

            Vcopy(ident[:, :], identf[:, :], ["identf"], ["ident"])
            P.op("dve", lambda e: e.memset(ones_bf[:, :], 1.0), [], ["ones"])
            P.op("dve", lambda e: e.memset(S32[:, :, :], 0.0), [], [("S32", h) for h in range(4)])
            P.op("dve", lambda e: e.memset(S_bfs[0][:, :, :], 0.0), [], [("Sbf", 0)])
            P.op("dve", lambda e: e.memset(S_bfs[1][:, :, :], 0.0), [], [("Sbf", 1)])
            P.op("dve", lambda e: e.memset(uh[:, :, :], 0.0), [], [("uh", m) for m in range(8)])

            def rec_xload(g):
                src_t = x_prev if g < 4 else x_cur
                t0 = (g % 4) * T
                par = g % 2
                src = src_t[t0:t0 + T, :].rearrange("(s p) d -> p s d", p=128)
                P.op("sp", lambda e: e.dma_start(out=xbuf[par][:, :, :], in_=src), [],
                     [("x", par, s) for s in range(4)], dma=f"x{par}")

            rec_xload(0)
            rec_xload(1)

            cT = cst[:, C_CT:C_CT + 8]
            A(ce[:, :], cT, AF.Exp, ["cst"], ["ce"], scale=-1.0)
            Vsadd(ce[:, :], ce[:, :], 1.0, ["ce"], ["ce"])
            Vrecip(ce[:, :], ce[:, :], ["ce"], ["ce"])
            Vtt(cact[:, :], ce[:, :], cT, ALU.mult, ["ce", "cst"], ["cact"])
            Vcopy(cact_bf[:, :], cact[:, :], ["cact"], ["cactbf"])
            for kc in range(8):
                Vsmul(cbm[:, kc, :], ones_bf[:, :], cact[:, kc:kc + 1], ["ones", "cact"], [("cbm", kc)])

            def ada_chunk(ci):
                wt, wk = next_w(("w_ada", 0, ci * 512, 512))
                fm = {0: 0, 1: 0, 2: 1, 3: 1, 6: 2, 7: 2, 8: 3, 9: 3}
                if ci in fm:
                    j0 = fm[ci] * 8 + (ci % 2) * 4
                    p_, pk = bank()
                    mms = []
                    for jj in range(4):
                        for kc in range(8):
                            mms.append((p_[:, jj:jj + 1], wt[:, kc, jj * 128:(jj + 1) * 128], cact_bf[:, kc:kc + 1],
                                        kc == 0, kc == 7))
                    MM(mms, [wk, "cactbf"], [pk])
                    Vtt(modT[:, j0:j0 + 4], p_[:, 0:4], cst[:, C_BADA + j0:C_BADA + j0 + 4], ALU.add,
                        [pk, "cst"], [("modT", j0)])
                else:
                    gb_ = g1b if ci in (4, 5) else g2b
                    gk = "g1b" if ci in (4, 5) else "g2b"
                    hs_ = slice((ci % 2) * 512, (ci % 2) * 512 + 512)
                    p_, pk = bank()
                    MM([(p_[:, :], cbm[:, kc, :], wt[:, kc, :], kc == 0, kc == 7) for kc in range(8)],
                       [wk] + [("cbm", kc) for kc in range(8)], [pk])
                    Vtt(gb_[:, hs_], p_[:, :], gb_[:, hs_], ALU.add, [pk, gk], [gk])
                done_w()

            def mod_finish(which):
                sc0 = 8 if which == 1 else 24
                nw0 = C_N1 if which == 1 else C_N2
                dst = a1T if which == 1 else a2T
                key = "a1T" if which == 1 else "a2T"
                Vsadd(dst[:, :], modT[:, sc0:sc0 + 8], 1.0, [("modT", sc0), ("modT", sc0 + 4)], [key])
                Vtt(dst[:, :], dst[:, :], cst[:, nw0:nw0 + 8], ALU.mult, [key, "cst"], [key])

            def stage_norm(xb, par, aT_, akey, sh0):
                norm_p1(xb, par)
                norm_p2(aT_, akey, sh0)

            def norm_p1(xb, par):
                for s in range(4):
                    xk = ("x", par, s)
                    A(xn[:, s, :], xb[:, s, :], AF.Square, [xk], [("xn", s), ("ss", s)], accum_out=ss[:, s:s + 1])
                    A(lnv[:, s:s + 1], ss[:, s:s + 1], AF.Ln, [("ss", s)], [("lnv", s)], scale=1.0 / D, bias=EPS)
                    A(rstd[:, s:s + 1], lnv[:, s:s + 1], AF.Exp, [("lnv", s)], [("rstd", s)], scale=-0.5)
                    Vsmul(xn[:, s, :], xb[:, s, :], rstd[:, s:s + 1], [xk, ("rstd", s)], [("xn", s)])

            def norm_p2(aT_, akey, sh0):
                shkeys = [("modT", sh0), ("modT", sh0 + 4)]
                for kc in range(8):
                    pb, pbk = bbank()
                    TR([(pb[:, s * 128:(s + 1) * 128], xn[:, s, kc * 128:(kc + 1) * 128]) for s in range(4)],
                       [("xn", s) for s in range(4)], [pbk])
                    a_ap = aT_[:, kc:kc + 1]
                    s_ap = modT[:, sh0 + kc:sh0 + kc + 1]
                    if kc % 2 == 0:
                        Vts(hT[:, kc, :], pb[:, 0:512], a_ap, s_ap, ALU.mult, ALU.add, [pbk, akey] + shkeys, [("hT", kc)])
                    else:
                        A(hT[:, kc, :], pb[:, 0:512], AF.Identity, [pbk, akey] + shkeys, [("hT", kc)], scale=a_ap, bias=s_ap)

            def sigmoid(p_, pk):
                t, tk = tmp()
                A(t[:, :], p_[:, :], AF.Exp, [pk], [tk], scale=-1.0)
                A(t[:, :], t[:, :], AF.Ln, [tk], [tk], bias=1.0)
                A(t[:, :], t[:, :], AF.Exp, [tk], [tk], scale=-1.0)
                return t, tk

            flag_ap = cst[:, C_FLAG:C_FLAG + 1]

            def la_stage():
                la_p1()
                la_p2()

            def la_p1():
                p_, pk = bank()
                MM([(p_[0:16, :], wlr[:, kc, 0:16], hT[:, kc, :], kc == 0, kc == 7) for kc in range(8)],
                   ["wlr"] + HT_ALL, [pk])
                A(lrT[0:16, :], p_[0:16, :], AF.Copy, [pk], ["lrT"])

            def la_p2():
                for s in range(4):
                    p_, pk = bank()
                    MM([(p_[:, :], lrT[0:17, s * 128:(s + 1) * 128], wg[0:17, :], True, True)], ["lrT", "wg"], [pk])
                    t, tk = tmp()
                    A(t[:, :], p_[:, :], AF.Exp, [pk], [tk], scale=-1.0)
                    A(spb[:, s, :], t[:, :], AF.Ln, [tk], [("sp", s)], bias=1.0)

            def gla_stage(main, last_prev, special=False, hoist=None, nxt=None):
                if last_prev:
                    P.op("sp", lambda e: e.dma_start(out=w32q, in_=W["w_in"][:, Q0:Q0 + 512].rearrange("(kc p) c -> p kc c", p=128)),
                         [], ["w32q"] + [("x", 1, s_) for s_ in range(4)], dma="c5")
                    Vcopy(hTh[:, :, :], hT[:, :, 510:512], HT_ALL, ["hTh"])
                if main:
                    wq, wqk = next_w(("w_in", 0, Q0, 512))
                if main:
                    wk_, wkk = next_w(("w_in", 0, K0, 512))
                else:
                    wk_, wkk = wkp, "wkp"
                e2s = []
                pbbs = []
                for h in range(4):
                    hs = slice(h * 128, (h + 1) * 128)
                    pbb, pbbk = bank()
                    MM([(pbb[:, s * 128:(s + 1) * 128], spb[:, s, hs], uneg[:, :], True, True) for s in range(4)],
                       [("sp", s) for s in range(4)] + ["uneg"], [pbbk])
                    pbbs.append((pbb, pbbk))
                for h in range(4):
                    pbb, pbbk = pbbs[h]
                    A(E1[:, h, :], pbb[:, :], AF.Exp, [pbbk], [("E1", h)])
                    e2, e2k = tmp()
                    A(e2[:, :], pbb[:, :], AF.Exp, [pbbk], [e2k], scale=-1.0)
                    e2s.append((e2, e2k))
                if (not main) and nxt is not None:
                    norm_p1(xbuf[nxt % 2], nxt % 2)

                def head_front(h):
                    hs = slice(h * 128, (h + 1) * 128)
                    e2, e2k = e2s[h]
                    if main:
                        pq, pqk = bank()
                        MM([(pq[:, :], wq[:, kc, hs], hT[:, kc, :], kc == 0, kc == 7) for kc in range(8)],
                           [wqk] + HT_ALL, [pqk])
                    pkk_, pkkk = bank()
                    MM([(pkk_[:, :], wk_[:, kc, hs], hT[:, kc, :], kc == 0, kc == 7) for kc in range(8)],
                       [wkk] + HT_ALL, [pkkk])
                    if special:
                        MM([(pq[:, 0:128], w32q[:, kc, hs], h32T[:, kc, :], kc == 0, kc == 7) for kc in range(8)],
                           ["w32q", "h32T", ("x", 1, 0)], [pqk])
                        MM([(pkk_[:, 0:128], w32k[:, kc, hs], h32T[:, kc, :], kc == 0, kc == 7) for kc in range(8)],
                           ["w32k", "h32T"], [pkkk])
                    if main:
                        Vstt(qdT[:, h, :], pq[:, :], 128.0 ** -0.5, E1[:, h, :], ALU.mult, ALU.mult,
                             [pqk, ("E1", h)], [("qdT", h)])
                    Vtt(kdT[:, h, :], pkk_[:, :], e2[:, :], ALU.mult, [pkkk, e2k], [("kdT", h)])
                    if special:
                        Vstt(qd32[:, h, :], pq[:, 0:128], 128.0 ** -0.5, E1[:, h, 0:128], ALU.mult, ALU.mult,
                             [pqk, ("E1", h)], [("qd32", h), "xn32"])
                        Vtt(kd32[:, h, :], pkk_[:, 0:128], e2[:, 0:128], ALU.mult, [pkkk, e2k], [("kd32", h), "xn32"])

                def head_back(h):
                    pb, pbk = bbank()
                    TR([(pb[:, s * 128:(s + 1) * 128], kdT[:, h, s * 128:(s + 1) * 128]) for s in range(4)], [("kdT", h)], [pbk])
                    Vcopy(ke[:, h, :], pb[:, 0:512], [pbk], [("ke", h)])

                for h in range(4):
                    head_front(h)
                    if h >= 1:
                        head_back(h - 1)
                head_back(3)
                if main:
                    done_w()
                    done_w()
                for n in range(2):
                    if main:
                        wv, wvk = next_w(("w_in", 0, V0 + n * 512, 512))
                    else:
                        wv, wvk = wvp[n], f"wvp{n}"
                    for s in range(4):
                        p_, pk = bank()
                        MM([(p_[:, :], hT[:, kc, s * 128:(s + 1) * 128], wv[:, kc, :], kc == 0, kc == 7) for kc in range(8)],
                           [wvk] + HT_ALL, [pk])
                        if s % 2 == 0:
                            Vcopy(v_sb[:, s, n * 512:(n + 1) * 512], p_[:, :], [pk], [("v", s)])
                        else:
                            A(v_sb[:, s, n * 512:(n + 1) * 512], p_[:, :], AF.Copy, [pk], [("v", s)])
                    if main:
                        done_w()
                if last_prev:
                    P.op("sp", lambda e: e.dma_start(out=w32k, in_=W["w_in"][:, K0:K0 + 512].rearrange("(kc p) c -> p kc c", p=128)),
                         [], ["w32k"], dma="c6")
                if (not main) and nxt is not None:
                    norm_p2(a1T, "a1T", 0)
                    la_p1()
                if main:
                    for n in range(2):
                        wgg, wggk = next_w(("w_in", 0, G0 + n * 512, 512))
                        for s in range(4):
                            p_, pk = bank()
                            MM([(p_[:, :], hT[:, kc, s * 128:(s + 1) * 128], wgg[:, kc, :], kc == 0, kc == 7) for kc in range(8)],
                               [wggk] + HT_ALL, [pk])
                            r_, rk = sigmoid(p_, pk)
                            Vtt(sg[:, s, n * 512:(n + 1) * 512], p_[:, :], r_[:, :], ALU.mult, [pk, rk],
                                [("sg", s, 2 * n), ("sg", s, 2 * n + 1)])
                        done_w()
                def phaseA(s):
                    cs = slice(s * 128, (s + 1) * 128)
                    st = {}
                    if main:
                        psc, psck = bank()
                        if special and s == 0:
                            MM([(psc[:, h * 128:(h + 1) * 128], kd32[:, h, :], qd32[:, h, :], True, True) for h in range(4)],
                               [("kd32", h) for h in range(4)] + [("qd32", h) for h in range(4)], [psck])
                        else:
                            MM([(psc[:, h * 128:(h + 1) * 128], kdT[:, h, cs], qdT[:, h, cs], True, True) for h in range(4)],
                               [("kdT", h) for h in range(4)] + [("qdT", h) for h in range(4)], [psck])
                        sc, sck = scbuf()
                        Vtt(sc[:, :], psc[:, :], cmask4[:, :], ALU.mult, [psck, "cmask4"], [sck])
                        st["sc"] = (sc, sck)
                    pts = []
                    for hp in range(2):
                        pt, ptk = bank()
                        MM([(pt[:, j * 256:(j + 1) * 256], ke[:, hp * 2 + j, cs],
                             v_sb[:, s, (hp * 2 + j) * 256:(hp * 2 + j + 1) * 256], True, True) for j in range(2)],
                           [("ke", hp * 2), ("ke", hp * 2 + 1), ("v", s)], [ptk])
                        pts.append((pt, ptk))
                    st["pts"] = pts
                    return st

                def supd(s, st):
                    for h in range(4):
                        pt, ptk = st["pts"][h // 2]
                        Vtt(S32[:, h, :], S32[:, h, :], pt[:, (h % 2) * 256:(h % 2 + 1) * 256], ALU.add,
                            [("S32", h), ptk], [("S32", h)])
                        Vsmul(S32[:, h, :], S32[:, h, :], E1[:, h, s * 128 + 127:s * 128 + 128],
                              [("S32", h), ("E1", h)], [("S32", h)])

                def o_and_act(s, st):
                    cs = slice(s * 128, (s + 1) * 128)
                    pos = []
                    if main:
                        sc, sck = st["sc"]
                        Sb = S_bfs[s % 2]
                        for hp in range(2):
                            po, pok = bank()
                            mms = []
                            for j in range(2):
                                h = hp * 2 + j
                                mms.append((po[:, j * 256:(j + 1) * 256], sc[:, h * 128:(h + 1) * 128],
                                            v_sb[:, s, h * 256:(h + 1) * 256], True, False))
                                mms.append((po[:, j * 256:(j + 1) * 256], qdT[:, h, cs], Sb[:, h, :], False, True))
                            MM(mms, [sck, ("v", s), ("qdT", hp * 2), ("qdT", hp * 2 + 1), ("Sbf", s % 2)], [pok])
                            pos.append((po, pok))
                    if main:
                        A(S_bfs[(s + 1) % 2][:, :, :], S32[:, :, :], AF.Copy, [("S32", h) for h in range(4)], [("Sbf", (s + 1) % 2)])
                    if main:
                        for h in range(4):
                            po, pok = pos[h // 2]
                            i = s * 4 + h
                            jt, jtk = tmp()
                            A(jt[:, :].bitcast(BF16)[:, 0:256], po[:, (h % 2) * 256:(h % 2 + 1) * 256], AF.Square, [pok],
                              [("sso", i), jtk], accum_out=sso[:, i:i + 1])
                        A(lno[:, s * 4:(s + 1) * 4], sso[:, s * 4:(s + 1) * 4], AF.Ln, [("sso", s * 4 + h) for h in range(4)],
                          [("lno", s)], scale=1.0 / 256, bias=EPS)
                        A(rso[:, s * 4:(s + 1) * 4], lno[:, s * 4:(s + 1) * 4], AF.Exp, [("lno", s)], [("rso", s)], scale=-0.5)
                    return pos

                def og_stage(s, pos):
                    if main:
                        for h in range(4):
                            po, pok = pos[h // 2]
                            i = s * 4 + h
                            sgs = sg[:, s, h * 256:(h + 1) * 256]
                            Vstt(sgs, po[:, (h % 2) * 256:(h % 2 + 1) * 256], rso[:, i:i + 1], sgs, ALU.mult, ALU.mult,
                                 [pok, ("rso", s), ("sg", s, h)], [("sg", s, h)])

                sts = {0: phaseA(0)}
                pos_prev = None
                for s in range(4):
                    supd(s, sts[s])
                    if s + 1 < 4:
                        sts[s + 1] = phaseA(s + 1)
                    pos = o_and_act(s, sts[s])
                    if pos_prev is not None:
                        og_stage(s - 1, pos_prev)
                    pos_prev = pos
                og_stage(3, pos_prev)
                if last_prev:
                    for h in range(4):
                        Vsmul(S32[:, h, :], S32[:, h, :], flag_ap, [("S32", h), "cst"], [("S32", h)])
                    A(S_bfs[0][:, :, :], S32[:, :, :], AF.Copy, [("S32", h) for h in range(4)], [("Sbf", 0)])
                if not main:
                    if nxt is not None:
                        la_p2()
                    return
                for kc in range(8):
                    pb, pbk = bbank()
                    TR([(pb[:, s * 128:(s + 1) * 128], sg[:, s, kc * 128:(kc + 1) * 128]) for s in range(4)],
                       [("sg", s, kc // 2) for s in range(4)], [pbk])
                    g_ap = cst[:, C_GNW + kc % 2:C_GNW + kc % 2 + 1]
                    if kc % 2:
                        A(ogT[:, kc, :], pb[:, 0:512], AF.Copy, [pbk, "cst"], [("ogT", kc)], scale=g_ap)
                    else:
                        Vsmul(ogT[:, kc, :], pb[:, 0:512], g_ap, [pbk, "cst"], [("ogT", kc)])

            def cw_ap(m, j):
                c = C_CW + m * 3 + j
                return cst[:, c:c + 1]

            def conv_stage(first=False):
                for mg in range(2):
                    wcb, wcbk = next_w(("w_in", 0, CB0 + mg * 512, 512))
                    wcc, wcck = next_w(("w_in", 0, CC0 + mg * 512, 512))
                    wcx, wcxk = next_w(("w_in", 0, CX0 + mg * 512, 512))
                    for mm in range(4):
                        m = mg * 4 + mm
                        ms = slice(mm * 128, (mm + 1) * 128)
                        pc, pck = bank()
                        MM([(pc[:, :], wcc[:, kc, ms], hT[:, kc, :], kc == 0, kc == 7) for kc in range(8)], [wcck] + HT_ALL, [pck])
                        px, pxk = bank()
                        MM([(px[:, :], wcx[:, kc, ms], hT[:, kc, :], kc == 0, kc == 7) for kc in range(8)], [wcxk] + HT_ALL, [pxk])
                        pcb, pcbk = bank()
                        MM([(pcb[:, :], wcb[:, kc, ms], hT[:, kc, :], kc == 0, kc == 7) for kc in range(8)], [wcbk] + HT_ALL, [pcbk])
                        if first:
                            ph, phk = bank()
                            MM([(ph[:, 0:2], wcc[:, kc, ms], hTh[:, kc, :], kc == 0, kc == 7) for kc in range(8)]
                               + [(ph[:, 2:4], wcx[:, kc, ms], hTh[:, kc, :], kc == 0, kc == 7) for kc in range(8)],
                               [wcck, wcxk, "hTh"], [phk])
                            th, thk = tmp()
                            A(th[:, 0:2], ph[:, 0:2], AF.Copy, [phk], [thk])
                            Vtt(th[:, 2:4], th[:, 0:2], ph[:, 2:4], ALU.mult, [thk, phk], [thk])
                            Vsmul(uh[:, m, :], th[:, 2:4], flag_ap, [thk, "cst"], [("uh", m)])
                        t, tk = tmp()
                        A(t[:, :], pc[:, :], AF.Copy, [pck], [tk])
                        u, uk = ubuf()
                        A(u[:, 0:2], uh[:, m, :], AF.Copy, [("uh", m)], [uk])
                        Vtt(u[:, 2:514], t[:, :], px[:, :], ALU.mult, [tk, pxk, uk], [uk])
                        A(uh[:, m, :], u[:, 512:514], AF.Copy, [uk], [("uh", m)])
                        a_, ak = tmp()
                        Vsmul(a_[:, :], u[:, 2:514], cw_ap(m, 2), [uk, "cst"], [ak])
                        Vstt(a_[:, :], u[:, 1:513], cw_ap(m, 1), a_[:, :], ALU.mult, ALU.add, [uk, "cst", ak], [ak])
                        Vstt(a_[:, :], u[:, 0:512], cw_ap(m, 0), a_[:, :], ALU.mult, ALU.add, [uk, "cst", ak], [ak])
                        Vtt(cbuT[:, m, :], a_[:, :], pcb[:, :], ALU.mult, [ak, pcbk], [("cbuT", m)])
                    done_w()
                    done_w()
                    done_w()

            def merge_stage(xb, par):
                OGT_ALL = [("ogT", kc) for kc in range(8)]
                CBU_ALL = [("cbuT", kc) for kc in range(8)]
                for mg in range(2):
                    wa, wak = next_w(("w_pa", 0, mg * 512, 512))
                    wga_, wgak = next_w(("w_in", 0, GA0 + mg * 512, 512))
                    wb, wbk = next_w(("w_pb", 0, mg * 512, 512))
                    wgb, wgbk = next_w(("w_in", 0, GB0 + mg * 512, 512))
                    for mm in range(4):
                        m = mg * 4 + mm
                        ms = slice(mm * 128, (mm + 1) * 128)
                        pya, pyak = bank()
                        MM([(pya[:, :], wa[:, kc, ms], ogT[:, kc, :], kc == 0, kc == 7) for kc in range(8)], [wak] + OGT_ALL, [pyak])
                        pga, pgak = bank()
                        MM([(pga[:, :], wga_[:, kc, ms], hT[:, kc, :], kc == 0, kc == 7) for kc in range(8)], [wgak] + HT_ALL, [pgak])
                        ra, rak = sigmoid(pga, pgak)
                        Vtt(ra[:, :], ra[:, :], pya[:, :], ALU.mult, [rak, pyak], [rak])
                        pyb, pybk = bank()
                        MM([(pyb[:, :], wb[:, kc, ms], cbuT[:, kc, :], kc == 0, kc == 7) for kc in range(8)], [wbk] + CBU_ALL, [pybk])
                        pgb, pgbk = bank()
                        MM([(pgb[:, :], wgb[:, kc, ms], hT[:, kc, :], kc == 0, kc == 7) for kc in range(8)], [wgbk] + HT_ALL, [pgbk])
                        rb, rbk = sigmoid(pgb, pgbk)
                        Vtt(rb[:, :], rb[:, :], pyb[:, :], ALU.mult, [rbk, pybk], [rbk])
                        Vtt(zT[:, m, :], ra[:, :], rb[:, :], ALU.add, [rak, rbk], [("zT", m)])
                    for _ in range(4):
                        done_w()
                ZT_ALL = [("zT", kc) for kc in range(8)]
                for n in range(2):
                    wo, wok = next_w(("w_o", 0, n * 512, 512))
                    ns = slice(n * 512, (n + 1) * 512)
                    for s in range(4):
                        p_, pk = bank()
                        MM([(p_[:, :], zT[:, kc, s * 128:(s + 1) * 128], wo[:, kc, :], kc == 0, kc == 7) for kc in range(8)],
                           [wok] + ZT_ALL, [pk])
                        t, tk = tmp()
                        Vtt(t[:, :], p_[:, :], g1b[:, ns], ALU.mult, [pk, "g1b"], [tk])
                        xk = ("x", par, s)
                        Vtt(xb[:, s, ns], xb[:, s, ns], t[:, :], ALU.add, [xk, tk], [xk])
                    done_w()

            def mlp_stage(xb, par, g, hoist=None, nxt=None):
                stage_norm(xb, par, a2T, "a2T", 16)
                if nxt is not None:
                    norm_p1(xbuf[nxt % 2], nxt % 2)
                for cg in range(8):
                    w1_, w1k = next_w(("w1", 0, cg * 512, 512))
                    for jj in range(4):
                        j = cg * 4 + jj
                        p_, pk = bank()
                        MM([(p_[:, :], w1_[:, kc, jj * 128:(jj + 1) * 128], hT[:, kc, :], kc == 0, kc == 7) for kc in range(8)],
                           [w1k] + HT_ALL, [pk])
                        t, tk = tmp()
                        A(t[:, :], p_[:, :], AF.Relu, [pk], [tk])
                        Vtt(aT[:, j, :], t[:, :], t[:, :], ALU.mult, [tk], [("aT", j)])
                    done_w()
                for n in range(2):
                    ns = slice(n * 512, (n + 1) * 512)
                    bks = [bank() for _ in range(4)]
                    for jg in range(4):
                        w2_, w2k = next_w(("w2", jg * 1024, n * 512, 512))
                        for s in range(4):
                            MM([(bks[s][0][:, :], aT[:, jg * 8 + jj, s * 128:(s + 1) * 128], w2_[:, jj, :],
                                 jg == 0 and jj == 0, jg == 3 and jj == 7) for jj in range(8)],
                               [w2k] + [("aT", jg * 8 + jj) for jj in range(8)], [bks[s][1]])
                        done_w()
                    for s in range(4):
                        t, tk = tmp()
                        Vtt(t[:, :], bks[s][0][:, :], g2b[:, ns], ALU.mult, [bks[s][1], "g2b"], [tk])
                        xk = ("x", par, s)
                        Vtt(xb[:, s, ns], xb[:, s, ns], t[:, :], ALU.add, [xk, tk], [xk])
                    if n == 0 and nxt is not None:
                        norm_p2(a1T, "a1T", 0)
                        la_p1()
                        la_p2()
                    if n == 0:
                        P.op("sp", lambda e: e.dma_start(out=fnw, in_=fnwb_d), [], [("xn", 0), ("xn", 1)], dma="c4")
                for s in range(4):
                    xk = ("x", par, s)
                    jt, jtk = tmp()
                    A(jt[:, :].bitcast(BF16), xb[:, s, :], AF.Square, [xk], [jtk, ("ss2", s)], accum_out=ss2[:, s:s + 1])
                    A(lnv2[:, s:s + 1], ss2[:, s:s + 1], AF.Ln, [("ss2", s)], [("lnv2", s)], scale=1.0 / D, bias=EPS)
                    A(rstd2[:, s:s + 1], lnv2[:, s:s + 1], AF.Exp, [("lnv2", s)], [("rstd2", s)], scale=-0.5)
                    A(xb[:, s, :], xb[:, s, :], AF.Copy, [xk, ("rstd2", s)], [xk], scale=rstd2[:, s:s + 1])
                    Vtt(xb[:, s, :], xb[:, s, :], fnw, ALU.mult, [xk, ("xn", 0), ("xn", 1)], [xk])
                    r0 = (g % 4) * T + s * 128
                    P.op("sp", lambda e, r0=r0, s=s: e.dma_start(out=out_d[r0:r0 + 128, :], in_=xb[:, s, :]), [xk],
                         [("out", g, s)], dma=f"o{par}")

            def special_prep(xb, par):
                Vsmul(xn32, xb[:, 0, :], rstd[:, 0:1], [("x", par, 0), ("rstd", 0)], ["xn32"])
                for half in range(2):
                    p_, pk = bank()
                    def fn(e, p_=p_, half=half):
                        ins = None
                        for j in range(4):
                            kc = half * 4 + j
                            ins = e.transpose(out=p_[:, j * 128:(j + 1) * 128], in_=xn32[:, kc * 128:(kc + 1) * 128],
                                              identity=identf[:, :])
                        return ins
                    P.op("pe", fn, ["xn32", "identf"], [pk])
                    for j in range(4):
                        kc = half * 4 + j
                        Vts(h32T[:, kc, :], p_[:, j * 128:(j + 1) * 128], a1T[:, kc:kc + 1], modT[:, kc:kc + 1],
                            ALU.mult, ALU.add, [pk, "a1T", ("modT", 0), ("modT", 4)], ["h32T"])

            for ci in range(4):
                ada_chunk(ci)
            mod_finish(1)
            rest = [4, 5, 6, 7, 8, 9, 10, 11]
            for g in range(8):
                par = g % 2
                xb = xbuf[par]
                if g >= 4:
                    P.wmode = "t4" if g == 4 else "t5" if g == 5 else "scr"
                    P.widx = 0
                if g == 0:
                    stage_norm(xb, par, a1T, "a1T", 0)
                    la_stage()
                nxt = g + 1 if g + 1 < 8 else None
                if g < 3:
                    rec_xload(g + 2)
                if g < 4:
                    gla_stage(False, g == 3, nxt=nxt)
                    for ci in rest[g * 2:g * 2 + 2]:
                        ada_chunk(ci)
                    rec_convs(2 if g < 3 else 100)
                    if g == 3:
                        mod_finish(2)
                else:
                    if g == 4:
                        special_prep(xb, par)
                    gla_stage(True, False, special=(g == 4))
                    if g == 4:
                        rec_xload(5)
                    conv_stage(first=(g == 4))
                    merge_stage(xb, par)
                    mlp_stage(xb, par, g, nxt=nxt)
                if g + 2 < 8 and g >= 4:
                    rec_xload(g + 2)
            P.op("sp", None, [("out", g, s_) for g in range(4, 8) for s_ in range(4)], [])

        wplan = []
        record(Prog(True, wplan))
        P = Prog(False, wplan)
        record(P)
        assert P.wi == len(wplan), (P.wi, len(wplan))

        sem_names = P.sems()
        S = {n: es.enter_context(nc.semaphore(n)) for n in sem_names}
        block = es.enter_context(nc.Block())

        def run_stream(eng_name):
            def body(e):
                for (waits, fn, sem, amt) in P.streams[eng_name]:
                    for (s_, v_) in waits:
                        e.wait_ge(S[s_], v_)
                    if fn is not None:
                        fn(e).then_inc(S[sem], amt)
            return body

        block.sync(run_stream("sp"))
        block.gpsimd(run_stream("pool"))
        block.tensor(run_stream("pe"))
        block.scalar(run_stream("act"))
        block.vector(run_stream("dve"))
    return nc


_NC = None


def kernel(x, c, w_ada, b_ada, norm1_w, w_in, w_gate_up, b_gate, gla_norm_w, conv_w,
           w_proj_a, w_proj_b, w_out, norm2_w, w_mlp1, w_mlp2, final_norm_w):
    global _NC
    f = lambda a: np.ascontiguousarray(np.asarray(a, dtype=np.float32))
    x = f(x)
    c = f(c)
    b_ada = f(b_ada)[0]
    shared = {
        "w_ada": f(w_ada)[0], "w_in": f(w_in)[0], "w_pa": f(w_proj_a)[0], "w_pb": f(w_proj_b)[0],
        "w_o": f(w_out)[0], "w1": f(w_mlp1)[0], "w2": f(w_mlp2)[0],
        "fnw_b": np.ascontiguousarray(np.broadcast_to(f(final_norm_w)[None, :], (128, D))),
        "bgate_b": np.ascontiguousarray(np.broadcast_to(
            np.concatenate([b_ada[2 * D:3 * D], b_ada[5 * D:6 * D]])[None, :], (128, 2 * D))),
        "wg_aug": np.ascontiguousarray(np.concatenate([f(w_gate_up)[0], f(b_gate)[0][None, :]], axis=0)),
    }
    colT = lambda v: np.ascontiguousarray(v.reshape(-1, 128).T)
    cbase = np.zeros((128, NCONST), np.float32)
    cbase[:, C_BADA:C_BADA + 32] = np.concatenate(
        [colT(b_ada[0:D]), colT(b_ada[D:2 * D]), colT(b_ada[3 * D:4 * D]), colT(b_ada[4 * D:5 * D])], axis=1)
    cbase[:, C_N1:C_N1 + 8] = colT(f(norm1_w)[0])
    cbase[:, C_N2:C_N2 + 8] = colT(f(norm2_w)[0])
    cwl = f(conv_w)[0]
    cbase[:, C_CW:C_CW + 24] = np.transpose(cwl.reshape(3, 8, 128), (2, 1, 0)).reshape(128, 24)
    cbase[:, C_GNW:C_GNW + 2] = colT(f(gla_norm_w)[0])
    if _NC is None:
        _NC = build_nc()
    in_maps = []
    for i in range(8):
        b, hf = i // 2, i % 2
        cs = cbase.copy()
        cs[:, C_CT:C_CT + 8] = colT(c[b])
        cs[:, C_FLAG] = float(hf)
        m = dict(shared)
        m["x_cur"] = np.ascontiguousarray(x[b, hf * TOK:(hf + 1) * TOK])
        m["x_prev"] = np.ascontiguousarray(x[b, 0:TOK])
        m["consts"] = cs
        in_maps.append(m)
    res = run_bass_kernel_spmd(_NC, in_maps, core_ids=list(range(8)))
    out = np.empty((4, 2 * TOK, D), np.float32)
    for i in range(8):
        b, hf = i // 2, i % 2
        out[b, hf * TOK:(hf + 1) * TOK] = np.asarray(res.results[i]["out"]).reshape(TOK, D)
    return out
```

```python
import numpy as np
from contextlib import ExitStack
import concourse.bass as bass
import concourse.mybir as mybir
from concourse.bass_utils import run_bass_kernel_spmd

F32 = mybir.dt.float32
BF16 = mybir.dt.bfloat16
AF = mybir.ActivationFunctionType
ALU = mybir.AluOpType

D = 1024
TOK = 2048
T = 512
NW = 7
NTMP = 7
NSCR = 40
PRECONV = ("w2",)
EPS = 1e-6
Q0, K0, V0, G0, LR0, CB0, CC0, CX0, GA0, GB0 = 0, 512, 1024, 2048, 3072, 3088, 4112, 5136, 6160, 7184
C_CT, C_BADA, C_N1, C_N2, C_CW, C_GNW, C_FLAG, NCONST = 0, 8, 40, 48, 56, 80, 82, 84


class Ev:
    __slots__ = ("eng", "sem", "val", "know", "dma")

    def __init__(self, eng, sem, val, know, dma):
        self.eng, self.sem, self.val, self.know, self.dma = eng, sem, val, know, dma


class Prog:
    ENGS = ("pe", "act", "dve", "pool", "sp")

    def __init__(self, dry, wplan):
        self.dry = dry
        self.wplan = wplan
        self.streams = {e: [] for e in self.ENGS}
        self.cnt = {e: 0 for e in self.ENGS}
        self.dcnt = {}
        self.know = {e: {} for e in self.ENGS}
        self.last_w = {}
        self.readers = {}
        self.groups = {}
        self.cur_view = {}
        self.barrier = {}
        self.name_keys = {}
        self.bi = 0
        self.bbi = 0
        self.ti = 0
        self.wi = 0
        self.wdone = 0
        self.wmode = "cast"
        self.widx = -1

    def op(self, eng, fn, reads=(), writes=(), dma=None):
        if self.dry:
            return
        deps = []
        for k in list(reads) + list(writes):
            name = k[0] if isinstance(k, tuple) else k
            self.name_keys.setdefault(name, set()).add(k)
            for g in self.groups.get(name, ()):
                if self.cur_view.get(g) != name:
                    old = self.cur_view.get(g)
                    evs = []
                    if old is not None:
                        for kk in self.name_keys.get(old, ()):
                            if kk in self.last_w:
                                evs.append(self.last_w[kk])
                            evs += self.readers.get(kk, [])
                    self.barrier[g] = evs
                    self.cur_view[g] = name
                deps += self.barrier.get(g, [])
        is_dma = dma is not None
        for k in reads:
            ev = self.last_w.get(k)
            if ev is not None:
                deps.append(ev)
        for k in writes:
            ev = self.last_w.get(k)
            if ev is not None and (is_dma or ev.dma or ev.eng != eng or eng != "pe"):
                deps.append(ev)
            for ev in self.readers.get(k, []):
                if is_dma or ev.dma or ev.eng != eng or eng != "pe":
                    deps.append(ev)
        kn = self.know[eng]
        waits = {}
        for ev in deps:
            if kn.get(ev.sem, 0) >= ev.val:
                continue
            waits[ev.sem] = max(waits.get(ev.sem, 0), ev.val)
            for s_, v_ in ev.know.items():
                if kn.get(s_, 0) < v_:
                    kn[s_] = v_
        if fn is None:
            self.streams[eng].append((list(waits.items()), None, None, 0))
            return
        if is_dma:
            sem = dma
            self.dcnt[sem] = self.dcnt.get(sem, 0) + 16
            val = self.dcnt[sem]
            amt = 16
        else:
            sem = "E_" + eng
            self.cnt[eng] += 1
            val = self.cnt[eng]
            amt = 1
        evk = dict(kn)
        evk[sem] = val
        ev = Ev(eng, sem, val, evk, is_dma)
        for k in reads:
            self.readers.setdefault(k, []).append(ev)
        for k in writes:
            self.last_w[k] = ev
            self.readers[k] = []
        self.streams[eng].append((list(waits.items()), fn, sem, amt))

    def sems(self):
        s = {"E_" + e for e in self.ENGS}
        s |= set(self.dcnt.keys())
        return sorted(s)


def build_nc():
    nc = bass.Bass("TRN2", target_bir_lowering=False)

    def din(name, shape):
        return nc.dram_tensor(name, shape, F32, kind="ExternalInput").ap()

    x_cur = din("x_cur", [TOK, D])
    x_prev = din("x_prev", [TOK, D])
    W = {
        "w_ada": din("w_ada", [D, 6 * D]),
        "w_in": din("w_in", [D, 8208]),
        "w_pa": din("w_pa", [D, D]),
        "w_pb": din("w_pb", [D, D]),
        "w_o": din("w_o", [D, D]),
        "w1": din("w1", [D, 4 * D]),
        "w2": din("w2", [4 * D, D]),
    }
    consts_d = din("consts", [128, NCONST])
    fnwb_d = din("fnw_b", [128, D])
    bgb_d = din("bgate_b", [128, 2 * D])
    wga_d = din("wg_aug", [17, 512])
    out_d = nc.dram_tensor("out", [TOK, D], F32, kind="ExternalOutput").ap()
    wscr = nc.dram_tensor("wscr", [NSCR, 128, 4096], BF16, kind="Internal").ap()

    with ExitStack() as es:
        def sb(name, shape, dt):
            return es.enter_context(nc.sbuf_tensor(name, shape, dt))

        xbuf = [sb(f"xbuf{i}", [128, 4, D], F32) for i in range(2)]
        xn = sb("xn", [128, 4, D], BF16)
        hT = sb("hT", [128, 8, T], BF16)
        wsl = [sb(f"wsl{i}", [128, 8, 512], BF16) for i in range(NW)]
        wlr = sb("wlr", [128, 8, 16], BF16)
        lrT = sb("lrT", [32, T], F32)
        wg = sb("wg", [32, 512], F32)
        spb = sb("spb", [128, 4, 512], F32)
        E1 = spb
        spc = sb("spc", [128, 4, 512], F32)
        qdT = sb("qdT", [128, 4, T], BF16)
        kdT = sb("kdT", [128, 4, T], BF16)
        ke = sb("ke", [128, 4, T], BF16)
        big = sb("big", [128, 16384], BF16)
        S32 = sb("S32", [128, 4, 256], F32)
        _sbf = sb("S_bf0", [128, 4, 256], BF16)
        S_bfs = [_sbf, _sbf]
        scb = [sb(f"scb{i}", [128, 512], BF16) for i in range(2)]
        cmask4 = sb("cmask4", [128, 512], BF16)
        ss2 = sb("ss2", [128, 4], F32)
        lnv2 = sb("lnv2", [128, 4], F32)
        rstd2 = sb("rstd2", [128, 4], F32)
        ub = [sb(f"ub{i}", [128, 514], F32) for i in range(2)]
        uh = sb("uh", [128, 8, 2], F32)
        hTh = sb("hTh", [128, 8, 2], BF16)
        tmps = [sb(f"tmp{i}", [128, 512], F32) for i in range(NTMP)]
        cst = sb("cst", [128, NCONST], F32)
        g1b = sb("g1b", [128, D], F32)
        g2b = sb("g2b", [128, D], F32)
        identf = sb("identf", [128, 128], F32)
        ident = sb("ident", [128, 128], BF16)
        ones_bf = sb("ones_bf", [128, 128], BF16)
        uneg = sb("uneg", [128, 128], F32)
        ce = sb("ce", [128, 8], F32)
        cact = sb("cact", [128, 8], F32)
        cact_bf = sb("cact_bf", [128, 8], BF16)
        modT = sb("modT", [128, 32], F32)
        a1T = sb("a1T", [128, 8], F32)
        a2T = sb("a2T", [128, 8], F32)
        ss = sb("ss", [128, 4], F32)
        lnv = sb("lnv", [128, 4], F32)
        rstd = sb("rstd", [128, 4], F32)
        sso = sb("sso", [128, 16], F32)
        lno = sb("lno", [128, 16], F32)
        rso = sb("rso", [128, 16], F32)

        psf = [es.enter_context(nc.psum_tensor(f"psf{i}", [128, 512], F32)) for i in range(8)]
        psb = [p_[:, :].bitcast(BF16) for p_ in psf]

        v_sb = big[:, 0:4096].rearrange("p (s c) -> p s c", s=4)
        ogT = big[:, 0:4096].rearrange("p (k c) -> p k c", k=8)
        sg = big[:, 4096:8192].rearrange("p (s c) -> p s c", s=4)
        cbuT = big[:, 8192:12288].rearrange("p (k c) -> p k c", k=8)
        zT = big[:, 12288:16384].rearrange("p (k c) -> p k c", k=8)
        aT = big[:, :].rearrange("p (j c) -> p j c", j=32)
        wkp = big[:, 4096:8192].rearrange("p (k c) -> p k c", k=8)
        wvp = [big[:, 8192:12288].rearrange("p (k c) -> p k c", k=8),
               big[:, 12288:16384].rearrange("p (k c) -> p k c", k=8)]
        cbm = qdT[:, 0:2, :].rearrange("p a (k c) -> p (a k) c", k=4)
        w32q = xbuf[1][:, :, :].rearrange("p s (a c) -> p (s a) c", a=2)
        w32k = big[:, 8192:16384].bitcast(F32).rearrange("p (k c) -> p k c", k=8)
        fnw = xn[:, 0:2, :].rearrange("p a c -> p (a c)").bitcast(F32)
        h32T = spc[:, 0:2, :].rearrange("p a (k c) -> p (a k) c", k=4)
        xn32 = spc[:, 2:4, :].rearrange("p a c -> p (a c)")
        qd32 = spc[:, 2, :].rearrange("p (h c) -> p h c", h=4)
        kd32 = spc[:, 3, :].rearrange("p (h c) -> p h c", h=4)

        def record(P):
            P.groups = {"v": ["A"], "ogT": ["A"], "sg": ["B"], "cbuT": ["C"], "zT": ["Dg"],
                        "aT": ["A", "B", "C", "Dg"], "w32k": ["C", "Dg"],
                        "wkp": ["B"], "wvp0": ["C"], "wvp1": ["Dg"],
                        "cbm": ["Q"], "qdT": ["Q"], "sp": ["SE"], "E1": ["SE"]}

            def A(out, in_, func, r, w, **kw):
                P.op("act", lambda e: e.activation(out=out, in_=in_, func=func, **kw), r, w)

            def Vtt(out, a, b, op, r, w):
                P.op("dve", lambda e: e.tensor_tensor(out=out, in0=a, in1=b, op=op), r, w)

            def Vts(out, a, s1, s2, op0, op1, r, w):
                P.op("dve", lambda e: e.tensor_scalar(out=out, in0=a, scalar1=s1, scalar2=s2, op0=op0, op1=op1), r, w)

            def Vsmul(out, a, s, r, w):
                P.op("dve", lambda e: e.tensor_scalar_mul(out=out, in0=a, scalar1=s), r, w)

            def Vsadd(out, a, s, r, w):
                P.op("dve", lambda e: e.tensor_scalar_add(out=out, in0=a, scalar1=s), r, w)

            def Vstt(out, a, s, b, op0, op1, r, w):
                P.op("dve", lambda e: e.scalar_tensor_tensor(out=out, in0=a, scalar=s, in1=b, op0=op0, op1=op1), r, w)

            def Vcopy(out, a, r, w):
                P.op("dve", lambda e: e.tensor_copy(out=out, in_=a), r, w)

            def Vrecip(out, a, r, w):
                P.op("dve", lambda e: e.reciprocal(out=out, in_=a), r, w)

            def MM(mms, r, w):
                def fn(e):
                    ins = None
                    for (o, l, rh, st, sp_) in mms:
                        ins = e.matmul(out=o, lhsT=l, rhs=rh, start=st, stop=sp_)
                    return ins
                P.op("pe", fn, r, w)

            def TR(trs, r, w):
                def fn(e):
                    ins = None
                    for (o, i) in trs:
                        ins = e.transpose(out=o, in_=i, identity=ident[:])
                    return ins
                P.op("pe", fn, list(r) + ["ident"], w)

            def bank():
                i = P.bi % 8
                P.bi += 1
                return psf[i], ("ps", i)

            def bbank():
                i = P.bi % 8
                P.bi += 1
                return psb[i], ("ps", i)

            def tmp():
                i = P.ti % NTMP
                P.ti += 1
                return tmps[i], ("tmp", i)

            sci = [0]

            def scbuf():
                i = sci[0] % 2
                sci[0] += 1
                return scb[i], ("sc", i)

            ubi = [0]

            def ubuf():
                i = ubi[0] % 2
                ubi[0] += 1
                return ub[i], ("u", i)

            def rec_load(j):
                wname, r0, c0, ncols, mode, idx = P.wplan[j]
                slot = j % NW
                if mode == "scr":
                    src = wscr[idx]
                    dst = wsl[slot][:, :, :].rearrange("p k c -> p (k c)")
                    P.op("pool", lambda e: e.dma_start(out=dst, in_=src), [("scr", idx)], [("w", slot)], dma=f"w{slot}")
                else:
                    src = W[wname][r0:r0 + 1024, c0:c0 + ncols].rearrange("(kc p) c -> p kc c", p=128)
                    dst = wsl[slot][:, :, 0:ncols]
                    P.op("pool", lambda e: e.dma_start(out=dst, in_=src), [], [("w", slot)], dma=f"w{slot}")
                if mode == "castwb":
                    wsrc = wsl[slot][:, :, :].rearrange("p k c -> p (k c)")
                    wdst = wscr[idx]
                    P.op("sp", lambda e: e.dma_start(out=wdst, in_=wsrc), [("w", slot)], [("scr", idx)], dma=f"wb{slot}")

            def is_preconv(spec):
                return spec[0] in PRECONV

            def next_w(spec):
                mode = P.wmode
                if mode in ("t4", "t5"):
                    if is_preconv(spec):
                        mode = "scr"
                    elif P.widx % 2 == 0:
                        mode = "castwb" if mode == "t4" else "scr"
                    else:
                        mode = "cast4" if mode == "t4" else "castwb"
                full = tuple(spec) + (mode, P.widx)
                if P.wmode != "cast":
                    P.widx += 1
                if P.dry:
                    P.wplan.append(full)
                    return wsl[0], ("w", 0)
                i = P.wi
                P.wi += 1
                assert P.wplan[i] == full, (i, P.wplan[i], full)
                return wsl[i % NW], ("w", i % NW)

            def done_w():
                if P.dry:
                    return
                j = P.wdone + NW
                P.wdone += 1
                if j < len(P.wplan):
                    rec_load(j)

            HT_ALL = [("hT", kc) for kc in range(8)]

            conv_list = [] if P.dry else [e_ for e_ in P.wplan if e_[4] == "scr" and is_preconv(e_) and e_[5] >= 0]
            seen_cv = set()
            conv_todo = []
            for e_ in conv_list:
                if e_[5] not in seen_cv:
                    seen_cv.add(e_[5])
                    conv_todo.append(e_)

            def rec_convs(n):
                for _ in range(n):
                    if not conv_todo:
                        return
                    wname, r0, c0, ncols, _m, idx = conv_todo.pop(0)
                    src = W[wname][r0:r0 + 1024, c0:c0 + ncols].rearrange("(kc p) c -> p kc c", p=128)
                    dst = wscr[idx].rearrange("p (k c) -> p k c", k=8)
                    P.op("pool", lambda e, dst=dst, src=src: e.dma_start(out=dst, in_=src), [], [("scr", idx)], dma=f"cv{idx}")

            P.op("sp", lambda e: e.dma_start(out=cst[:, :], in_=consts_d), [], ["cst"], dma="c0")
            P.op("sp", lambda e: e.dma_start(out=wg[0:17, :], in_=wga_d), [], ["wg"], dma="c1")
            P.op("sp", lambda e: e.dma_start(out=g1b[:, :], in_=bgb_d[:, 0:D]), [], ["g1b"], dma="c2")
            P.op("sp", lambda e: e.dma_start(out=g2b[:, :], in_=bgb_d[:, D:2 * D]), [], ["g2b"], dma="c3")

            P.op("pool", lambda e: e.memset(identf[:, :], 0.0), [], ["identf"])
            P.op("pool", lambda e: e.affine_select(out=identf[:, :], in_=identf[:, :], pattern=[[-1, 128]],
                                                   compare_op=ALU.not_equal, fill=1.0, base=0, channel_multiplier=1),
                 ["identf"], ["identf"])
            P.op("pool", lambda e: e.memset(cmask4[:, :], 1.0), [], ["cmask4"])
            P.op("pool", lambda e: e.affine_select(out=cmask4[:, :], in_=cmask4[:, :], pattern=[[0, 4], [1, 128]],
                                                   compare_op=ALU.is_ge, fill=0.0, base=0, channel_multiplier=-1),
                 ["cmask4"], ["cmask4"])
            P.op("pool", lambda e: e.memset(uneg[:, :], -1.0 / 16.0), [], ["uneg"])
            P.op("pool", lambda e: e.affine_select(out=uneg[:, :], in_=uneg[:, :], pattern=[[1, 128]],
                                                   compare_op=ALU.is_ge, fill=0.0, base=0, channel_multiplier=-1),
                 ["uneg"], ["uneg"])
            P.op("pool", lambda e: e.memset(lrT[:, :], 1.0), [], ["lrT"])
            P.op("pool", lambda e: e.dma_start(out=wlr[:, :, :], in_=W["w_in"][:, LR0:LR0 + 16].rearrange("(kc p) c -> p kc c", p=128)),
                 [], ["wlr"], dma="c7")
            if not P.dry:
                for j in range(min(4, len(P.wplan))):
                    rec_load(j)
            P.op("pool", lambda e: e.dma_start(out=wkp, in_=W["w_in"][:, K0:K0 + 512].rearrange("(kc p) c -> p kc c", p=128)),
                 [], ["wkp"], dma="c8")
            for n_ in range(2):
                P.op("pool", lambda e, n_=n_: e.dma_start(out=wvp[n_], in_=W["w_in"][:, V0 + n_ * 512:V0 + (n_ + 1) * 512].rearrange("(kc p) c -> p kc c", p=128)),
                     [], [f"wvp{n_}"], dma=f"c{9 + n_}")
            if not P.dry:
                for j in range(4, min(NW, len(P.wplan))):
                    rec_load(j)

            Vcopy(ident[:, :], identf[:, :], ["identf"], ["ident"])
            P.op("dve", lambda e: e.memset(ones_bf[:, :], 1.0), [], ["ones"])
            P.op("dve", lambda e: e.memset(S32[:, :, :], 0.0), [], [("S32", h) for h in range(4)])
            P.op("dve", lambda e: e.memset(S_bfs[0][:, :, :], 0.0), [], [("Sbf", 0)])
            P.op("dve", lambda e: e.memset(uh[:, :, :], 0.0), [], [("uh", m) for m in range(8)])

            def rec_xload(g):
                src_t = x_prev if g < 4 else x_cur
                t0 = (g % 4) * T
                par = g % 2
                src = src_t[t0:t0 + T, :].rearrange("(s p) d -> p s d", p=128)
                P.op("sp", lambda e: e.dma_start(out=xbuf[par][:, :, :], in_=src), [],
                     [("x", par, s) for s in range(4)], dma=f"x{par}")

            rec_xload(0)
            rec_xload(1)

            cT = cst[:, C_CT:C_CT + 8]
            A(ce[:, :], cT, AF.Exp, ["cst"], ["ce"], scale=-1.0)
            Vsadd(ce[:, :], ce[:, :], 1.0, ["ce"], ["ce"])
            Vrecip(ce[:, :], ce[:, :], ["ce"], ["ce"])
            Vtt(cact[:, :], ce[:, :], cT, ALU.mult, ["ce", "cst"], ["cact"])
            Vcopy(cact_bf[:, :], cact[:, :], ["cact"], ["cactbf"])
            for kc in range(8):
                Vsmul(cbm[:, kc, :], ones_bf[:, :], cact[:, kc:kc + 1], ["ones", "cact"], [("cbm", kc)])

            def ada_chunk(ci):
                wt, wk = next_w(("w_ada", 0, ci * 512, 512))
                fm = {0: 0, 1: 0, 2: 1, 3: 1, 6: 2, 7: 2, 8: 3, 9: 3}
                if ci in fm:
                    j0 = fm[ci] * 8 + (ci % 2) * 4
                    p_, pk = bank()
                    mms = []
                    for jj in range(4):
                        for kc in range(8):
                            mms.append((p_[:, jj:jj + 1], wt[:, kc, jj * 128:(jj + 1) * 128], cact_bf[:, kc:kc + 1],
                                        kc == 0, kc == 7))
                    MM(mms, [wk, "cactbf"], [pk])
                    Vtt(modT[:, j0:j0 + 4], p_[:, 0:4], cst[:, C_BADA + j0:C_BADA + j0 + 4], ALU.add,
                        [pk, "cst"], [("modT", j0)])
                else:
                    gb_ = g1b if ci in (4, 5) else g2b
                    gk = "g1b" if ci in (4, 5) else "g2b"
                    hs_ = slice((ci % 2) * 512, (ci % 2) * 512 + 512)
                    p_, pk = bank()
                    MM([(p_[:, :], cbm[:, kc, :], wt[:, kc, :], kc == 0, kc == 7) for kc in range(8)],
                       [wk] + [("cbm", kc) for kc in range(8)], [pk])
                    Vtt(gb_[:, hs_], p_[:, :], gb_[:, hs_], ALU.add, [pk, gk], [gk])
                done_w()

            def mod_finish(which):
                sc0 = 8 if which == 1 else 24
                nw0 = C_N1 if which == 1 else C_N2
                dst = a1T if which == 1 else a2T
                key = "a1T" if which == 1 else "a2T"
                Vsadd(dst[:, :], modT[:, sc0:sc0 + 8], 1.0, [("modT", sc0), ("modT", sc0 + 4)], [key])
                Vtt(dst[:, :], dst[:, :], cst[:, nw0:nw0 + 8], ALU.mult, [key, "cst"], [key])

            def stage_norm(xb, par, aT_, akey, sh0):
                norm_p1(xb, par)
                norm_p2(aT_, akey, sh0)

            def norm_p1(xb, par):
                for s in range(4):
                    xk = ("x", par, s)
                    A(xn[:, s, :], xb[:, s, :], AF.Square, [xk], [("xn", s), ("ss", s)], accum_out=ss[:, s:s + 1])
                    A(lnv[:, s:s + 1], ss[:, s:s + 1], AF.Ln, [("ss", s)], [("lnv", s)], scale=1.0 / D, bias=EPS)
                    A(rstd[:, s:s + 1], lnv[:, s:s + 1], AF.Exp, [("lnv", s)], [("rstd", s)], scale=-0.5)
                    Vsmul(xn[:, s, :], xb[:, s, :], rstd[:, s:s + 1], [xk, ("rstd", s)], [("xn", s)])

            def norm_p2(aT_, akey, sh0):
                shkeys = [("modT", sh0), ("modT", sh0 + 4)]
                for kc in range(8):
                    pb, pbk = bbank()
                    TR([(pb[:, s * 128:(s + 1) * 128], xn[:, s, kc * 128:(kc + 1) * 128]) for s in range(4)],
                       [("xn", s) for s in range(4)], [pbk])
                    a_ap = aT_[:, kc:kc + 1]
                    s_ap = modT[:, sh0 + kc:sh0 + kc + 1]
                    if kc % 2 == 0:
                        Vts(hT[:, kc, :], pb[:, 0:512], a_ap, s_ap, ALU.mult, ALU.add, [pbk, akey] + shkeys, [("hT", kc)])
                    else:
                        A(hT[:, kc, :], pb[:, 0:512], AF.Identity, [pbk, akey] + shkeys, [("hT", kc)], scale=a_ap, bias=s_ap)

            def sigmoid(p_, pk):
                t, tk = tmp()
                A(t[:, :], p_[:, :], AF.Exp, [pk], [tk], scale=-1.0)
                A(t[:, :], t[:, :], AF.Ln, [tk], [tk], bias=1.0)
                A(t[:, :], t[:, :], AF.Exp, [tk], [tk], scale=-1.0)
                return t, tk

            flag_ap = cst[:, C_FLAG:C_FLAG + 1]

            def la_stage():
                la_p1()
                la_p2()

            def la_p1():
                p_, pk = bank()
                MM([(p_[0:16, :], wlr[:, kc, 0:16], hT[:, kc, :], kc == 0, kc == 7) for kc in range(8)],
                   ["wlr"] + HT_ALL, [pk])
                A(lrT[0:16, :], p_[0:16, :], AF.Copy, [pk], ["lrT"])

            def la_p2():
                for s in range(4):
                    p_, pk = bank()
                    MM([(p_[:, :], lrT[0:17, s * 128:(s + 1) * 128], wg[0:17, :], True, True)], ["lrT", "wg"], [pk])
                    t, tk = tmp()
                    A(t[:, :], p_[:, :], AF.Exp, [pk], [tk], scale=-1.0)
                    A(spb[:, s, :], t[:, :], AF.Ln, [tk], [("sp", s)], bias=1.0)

            def gla_stage(main, last_prev, special=False, hoist=None, nxt=None):
                if last_prev:
                    P.op("sp", lambda e: e.dma_start(out=w32q, in_=W["w_in"][:, Q0:Q0 + 512].rearrange("(kc p) c -> p kc c", p=128)),
                         [], ["w32q"] + [("x", 1, s_) for s_ in range(4)], dma="c5")
                    Vcopy(hTh[:, :, :], hT[:, :, 510:512], HT_ALL, ["hTh"])
                if main:
                    wq, wqk = next_w(("w_in", 0, Q0, 512))
                if main:
                    wk_, wkk = next_w(("w_in", 0, K0, 512))
                else:
                    wk_, wkk = wkp, "wkp"
                e2s = []
                pbbs = []
                for h in range(4):
                    hs = slice(h * 128, (h + 1) * 128)
                    pbb, pbbk = bank()
                    MM([(pbb[:, s * 128:(s + 1) * 128], spb[:, s, hs], uneg[:, :], True, True) for s in range(4)],
                       [("sp", s) for s in range(4)] + ["uneg"], [pbbk])
                    pbbs.append((pbb, pbbk))
                for h in range(4):
                    pbb, pbbk = pbbs[h]
                    A(E1[:, h, :], pbb[:, :], AF.Exp, [pbbk], [("E1", h)])
                    e2, e2k = tmp()
                    A(e2[:, :], pbb[:, :], AF.Exp, [pbbk], [e2k], scale=-1.0)
                    e2s.append((e2, e2k))
                if (not main) and nxt is not None:
                    norm_p1(xbuf[nxt % 2], nxt % 2)

                def head_front(h):
                    hs = slice(h * 128, (h + 1) * 128)
                    e2, e2k = e2s[h]
                    if main:
                        pq, pqk = bank()
                        MM([(pq[:, :], wq[:, kc, hs], hT[:, kc, :], kc == 0, kc == 7) for kc in range(8)],
                           [wqk] + HT_ALL, [pqk])
                    pkk_, pkkk = bank()
                    MM([(pkk_[:, :], wk_[:, kc, hs], hT[:, kc, :], kc == 0, kc == 7) for kc in range(8)],
                       [wkk] + HT_ALL, [pkkk])
                    if special:
                        MM([(pq[:, 0:128], w32q[:, kc, hs], h32T[:, kc, :], kc == 0, kc == 7) for kc in range(8)],
                           ["w32q", "h32T", ("x", 1, 0)], [pqk])
                        MM([(pkk_[:, 0:128], w32k[:, kc, hs], h32T[:, kc, :], kc == 0, kc == 7) for kc in range(8)],
                           ["w32k", "h32T"], [pkkk])
                    if main:
                        Vstt(qdT[:, h, :], pq[:, :], 128.0 ** -0.5, E1[:, h, :], ALU.mult, ALU.mult,
                             [pqk, ("E1", h)], [("qdT", h)])
                    Vtt(kdT[:, h, :], pkk_[:, :], e2[:, :], ALU.mult, [pkkk, e2k], [("kdT", h)])
                    if special:
                        Vstt(qd32[:, h, :], pq[:, 0:128], 128.0 ** -0.5, E1[:, h, 0:128], ALU.mult, ALU.mult,
                             [pqk, ("E1", h)], [("qd32", h), "xn32"])
                        Vtt(kd32[:, h, :], pkk_[:, 0:128], e2[:, 0:128], ALU.mult, [pkkk, e2k], [("kd32", h), "xn32"])

                def head_back(h):
                    pb, pbk = bbank()
                    TR([(pb[:, s * 128:(s + 1) * 128], kdT[:, h, s * 128:(s + 1) * 128]) for s in range(4)], [("kdT", h)], [pbk])
                    Vcopy(ke[:, h, :], pb[:, 0:512], [pbk], [("ke", h)])

                for h in range(4):
                    head_front(h)
                    if h >= 1:
                        head_back(h - 1)
                head_back(3)
                if main:
                    done_w()
                    done_w()
                for n in range(2):
                    if main:
                        wv, wvk = next_w(("w_in", 0, V0 + n * 512, 512))
                    else:
                        wv, wvk = wvp[n], f"wvp{n}"
                    for s in range(4):
                        p_, pk = bank()
                        MM([(p_[:, :], hT[:, kc, s * 128:(s + 1) * 128], wv[:, kc, :], kc == 0, kc == 7) for kc in range(8)],
                           [wvk] + HT_ALL, [pk])
                        if s % 2 == 0:
                            Vcopy(v_sb[:, s, n * 512:(n + 1) * 512], p_[:, :], [pk], [("v", s)])
                        else:
                            A(v_sb[:, s, n * 512:(n + 1) * 512], p_[:, :], AF.Copy, [pk], [("v", s)])
                    if main:
                        done_w()
                if last_prev:
                    P.op("sp", lambda e: e.dma_start(out=w32k, in_=W["w_in"][:, K0:K0 + 512].rearrange("(kc p) c -> p kc c", p=128)),
                         [], ["w32k"], dma="c6")
                if (not main) and nxt is not None:
                    norm_p2(a1T, "a1T", 0)
                    la_p1()
                if main:
                    for n in range(2):
                        wgg, wggk = next_w(("w_in", 0, G0 + n * 512, 512))
                        for s in range(4):
                            p_, pk = bank()
                            MM([(p_[:, :], hT[:, kc, s * 128:(s + 1) * 128], wgg[:, kc, :], kc == 0, kc == 7) for kc in range(8)],
                               [wggk] + HT_ALL, [pk])
                            r_, rk = sigmoid(p_, pk)
                            Vtt(sg[:, s, n * 512:(n + 1) * 512], p_[:, :], r_[:, :], ALU.mult, [pk, rk],
                                [("sg", s, 2 * n), ("sg", s, 2 * n + 1)])
                        done_w()
                def phaseA(s):
                    cs = slice(s * 128, (s + 1) * 128)
                    st = {}
                    if main:
                        psc, psck = bank()
                        if special and s == 0:
                            MM([(psc[:, h * 128:(h + 1) * 128], kd32[:, h, :], qd32[:, h, :], True, True) for h in range(4)],
                               [("kd32", h) for h in range(4)] + [("qd32", h) for h in range(4)], [psck])
                        else:
                            MM([(psc[:, h * 128:(h + 1) * 128], kdT[:, h, cs], qdT[:, h, cs], True, True) for h in range(4)],
                               [("kdT", h) for h in range(4)] + [("qdT", h) for h in range(4)], [psck])
                        sc, sck = scbuf()
                        Vtt(sc[:, :], psc[:, :], cmask4[:, :], ALU.mult, [psck, "cmask4"], [sck])
                        st["sc"] = (sc, sck)
                    pts = []
                    for hp in range(2):
                        pt, ptk = bank()
                        MM([(pt[:, j * 256:(j + 1) * 256], ke[:, hp * 2 + j, cs],
                             v_sb[:, s, (hp * 2 + j) * 256:(hp * 2 + j + 1) * 256], True, True) for j in range(2)],
                           [("ke", hp * 2), ("ke", hp * 2 + 1), ("v", s)], [ptk])
                        pts.append((pt, ptk))
                    st["pts"] = pts
                    return st

                def supd(s, st):
                    for h in range(4):
                        pt, ptk = st["pts"][h // 2]
                        Vtt(S32[:, h, :], S32[:, h, :], pt[:, (h % 2) * 256:(h % 2 + 1) * 256], ALU.add,
                            [("S32", h), ptk], [("S32", h)])
                        Vsmul(S32[:, h, :], S32[:, h, :], E1[:, h, s * 128 + 127:s * 128 + 128],
                              [("S32", h), ("E1", h)], [("S32", h)])

                def o_and_act(s, st):
                    cs = slice(s * 128, (s + 1) * 128)
                    pos = []
                    if main:
                        sc, sck = st["sc"]
                        Sb = S_bfs[s % 2]
                        for hp in range(2):
                            po, pok = bank()
                            mms = []
                            for j in range(2):
                                h = hp * 2 + j
                                mms.append((po[:, j * 256:(j + 1) * 256], sc[:, h * 128:(h + 1) * 128],
                                            v_sb[:, s, h * 256:(h + 1) * 256], True, False))
                                mms.append((po[:, j * 256:(j + 1) * 256], qdT[:, h, cs], Sb[:, h, :], False, True))
                            MM(mms, [sck, ("v", s), ("qdT", hp * 2), ("qdT", hp * 2 + 1), ("Sbf", 0)], [pok])
                            pos.append((po, pok))
                    if main:
                        A(S_bfs[(s + 1) % 2][:, :, :], S32[:, :, :], AF.Copy, [("S32", h) for h in range(4)], [("Sbf", 0)])
                    if main:
                        for h in range(4):
                            po, pok = pos[h // 2]
                            i = s * 4 + h
                            jt, jtk = tmp()
                            A(jt[:, :].bitcast(BF16)[:, 0:256], po[:, (h % 2) * 256:(h % 2 + 1) * 256], AF.Square, [pok],
                              [("sso", i), jtk], accum_out=sso[:, i:i + 1])
                        A(lno[:, s * 4:(s + 1) * 4], sso[:, s * 4:(s + 1) * 4], AF.Ln, [("sso", s * 4 + h) for h in range(4)],
                          [("lno", s)], scale=1.0 / 256, bias=EPS)
                        A(rso[:, s * 4:(s + 1) * 4], lno[:, s * 4:(s + 1) * 4], AF.Exp, [("lno", s)], [("rso", s)], scale=-0.5)
                    return pos

                def og_stage(s, pos):
                    if main:
                        for h in range(4):
                            po, pok = pos[h // 2]
                            i = s * 4 + h
                            sgs = sg[:, s, h * 256:(h + 1) * 256]
                            Vstt(sgs, po[:, (h % 2) * 256:(h % 2 + 1) * 256], rso[:, i:i + 1], sgs, ALU.mult, ALU.mult,
                                 [pok, ("rso", s), ("sg", s, h)], [("sg", s, h)])

                sts = {0: phaseA(0)}
                pos_prev = None
                for s in range(4):
                    supd(s, sts[s])
                    if s + 1 < 4:
                        sts[s + 1] = phaseA(s + 1)
                    pos = o_and_act(s, sts[s])
                    if pos_prev is not None:
                        og_stage(s - 1, pos_prev)
                    pos_prev = pos
                og_stage(3, pos_prev)
                if last_prev:
                    for h in range(4):
                        Vsmul(S32[:, h, :], S32[:, h, :], flag_ap, [("S32", h), "cst"], [("S32", h)])
                    A(S_bfs[0][:, :, :], S32[:, :, :], AF.Copy, [("S32", h) for h in range(4)], [("Sbf", 0)])
                if not main:
                    if nxt is not None:
                        la_p2()
                    return
                for kc in range(8):
                    pb, pbk = bbank()
                    TR([(pb[:, s * 128:(s + 1) * 128], sg[:, s, kc * 128:(kc + 1) * 128]) for s in range(4)],
                       [("sg", s, kc // 2) for s in range(4)], [pbk])
                    g_ap = cst[:, C_GNW + kc % 2:C_GNW + kc % 2 + 1]
                    if kc % 2:
                        A(ogT[:, kc, :], pb[:, 0:512], AF.Copy, [pbk, "cst"], [("ogT", kc)], scale=g_ap)
                    else:
                        Vsmul(ogT[:, kc, :], pb[:, 0:512], g_ap, [pbk, "cst"], [("ogT", kc)])

            def cw_ap(m, j):
                c = C_CW + m * 3 + j
                return cst[:, c:c + 1]

            def conv_stage(first=False):
                for mg in range(2):
                    wcb, wcbk = next_w(("w_in", 0, CB0 + mg * 512, 512))
                    wcc, wcck = next_w(("w_in", 0, CC0 + mg * 512, 512))
                    wcx, wcxk = next_w(("w_in", 0, CX0 + mg * 512, 512))
                    for mm in range(4):
                        m = mg * 4 + mm
                        ms = slice(mm * 128, (mm + 1) * 128)
                        pc, pck = bank()
                        MM([(pc[:, :], wcc[:, kc, ms], hT[:, kc, :], kc == 0, kc == 7) for kc in range(8)], [wcck] + HT_ALL, [pck])
                        px, pxk = bank()
                        MM([(px[:, :], wcx[:, kc, ms], hT[:, kc, :], kc == 0, kc == 7) for kc in range(8)], [wcxk] + HT_ALL, [pxk])
                        pcb, pcbk = bank()
                        MM([(pcb[:, :], wcb[:, kc, ms], hT[:, kc, :], kc == 0, kc == 7) for kc in range(8)], [wcbk] + HT_ALL, [pcbk])
                        if first:
                            ph, phk = bank()
                            MM([(ph[:, 0:2], wcc[:, kc, ms], hTh[:, kc, :], kc == 0, kc == 7) for kc in range(8)]
                               + [(ph[:, 2:4], wcx[:, kc, ms], hTh[:, kc, :], kc == 0, kc == 7) for kc in range(8)],
                               [wcck, wcxk, "hTh"], [phk])
                            th, thk = tmp()
                            A(th[:, 0:2], ph[:, 0:2], AF.Copy, [phk], [thk])
                            Vtt(th[:, 2:4], th[:, 0:2], ph[:, 2:4], ALU.mult, [thk, phk], [thk])
                            Vsmul(uh[:, m, :], th[:, 2:4], flag_ap, [thk, "cst"], [("uh", m)])
                        t, tk = tmp()
                        A(t[:, :], pc[:, :], AF.Copy, [pck], [tk])
                        u, uk = ubuf()
                        A(u[:, 0:2], uh[:, m, :], AF.Copy, [("uh", m)], [uk])
                        Vtt(u[:, 2:514], t[:, :], px[:, :], ALU.mult, [tk, pxk, uk], [uk])
                        A(uh[:, m, :], u[:, 512:514], AF.Copy, [uk], [("uh", m)])
                        a_, ak = tmp()
                        Vsmul(a_[:, :], u[:, 2:514], cw_ap(m, 2), [uk, "cst"], [ak])
                        Vstt(a_[:, :], u[:, 1:513], cw_ap(m, 1), a_[:, :], ALU.mult, ALU.add, [uk, "cst", ak], [ak])
                        Vstt(a_[:, :], u[:, 0:512], cw_ap(m, 0), a_[:, :], ALU.mult, ALU.add, [uk, "cst", ak], [ak])
                        Vtt(cbuT[:, m, :], a_[:, :], pcb[:, :], ALU.mult, [ak, pcbk], [("cbuT", m)])
                    done_w()
                    done_w()
                    done_w()

            def merge_stage(xb, par):
                OGT_ALL = [("ogT", kc) for kc in range(8)]
                CBU_ALL = [("cbuT", kc) for kc in range(8)]
                for mg in range(2):
                    wa, wak = next_w(("w_pa", 0, mg * 512, 512))
                    wga_, wgak = next_w(("w_in", 0, GA0 + mg * 512, 512))
                    wb, wbk = next_w(("w_pb", 0, mg * 512, 512))
                    wgb, wgbk = next_w(("w_in", 0, GB0 + mg * 512, 512))
                    for mm in range(4):
                        m = mg * 4 + mm
                        ms = slice(mm * 128, (mm + 1) * 128)
                        pya, pyak = bank()
                        MM([(pya[:, :], wa[:, kc, ms], ogT[:, kc, :], kc == 0, kc == 7) for kc in range(8)], [wak] + OGT_ALL, [pyak])
                        pga, pgak = bank()
                        MM([(pga[:, :], wga_[:, kc, ms], hT[:, kc, :], kc == 0, kc == 7) for kc in range(8)], [wgak] + HT_ALL, [pgak])
                        ra, rak = sigmoid(pga, pgak)
                        Vtt(ra[:, :], ra[:, :], pya[:, :], ALU.mult, [rak, pyak], [rak])
                        pyb, pybk = bank()
                        MM([(pyb[:, :], wb[:, kc, ms], cbuT[:, kc, :], kc == 0, kc == 7) for kc in range(8)], [wbk] + CBU_ALL, [pybk])
                        pgb, pgbk = bank()
                        MM([(pgb[:, :], wgb[:, kc, ms], hT[:, kc, :], kc == 0, kc == 7) for kc in range(8)], [wgbk] + HT_ALL, [pgbk])
                        rb, rbk = sigmoid(pgb, pgbk)
                        Vtt(rb[:, :], rb[:, :], pyb[:, :], ALU.mult, [rbk, pybk], [rbk])
                        Vtt(zT[:, m, :], ra[:, :], rb[:, :], ALU.add, [rak, rbk], [("zT", m)])
                    for _ in range(4):
                        done_w()
                ZT_ALL = [("zT", kc) for kc in range(8)]
                for n in range(2):
                    wo, wok = next_w(("w_o", 0, n * 512, 512))
                    ns = slice(n * 512, (n + 1) * 512)
                    for s in range(4):
                        p_, pk = bank()
                        MM([(p_[:, :], zT[:, kc, s * 128:(s + 1) * 128], wo[:, kc, :], kc == 0, kc == 7) for kc in range(8)],
                           [wok] + ZT_ALL, [pk])
                        t, tk = tmp()
                        Vtt(t[:, :], p_[:, :], g1b[:, ns], ALU.mult, [pk, "g1b"], [tk])
                        xk = ("x", par, s)
                        Vtt(xb[:, s, ns], xb[:, s, ns], t[:, :], ALU.add, [xk, tk], [xk])
                    done_w()

            def mlp_stage(xb, par, g, hoist=None, nxt=None):
                stage_norm(xb, par, a2T, "a2T", 16)
                if nxt is not None:
                    norm_p1(xbuf[nxt % 2], nxt % 2)
                for cg in range(8):
                    w1_, w1k = next_w(("w1", 0, cg * 512, 512))
                    for jj in range(4):
                        j = cg * 4 + jj
                        p_, pk = bank()
                        MM([(p_[:, :], w1_[:, kc, jj * 128:(jj + 1) * 128], hT[:, kc, :], kc == 0, kc == 7) for kc in range(8)],
                           [w1k] + HT_ALL, [pk])
                        t, tk = tmp()
                        A(t[:, :], p_[:, :], AF.Relu, [pk], [tk])
                        Vtt(aT[:, j, :], t[:, :], t[:, :], ALU.mult, [tk], [("aT", j)])
                    done_w()
                for n in range(2):
                    ns = slice(n * 512, (n + 1) * 512)
                    bks = [bank() for _ in range(4)]
                    for jg in range(4):
                        w2_, w2k = next_w(("w2", jg * 1024, n * 512, 512))
                        for s in range(4):
                            MM([(bks[s][0][:, :], aT[:, jg * 8 + jj, s * 128:(s + 1) * 128], w2_[:, jj, :],
                                 jg == 0 and jj == 0, jg == 3 and jj == 7) for jj in range(8)],
                               [w2k] + [("aT", jg * 8 + jj) for jj in range(8)], [bks[s][1]])
                        done_w()
                    for s in range(4):
                        t, tk = tmp()
                        Vtt(t[:, :], bks[s][0][:, :], g2b[:, ns], ALU.mult, [bks[s][1], "g2b"], [tk])
                        xk = ("x", par, s)
                        Vtt(xb[:, s, ns], xb[:, s, ns], t[:, :], ALU.add, [xk, tk], [xk])
                    if n == 0 and nxt is not None:
                        norm_p2(a1T, "a1T", 0)
                        la_p1()
                        la_p2()
                    if n == 0:
                        P.op("sp", lambda e: e.dma_start(out=fnw, in_=fnwb_d), [], [("xn", 0), ("xn", 1)], dma="c4")
                for s in range(4):
                    xk = ("x", par, s)
                    jt, jtk = tmp()
                    A(jt[:, :].bitcast(BF16), xb[:, s, :], AF.Square, [xk], [jtk, ("ss2", s)], accum_out=ss2[:, s:s + 1])
                    A(lnv2[:, s:s + 1], ss2[:, s:s + 1], AF.Ln, [("ss2", s)], [("lnv2", s)], scale=1.0 / D, bias=EPS)
                    A(rstd2[:, s:s + 1], lnv2[:, s:s + 1], AF.Exp, [("lnv2", s)], [("rstd2", s)], scale=-0.5)
                    A(xb[:, s, :], xb[:, s, :], AF.Copy, [xk, ("rstd2", s)], [xk], scale=rstd2[:, s:s + 1])
                    Vtt(xb[:, s, :], xb[:, s, :], fnw, ALU.mult, [xk, ("xn", 0), ("xn", 1)], [xk])
                    r0 = (g % 4) * T + s * 128
                    P.op("sp", lambda e, r0=r0, s=s: e.dma_start(out=out_d[r0:r0 + 128, :], in_=xb[:, s, :]), [xk],
                         [("out", g, s)], dma=f"o{par}")

            def special_prep(xb, par):
                Vsmul(xn32, xb[:, 0, :], rstd[:, 0:1], [("x", par, 0), ("rstd", 0)], ["xn32"])
                for half in range(2):
                    p_, pk = bank()
                    def fn(e, p_=p_, half=half):
                        ins = None
                        for j in range(4):
                            kc = half * 4 + j
                            ins = e.transpose(out=p_[:, j * 128:(j + 1) * 128], in_=xn32[:, kc * 128:(kc + 1) * 128],
                                              identity=identf[:, :])
                        return ins
                    P.op("pe", fn, ["xn32", "identf"], [pk])
                    for j in range(4):
                        kc = half * 4 + j
                        Vts(h32T[:, kc, :], p_[:, j * 128:(j + 1) * 128], a1T[:, kc:kc + 1], modT[:, kc:kc + 1],
                            ALU.mult, ALU.add, [pk, "a1T", ("modT", 0), ("modT", 4)], ["h32T"])

            for ci in range(4):
                ada_chunk(ci)
            mod_finish(1)
            rest = [4, 5, 6, 7, 8, 9, 10, 11]
            for g in range(8):
                par = g % 2
                xb = xbuf[par]
                if g >= 4:
                    P.wmode = "t4" if g == 4 else "t5" if g == 5 else "scr"
                    P.widx = 0
                if g == 0:
                    stage_norm(xb, par, a1T, "a1T", 0)
                    la_stage()
                nxt = g + 1 if g + 1 < 8 else None
                if g < 3:
                    rec_xload(g + 2)
                if g < 4:
                    gla_stage(False, g == 3, nxt=nxt)
                    for ci in rest[g * 2:g * 2 + 2]:
                        ada_chunk(ci)
                    rec_convs(2 if g < 3 else 100)
                    if g == 3:
                        mod_finish(2)
                else:
                    if g == 4:
                        special_prep(xb, par)
                    gla_stage(True, False, special=(g == 4))
                    if g == 4:
                        rec_xload(5)
                    conv_stage(first=(g == 4))
                    merge_stage(xb, par)
                    mlp_stage(xb, par, g, nxt=nxt)
                if g + 2 < 8 and g >= 4:
                    rec_xload(g + 2)
            P.op("sp", None, [("out", g, s_) for g in range(4, 8) for s_ in range(4)], [])

        wplan = []
        record(Prog(True, wplan))
        P = Prog(False, wplan)
        record(P)
        assert P.wi == len(wplan), (P.wi, len(wplan))

        sem_names = P.sems()
        S = {n: es.enter_context(nc.semaphore(n)) for n in sem_names}
        block = es.enter_context(nc.Block())

        def run_stream(eng_name):
            def body(e):
                for (waits, fn, sem, amt) in P.streams[eng_name]:
                    for (s_, v_) in waits:
                        e.wait_ge(S[s_], v_)
                    if fn is not None:
                        fn(e).then_inc(S[sem], amt)
            return body

        block.sync(run_stream("sp"))
        block.gpsimd(run_stream("pool"))
        block.tensor(run_stream("pe"))
        block.scalar(run_stream("act"))
        block.vector(run_stream("dve"))
    return nc


_NC = None


def kernel(x, c, w_ada, b_ada, norm1_w, w_in, w_gate_up, b_gate, gla_norm_w, conv_w,
           w_proj_a, w_proj_b, w_out, norm2_w, w_mlp1, w_mlp2, final_norm_w):
    global _NC
    f = lambda a: np.ascontiguousarray(np.asarray(a, dtype=np.float32))
    x = f(x)
    c = f(c)
    b_ada = f(b_ada)[0]
    shared = {
        "w_ada": f(w_ada)[0], "w_in": f(w_in)[0], "w_pa": f(w_proj_a)[0], "w_pb": f(w_proj_b)[0],
        "w_o": f(w_out)[0], "w1": f(w_mlp1)[0], "w2": f(w_mlp2)[0],
        "fnw_b": np.ascontiguousarray(np.broadcast_to(f(final_norm_w)[None, :], (128, D))),
        "bgate_b": np.ascontiguousarray(np.broadcast_to(
            np.concatenate([b_ada[2 * D:3 * D], b_ada[5 * D:6 * D]])[None, :], (128, 2 * D))),
        "wg_aug": np.ascontiguousarray(np.concatenate([f(w_gate_up)[0], f(b_gate)[0][None, :]], axis=0)),
    }
    colT = lambda v: np.ascontiguousarray(v.reshape(-1, 128).T)
    cbase = np.zeros((128, NCONST), np.float32)
    cbase[:, C_BADA:C_BADA + 32] = np.concatenate(
        [colT(b_ada[0:D]), colT(b_ada[D:2 * D]), colT(b_ada[3 * D:4 * D]), colT(b_ada[4 * D:5 * D])], axis=1)
    cbase[:, C_N1:C_N1 + 8] = colT(f(norm1_w)[0])
    cbase[:, C_N2:C_N2 + 8] = colT(f(norm2_w)[0])
    cwl = f(conv_w)[0]
    cbase[:, C_CW:C_CW + 24] = np.transpose(cwl.reshape(3, 8, 128), (2, 1, 0)).reshape(128, 24)
    cbase[:, C_GNW:C_GNW + 2] = colT(f(gla_norm_w)[0])
    if _NC is None:
        _NC = build_nc()
    in_maps = []
    for i in range(8):
        b, hf = i // 2, i % 2
        cs = cbase.copy()
        cs[:, C_CT:C_CT + 8] = colT(c[b])
        cs[:, C_FLAG] = float(hf)
        m = dict(shared)
        m["x_cur"] = np.ascontiguousarray(x[b, hf * TOK:(hf + 1) * TOK])
        m["x_prev"] = np.ascontiguousarray(x[b, 0:TOK])
        m["consts"] = cs
        in_maps.append(m)
    res = run_bass_kernel_spmd(_NC, in_maps, core_ids=list(range(8)))
    out = np.empty((4, 2 * TOK, D), np.float32)
    for i in range(8):
        b, hf = i // 2, i % 2
        out[b, hf * TOK:(hf + 1) * TOK] = np.asarray(res.results[i]["out"]).reshape(TOK, D)
    return out
```
